# Optimizing a Trainium2 kernel written in Bass

```python
import jax, jax.numpy as jnp
from jax import lax
import numpy as np

D_MODEL = 1024
BATCH = 16
SEQ = 2048
DEPTH = 1

GRID_W = 64
CTX_LEN = 256
D_CONV = 1024
CONV_WIDTH = 31
N_HEADS = 8
QK_NOPE = 128
QK_ROPE = 64
V_DIM = 128
Q_LORA = 384
KV_LORA = 256
ROPE_THETA = 10000.0
Q_BLOCK = 128
ATTN_SCALE = (QK_NOPE + QK_ROPE) ** -0.5
D_FF = ((8 * D_MODEL // 3 + 255) // 256) * 256
EPS = 1e-6

_OFF_Q = 2 * D_CONV
_OFF_KV = _OFF_Q + Q_LORA
_OFF_KR = _OFF_KV + KV_LORA
_OFF_GATE = _OFF_KR + QK_ROPE
D_IN = _OFF_GATE + 2 * D_MODEL

kernel_name = "hybrid_conformer_mla_dit_block"


def rms_norm(x, g):
    xf = x.astype(jnp.float32)
    y = xf * lax.rsqrt(jnp.mean(jnp.square(xf), axis=-1, keepdims=True) + EPS)
    return (y * g.astype(jnp.float32)).astype(x.dtype)


def layer_norm(x, g, b):
    xf = x.astype(jnp.float32)
    mu = jnp.mean(xf, axis=-1, keepdims=True)
    var = jnp.mean(jnp.square(xf - mu), axis=-1, keepdims=True)
    y = (xf - mu) * lax.rsqrt(var + EPS)
    return (y * g.astype(jnp.float32) + b.astype(jnp.float32)).astype(x.dtype)


def modulate(h, shift, scale):
    return h * (1 + scale) + shift


def axial_angles(rows):
    row = jnp.repeat(jnp.arange(rows, dtype=jnp.float32), GRID_W)
    col = jnp.tile(jnp.arange(GRID_W, dtype=jnp.float32), rows)
    axis_dim = QK_ROPE // 2
    inv_freq = ROPE_THETA ** (-jnp.arange(0, axis_dim, 2, dtype=jnp.float32) / axis_dim)
    return row[:, None] * inv_freq, col[:, None] * inv_freq


def rotate_segment(x, ang):
    x1, x2 = jnp.split(x, 2, axis=-1)
    cos = jnp.cos(ang).astype(x.dtype)
    sin = jnp.sin(ang).astype(x.dtype)
    return jnp.concatenate([x1 * cos - x2 * sin, x2 * cos + x1 * sin], axis=-1)


def axial_rope(x, ang_row, ang_col):
    xr, xc = jnp.split(x, 2, axis=-1)
    return jnp.concatenate([rotate_segment(xr, ang_row), rotate_segment(xc, ang_col)], axis=-1)


def mla_query(q_a, g_q, w_uq):
    bsz, n, _ = q_a.shape
    q = (rms_norm(q_a, g_q) @ w_uq).reshape(bsz, n, N_HEADS, QK_NOPE + QK_ROPE)
    return q[..., :QK_NOPE], q[..., QK_NOPE:]


def mla_kv(kv_a, g_kv, w_ukv):
    bsz, n, _ = kv_a.shape
    kv = (rms_norm(kv_a, g_kv) @ w_ukv).reshape(bsz, n, N_HEADS, QK_NOPE + V_DIM)
    return kv[..., :QK_NOPE], kv[..., QK_NOPE:]


def mla_attend(q_nope, q_rope, k_nope, k_rope, v):
    s = (jnp.einsum('bqhd,bkhd->bhqk', q_nope, k_nope)
         + jnp.einsum('bqhr,bkr->bhqk', q_rope, k_rope))
    p = jax.nn.softmax(s.astype(jnp.float32) * ATTN_SCALE, axis=-1).astype(v.dtype)
    return jnp.einsum('bhqk,bkhd->bqhd', p, v)


def blocked_mla(q_nope, q_rope, k_nope, k_rope, v):
    bsz, n = q_nope.shape[:2]
    nb = n // Q_BLOCK

    def to_blocks(t):
        return jnp.moveaxis(t.reshape(bsz, nb, Q_BLOCK, *t.shape[2:]), 1, 0)

    out = lax.map(lambda qs: mla_attend(qs[0], qs[1], k_nope, k_rope, v),
                  (to_blocks(q_nope), to_blocks(q_rope)))
    return jnp.moveaxis(out, 0, 1).reshape(bsz, n, N_HEADS * V_DIM)


def conformer_conv(conv_in, w_dw, b_dw, ln_g, ln_b, w_pw):
    a, g = jnp.split(conv_in, 2, axis=-1)
    u = a * jax.nn.sigmoid(g)
    u = lax.conv_general_dilated(
        u, w_dw[:, None, :], window_strides=(1,),
        padding=((CONV_WIDTH // 2, CONV_WIDTH // 2),),
        dimension_numbers=('NWC', 'WIO', 'NWC'),
        feature_group_count=D_CONV) + b_dw
    u = jax.nn.silu(layer_norm(u, ln_g, ln_b))
    return u @ w_pw


def merge_branches(y_conv, y_mla, gates, w_out):
    g_conv, g_mla = jnp.split(gates, 2, axis=-1)
    return (jax.nn.sigmoid(g_conv) * y_conv + jax.nn.sigmoid(g_mla) * y_mla) @ w_out


def swiglu(h, w_13, w_2):
    a, b = jnp.split(h @ w_13, 2, axis=-1)
    return (jax.nn.silu(a) * b) @ w_2


def setup_inputs(seed: int = 0) -> dict:
    key = jax.random.key(seed)
    ks = jax.random.split(key, 24)
    f32 = jnp.float32

    def nrm(k, shape, scale=1.0):
        return scale * jax.random.normal(k, shape, f32)

    def gain(k, n):
        return 1.0 + 0.01 * jax.random.normal(k, (DEPTH, n), f32)

    return {
        "x": nrm(ks[0], (BATCH, SEQ, D_MODEL)),
        "c": nrm(ks[1], (BATCH, D_MODEL)),
        "ctx": nrm(ks[2], (BATCH, CTX_LEN, D_MODEL)),
        "c_ctx": nrm(ks[3], (D_MODEL,)),
        "w_mod": nrm(ks[4], (DEPTH, D_MODEL, 6 * D_MODEL), 0.5 * D_MODEL ** -0.5),
        "b_mod": nrm(ks[5], (DEPTH, 6 * D_MODEL), 0.01),
        "g_mix": gain(ks[6], D_MODEL),
        "g_ffn": gain(ks[7], D_MODEL),
        "w_in": nrm(ks[8], (DEPTH, D_MODEL, D_IN), D_MODEL ** -0.5),
        "g_q": gain(ks[9], Q_LORA),
        "g_kv": gain(ks[10], KV_LORA),
        "w_uq": nrm(ks[11], (DEPTH, Q_LORA, N_HEADS * (QK_NOPE + QK_ROPE)), Q_LORA ** -0.5),
        "w_ukv": nrm(ks[12], (DEPTH, KV_LORA, N_HEADS * (QK_NOPE + V_DIM)), KV_LORA ** -0.5),
        "w_o_mla": nrm(ks[13], (DEPTH, N_HEADS * V_DIM, D_MODEL), (N_HEADS * V_DIM) ** -0.5),
        "w_dw": nrm(ks[14], (DEPTH, CONV_WIDTH, D_CONV), CONV_WIDTH ** -0.5),
        "b_dw": nrm(ks[15], (DEPTH, D_CONV), 0.01),
        "ln_g": gain(ks[16], D_CONV),
        "ln_b": nrm(ks[17], (DEPTH, D_CONV), 0.01),
        "w_pw": nrm(ks[18], (DEPTH, D_CONV, D_MODEL), D_CONV ** -0.5),
        "w_out": nrm(ks[19], (DEPTH, D_MODEL, D_MODEL), D_MODEL ** -0.5),
        "w_13": nrm(ks[20], (DEPTH, D_MODEL, 2 * D_FF), D_MODEL ** -0.5),
        "w_2": nrm(ks[21], (DEPTH, D_FF, D_MODEL), D_FF ** -0.5),
        "g_final": 1.0 + 0.01 * jax.random.normal(ks[22], (D_MODEL,), f32),
    }


def reference(x, c, ctx, c_ctx, w_mod, b_mod, g_mix, g_ffn, w_in, g_q, g_kv, w_uq, w_ukv,
              w_o_mla, w_dw, b_dw, ln_g, ln_b, w_pw, w_out, w_13, w_2, g_final):
    bsz, n, _ = x.shape
    ROWS = n // GRID_W
    ang_r, ang_c = axial_angles(ROWS)
    silu_c = jax.nn.silu(c)
    silu_cc = jax.nn.silu(c_ctx)

    for l in range(DEPTH):
        last = l == DEPTH - 1
        mx = jnp.split((silu_c @ w_mod[l] + b_mod[l])[:, None, :], 6, axis=-1)
        mc = jnp.split(silu_cc @ w_mod[l] + b_mod[l], 6, axis=-1)

        hx = modulate(rms_norm(x, g_mix[l]), mx[0], mx[1])
        hc = modulate(rms_norm(ctx, g_mix[l]), mc[0], mc[1])

        zx = hx @ w_in[l]
        conv_x, qa_x, kva_x, kr_x, gt_x = jnp.split(
            zx, [_OFF_Q, _OFF_KV, _OFF_KR, _OFF_GATE], axis=-1)
        qn_x, qr_x = mla_query(qa_x, g_q[l], w_uq[l])
        kn_x, v_x = mla_kv(kva_x, g_kv[l], w_ukv[l])
        qr_x = axial_rope(qr_x, ang_r[:, None], ang_c[:, None])
        kr_x = axial_rope(kr_x, ang_r, ang_c)

        if last:
            zc = hc @ w_in[l][:, _OFF_KV:_OFF_GATE]
            kva_c, kr_c = jnp.split(zc, [KV_LORA], axis=-1)
        else:
            zc = hc @ w_in[l]
            conv_c, qa_c, kva_c, kr_c, gt_c = jnp.split(
                zc, [_OFF_Q, _OFF_KV, _OFF_KR, _OFF_GATE], axis=-1)
        kn_c, v_c = mla_kv(kva_c, g_kv[l], w_ukv[l])

        k_nope = jnp.concatenate([kn_c, kn_x], axis=1)
        k_rope = jnp.concatenate([kr_c, kr_x], axis=1)
        v_all = jnp.concatenate([v_c, v_x], axis=1)
        att_x = blocked_mla(qn_x, qr_x, k_nope, k_rope, v_all) @ w_o_mla[l]
        y_conv_x = conformer_conv(conv_x, w_dw[l], b_dw[l], ln_g[l], ln_b[l], w_pw[l])
        x = x + mx[2] * merge_branches(y_conv_x, att_x, gt_x, w_out[l])

        hx2 = modulate(rms_norm(x, g_ffn[l]), mx[3], mx[4])
        x = x + mx[5] * swiglu(hx2, w_13[l], w_2[l])

        if not last:
            qn_c, qr_c = mla_query(qa_c, g_q[l], w_uq[l])
            att_c = mla_attend(qn_c, qr_c, kn_c, kr_c, v_c).reshape(
                bsz, ctx.shape[1], N_HEADS * V_DIM) @ w_o_mla[l]
            y_conv_c = conformer_conv(conv_c, w_dw[l], b_dw[l], ln_g[l], ln_b[l], w_pw[l])
            ctx = ctx + mc[2] * merge_branches(y_conv_c, att_c, gt_c, w_out[l])
            hc2 = modulate(rms_norm(ctx, g_ffn[l]), mc[3], mc[4])
            ctx = ctx + mc[5] * swiglu(hc2, w_13[l], w_2[l])

    return rms_norm(x, g_final)
```

```python
import contextlib
import numpy as np
import concourse.bass as bass
import concourse.mybir as mybir
from concourse.bass_utils import run_bass_kernel_spmd

F32 = mybir.dt.float32
BF16 = mybir.dt.bfloat16
I32 = mybir.dt.int32
AF = mybir.ActivationFunctionType
ALU = mybir.AluOpType

P = 128
D = 1024
NCH = 8
SEQ = 2048
CTX = 256
NK = CTX + SEQ
NKC = NK // P
TT = 512
NT = SEQ // TT
HL = 16
HW = TT + 2 * HL
NH = 8
DFF = 2816
NF = DFF // P
EPS = 1e-6
ATTN_SCALE = float(192 ** -0.5)
N_CORES = 8
NB = 2
NPIECE = 46
PIECE = 4096
NWM = 12
NWM1 = 4
RING = 3
MAGIC = float(0x5F3759DF)
NPE_TAPS = 8
PRE_INFLIGHT = 2

V_GMIX, V_GFFN, V_GQ, V_GKV, V_BDW, V_LNG, V_LNB, V_GFIN, NV = 0, 8, 16, 19, 21, 29, 37, 45, 53

OFF_Q = 2048
OFF_KV = OFF_Q + 384
OFF_KR = OFF_KV + 256
OFF_GATE = OFF_KR + 64


class Tok:
    __slots__ = ("sem", "val", "key")

    def __init__(self, sem, val, key):
        self.sem, self.val, self.key = sem, val, key


class Buf:
    __slots__ = ("name", "w", "r")

    def __init__(self, name):
        self.name, self.w, self.r = name, None, {}


class Queue:
    def __init__(self, name, handle, sem, is_pe=False):
        self.name, self.h, self.sem, self.is_pe = name, handle, sem, is_pe
        self.cnt = 0
        self.seen = {}


class Chan:
    def __init__(self, name, sem):
        self.name, self.sem, self.val = name, sem, 0


import os
STRICT = os.environ.get("KSTRICT", "1") != "0"


class Tracker:
    def wait(self, q, tok):
        if tok is None:
            return
        if q.seen.get(tok.key, 0) >= tok.val:
            return
        q.h.wait_ge(tok.sem, tok.val)
        q.seen[tok.key] = tok.val

    def deps(self, q, reads, writes, chain_ok=False, is_dma=False):
        me = None if (is_dma or (STRICT and not q.is_pe)) else q.name
        for b in reads:
            t = b.w
            if t is None:
                continue
            if t.key == me and (q.is_pe or chain_ok):
                continue
            self.wait(q, t)
        for b in writes:
            t = b.w
            if t is not None and t.key != me:
                self.wait(q, t)
            for t in b.r.values():
                if t.key != me:
                    self.wait(q, t)

    def record(self, tok, reads, writes):
        for b in reads:
            old = b.r.get(tok.key)
            if old is None or old.val < tok.val:
                b.r[tok.key] = tok
        for b in writes:
            b.w = tok
            b.r = {}

    def op(self, q, fn, reads=(), writes=(), inc=True, chain_ok=False):
        self.deps(q, reads, writes, chain_ok)
        ins = fn()
        if inc:
            q.cnt += 1
            ins.then_inc(q.sem, 1)
            tok = Tok(q.sem, q.cnt, q.name)
        else:
            tok = Tok(q.sem, q.cnt + 1, q.name)
        self.record(tok, reads, writes)
        return tok

    def dma(self, q, chan, out, in_, reads=(), writes=()):
        self.deps(q, reads, writes, is_dma=True)
        chan.val += 16
        q.h.dma_start(out=out, in_=in_).then_inc(chan.sem, 16)
        tok = Tok(chan.sem, chan.val, chan.name)
        self.record(tok, reads, writes)
        return tok


def ring_sequence():
    seq = [("wm", i) for i in range(NWM1)]
    for _bl in range(NB):
        seq += [("ws", 0), ("ws", 1)]
        if _bl == 0:
            seq += [("ws", 38 + i) for i in range(NWM - NWM1)]
        for _t in range(NT):
            seq += [("ws", p) for p in (6, 2, 3, 4, 5, 7, 8, 13, 14, 15, 16, 9, 10, 11, 12, 17, 18)]
            seq += [("ws", p) for p in range(19, 30)]
            seq += [("ws", p) for p in range(30, 38)]
    return seq


def build_nc(dbg=False):
    nc = bass.Bass("TRN2", target_bir_lowering=False)
    dram = lambda name, shape, dt, kind: nc.dram_tensor(name, shape, dt, kind=kind).ap()
    xT = dram("xT", [NB, P, NCH, SEQ], F32, "ExternalInput")
    cxT = dram("cxT", [NB, P, NCH, CTX], F32, "ExternalInput")
    cvec = dram("cvec", [P, NCH, 4], F32, "ExternalInput")
    WM = dram("WM", [NWM1, P, PIECE], F32, "ExternalInput")
    WF = dram("WF", [NPIECE, P, PIECE], F32, "ExternalInput")
    bmod = dram("bmod", [P, 48], F32, "ExternalInput")
    vecs = dram("vecs", [P, NV], F32, "ExternalInput")
    wdw = dram("wdw", [P, NCH, 31], F32, "ExternalInput")
    tab = dram("tab", [P, 2, NK], F32, "ExternalInput")
    ident = dram("ident", [P, P], F32, "ExternalInput")
    outT = dram("outT", [NB, P, NCH, SEQ], F32, "ExternalOutput")
    WS = dram("WS", [NPIECE, P, PIECE], BF16, "Internal")
    if dbg:
        dbgKT = dram("dbgKT", [P, NH, NK], BF16, "ExternalOutput")
        dbgV = dram("dbgV", [P, NKC, 1024], BF16, "ExternalOutput")
        dbgKR = dram("dbgKR", [P, 2, NK], BF16, "ExternalOutput")
        dbgMOD = dram("dbgMOD", [P, 48, 4], F32, "ExternalOutput")

    tk = Tracker()
    with contextlib.ExitStack() as es:
        sb = lambda name, shape, dt: es.enter_context(nc.sbuf_tensor(name, shape, dt))
        sem = lambda name: es.enter_context(nc.semaphore(name))

        PE = Queue("pe", nc.tensor, sem("s_pe"), is_pe=True)
        ACT = Queue("act", nc.scalar, sem("s_act"))
        DVE = Queue("dve", nc.vector, sem("s_dve"))
        POOL = Queue("pool", nc.gpsimd, sem("s_pool"))
        SP = Queue("sp", nc.sync, sem("s_sp"))

        ONES = sb("ONES", [P, P], BF16)
        VEC = sb("VEC", [P, NV], F32)
        WDW = sb("WDW", [P, NCH, 31], F32)
        BMOD = sb("BMOD", [P, 48], F32)
        MOD = sb("MOD", [P, 48, 4], F32)
        SC = sb("SC", [P, NCH, 4], F32)
        SCB = sb("SCB", [P, NCH, 4], BF16)
        CV4 = sb("CV4", [P, NCH, 4], F32)
        DER = sb("DER", [P, 3, 4, NCH], F32)
        LNH = sb("LNH", [P, 2, NCH], F32)
        KT = sb("KT", [P, NH, NK], BF16)
        KR2 = sb("KR2", [P, 2, NK], BF16)
        V = sb("V", [P, NKC, 1024], BF16)
        XH = sb("XH", [P, NCH, TT], F32)
        XC = sb("XC", [P, 2, HW], F32)
        HX = sb("HX", [P, NCH, HW], BF16)
        AU = sb("AU", [P, NCH, HW], BF16)
        AC = sb("AC", [P, NCH, TT], BF16)
        AQF = sb("AQF", [P, NF * TT], BF16)
        AQ = AQF[:, :].rearrange("p (f t) -> p f t", t=TT)
        XB = AQF[:, 0:2 * NCH * TT].bitcast(F32).rearrange("p (c t) -> p c t", t=TT)
        SQ = sb("SQ", [P, 3, HW], BF16)
        PT = sb("PT", [P, 3, TT], BF16)
        SG = sb("SG", [P, 2, HW], BF16)
        RS = sb("RS", [P, HW], F32)
        RV = sb("RV", [P, HW], F32)
        RA = sb("RA", [P, HW], F32)
        T1 = sb("T1", [P, HW], F32)
        T2 = sb("T2", [P, HW], F32)
        ACC = sb("ACC", [P, 2, TT], F32)
        IDB = sb("IDB", [P, P], BF16)
        DG = sb("DG", [P, 4, P], BF16)
        CS = sb("CS", [P, 2, TT], F32)
        NRM = sb("NRM", [P, 3, TT], BF16)
        RINGT = sb("RINGT", [P, RING, PIECE], BF16)

        PS = [es.enter_context(nc.psum_tensor(f"ps{i}", [P, TT], F32)) for i in range(8)]
        PSB = [Buf(f"ps{i}") for i in range(8)]

        class Rot:
            def __init__(self, ids):
                self.ids, self.i = ids, 0

            def next(self):
                j = self.ids[self.i % len(self.ids)]
                self.i += 1
                return PS[j], PSB[j]

        GEN3 = Rot([0, 1, 2])
        GEN7 = Rot([0, 1, 2, 4, 5, 6])
        GEN = GEN7
        STAT = Rot([3])
        OACC = Rot([4, 5])
        DACC = Rot([6, 7])

        bVEC, bWDW, bBMOD, bMOD, bSC, bCV4, bDER, bLNH, bONES = (Buf(n) for n in
                                                                    "VEC WDW BMOD MOD SC CV4 DER LNH ONES".split())
        bKT = [Buf(f"KT{h}") for h in range(NH)]
        bKR = Buf("KR2")
        bV = Buf("V")
        bXH = [Buf(f"XH{c}") for c in range(NCH)]
        bXC = [Buf(f"XC{c}") for c in range(2)]
        bHX = [Buf(f"HX{c}") for c in range(NCH)]
        bAU = [Buf(f"AU{c}") for c in range(NCH)]
        bAC = [Buf(f"AC{c}") for c in range(NCH)]
        bAQ = [Buf(f"AQ{c}") for c in range(NF)]
        bSQ = [Buf(f"SQ{c}") for c in range(3)]
        bPT = [Buf(f"PT{c}") for c in range(3)]
        bSG = [Buf(f"SG{c}") for c in range(2)]
        bRS, bRV, bRA, bT1, bT2, bCS, bIDB = (Buf(n) for n in "RS RV RA T1 T2 CS IDB".split())
        bACC = [Buf("ACC0"), Buf("ACC1")]
        bDG = [Buf(f"DG{i}") for i in range(4)]
        dg_cnt = [0]
        bNRM = [Buf(f"NRM{c}") for c in range(3)]
        bRING = [Buf(f"RING{c}") for c in range(RING)]
        bWSg = [Buf(f"WSg{g}") for g in range(4)]
        bWSh = [Buf(f"WSh{g}") for g in range(4)]
        bGate = Buf("gate")
        bSCB = Buf("SCB")

        ch_const = Chan("ch_const", sem("c_const"))
        ch_pre = [Chan(f"ch_pre{g}", sem(f"c_pre{g}")) for g in range(2)]
        ch_ring = [Chan(f"ch_ring{s}", sem(f"c_ring{s}")) for s in range(RING)]
        ch_ringsw = [Chan(f"ch_ringsw{s}", sem(f"c_ringsw{s}")) for s in range(RING)]
        ch_x = Chan("ch_x", sem("c_x"))
        ch_xa = [Chan(f"ch_xa{i}", sem(f"c_xa{i}")) for i in range(2)]
        ch_csa = Chan("ch_csa", sem("c_csa"))
        ch_xc = [Chan(f"ch_xc{i}", sem(f"c_xc{i}")) for i in range(2)]
        ch_cs = Chan("ch_cs", sem("c_cs"))
        ch_out = Chan("ch_out", sem("c_out"))
        ch_dbg = Chan("ch_dbg", sem("c_dbg"))

        def vcol(off, c):
            return VEC[:, off + c:off + c + 1]

        seq = ring_sequence()
        ring_state = {"issue": 0, "get": 0, "free": list(range(RING)), "slot_of": {}}
        cast_emitted = set()

        def piece_group(pid):
            return 0 if pid < 2 else (1 if (pid < 9 or pid >= 38) else (2 if pid < 19 else 3))

        def ring_issue():
            st = ring_state
            while st["issue"] < len(seq) and st["free"]:
                n = st["issue"]
                kind, pid = seq[n]
                if kind == "ws" and pid not in cast_emitted:
                    break
                slot = st["free"].pop(0)
                if kind == "wm":
                    tk.dma(POOL, ch_ringsw[slot], RINGT[:, slot, :].rearrange("p (a b) -> p a b", b=2048),
                           WM[pid].rearrange("p (a b) -> p a b", b=2048), reads=[], writes=[bRING[slot]])
                    if pid == NWM1 - 1:
                        bGate.w = Tok(ch_ringsw[slot].sem, ch_ringsw[slot].val, ch_ringsw[slot].name)
                else:
                    tk.dma(SP, ch_ring[slot], RINGT[:, slot, :], WS[pid],
                           reads=[bWSg[piece_group(pid)], bWSh[piece_group(pid)]], writes=[bRING[slot]])
                st["slot_of"][n] = slot
                st["issue"] += 1

        def ring_get(kind, pid):
            st = ring_state
            n = st["get"]
            assert seq[n] == (kind, pid), (n, seq[n], kind, pid)
            if n not in st["slot_of"]:
                ring_issue()
            assert n in st["slot_of"], "ring deadlock: too many pieces held"
            st["get"] += 1
            return st["slot_of"][n]

        def ring_release(slot):
            ring_state["free"].append(slot)
            ring_issue()

        def wview(slot, kcn, cols):
            return RINGT[:, slot, 0:kcn * cols].rearrange("p (k c) -> p k c", c=cols)

        def mm(out_ap, outbuf, pairs, reads):
            n = len(pairs)
            tok = None
            for i, (l, r) in enumerate(pairs):
                tok = tk.op(PE, lambda l=l, r=r, i=i: nc.tensor.matmul(out_ap, l, r, start=(i == 0), stop=(i == n - 1)),
                            reads=reads, writes=[outbuf], inc=(i == n - 1))
            return tok

        def act(out, in_, func, reads, writes, bias=0.0, scale=1.0):
            return tk.op(ACT, lambda: nc.scalar.activation(out, in_, func, bias=bias, scale=scale),
                         reads=reads, writes=writes)

        def tt(out, in0, in1, op, reads, writes, q=None, chain_ok=False):
            q = q or DVE
            return tk.op(q, lambda: q.h.tensor_tensor(out=out, in0=in0, in1=in1, op=op), reads=reads, writes=writes,
                         chain_ok=chain_ok)

        def ts(out, in0, s1, s2, op0, op1, reads, writes, q=None, chain_ok=False):
            q = q or DVE
            if op1 is None:
                return tk.op(q, lambda: q.h.tensor_scalar(out=out, in0=in0, scalar1=s1, scalar2=None, op0=op0),
                             reads=reads, writes=writes, chain_ok=chain_ok)
            return tk.op(q, lambda: q.h.tensor_scalar(out=out, in0=in0, scalar1=s1, scalar2=s2, op0=op0, op1=op1),
                         reads=reads, writes=writes, chain_ok=chain_ok)

        def stt(out, in0, scalar, in1, op0, op1, reads, writes, chain_ok=False):
            return tk.op(DVE, lambda: nc.vector.scalar_tensor_tensor(out=out, in0=in0, scalar=scalar, in1=in1,
                                                                     op0=op0, op1=op1),
                         reads=reads, writes=writes, chain_ok=chain_ok)

        def rsqrt_chain(n, iters=2):
            rs, rv, ra = RS[:, 0:n], RV[:, 0:n], RA[:, 0:n]
            ts(rs.bitcast(I32), rv.bitcast(I32), -0.5, MAGIC, ALU.mult, ALU.add, [bRV], [bRS])
            for _ in range(iters):
                tt(ra, rs, rs, ALU.mult, [bRS], [bRA])
                stt(ra, ra, -0.5, rv, ALU.mult, ALU.mult, [bRA, bRV], [bRA])
                stt(rs, ra, 1.5, rs, ALU.add, ALU.mult, [bRA, bRS], [bRS])

        def rms_stats(srcs, srcbufs, n, inv_dim):
            ps, pb = STAT.next()
            pairs = []
            nsrc = len(srcs)
            for c, (s, b) in enumerate(zip(srcs, srcbufs)):
                q = c % 3
                act(SQ[:, q, 0:n], s, AF.Square, L(b), [bSQ[q]])
                tk.op(PE, lambda q=q, c=c: nc.tensor.matmul(ps[:, 0:n], ONES[:, :], SQ[:, q, 0:n], start=(c == 0),
                                                              stop=(c == nsrc - 1)),
                      reads=[bSQ[q], bONES], writes=[pb], inc=True)
            ts(RV[:, 0:n], ps[:, 0:n], inv_dim, EPS, ALU.mult, ALU.add, [pb], [bRV])
            rsqrt_chain(n)

        def norm_mod(src_of, srcbuf_of, dst_of, dstbuf_of, n, gm_of, shift_of):
            rms_stats([src_of(c) for c in range(NCH)], [srcbuf_of(c) for c in range(NCH)], n, 1.0 / D)
            for c in range(NCH):
                tmp, btmp = (T1, bT1) if c % 2 == 0 else (T2, bT2)
                stt(tmp[:, 0:n], src_of(c), gm_of(c), RS[:, 0:n], ALU.mult, ALU.mult, L(srcbuf_of(c)) + [bRS, bDER], [btmp])
                act(dst_of(c), tmp[:, 0:n], AF.Identity, [btmp, bMOD], [dstbuf_of(c)], bias=shift_of(c), scale=1.0)

        def evac_copy(i, out, in_, reads, writes, scale=1.0):
            if i % 2 == 0:
                act(out, in_, AF.Identity, reads, writes, scale=scale)
            else:
                ts(out, in_, scale, None, ALU.mult, None, reads, writes)

        tk.dma(POOL, ch_const, VEC[:, :], vecs, writes=[bVEC])
        tk.dma(POOL, ch_const, WDW[:, :, :], wdw, writes=[bWDW])
        tk.dma(POOL, ch_const, BMOD[:, :], bmod, writes=[bBMOD])
        tk.dma(POOL, ch_const, CV4[:, :, :], cvec, writes=[bCV4])
        tk.dma(POOL, ch_const, T1[:, 0:P], ident, writes=[bT1])
        full = Tok(ch_const.sem, ch_const.val, ch_const.name)
        for b in (bVEC, bWDW, bBMOD, bCV4, bT1):
            b.w = full
        ts(IDB[:, :], T1[:, 0:P], 1.0, None, ALU.mult, None, [bT1], [bIDB])
        pre_toks = []

        def precast(pids):
            for pid in pids:
                g = piece_group(pid)
                par = len(pre_toks) % 2
                if len(pre_toks) >= 2:
                    tk.wait(POOL, pre_toks[-2])
                t = tk.dma(POOL, ch_pre[par], WS[pid].rearrange("p (a b) -> p a b", b=2048),
                           WF[pid].rearrange("p (a b) -> p a b", b=2048), reads=[bGate], writes=[])
                pre_toks.append(t)
                (bWSg if par == 0 else bWSh)[g].w = t
                cast_emitted.add(pid)

        precast([0, 1])

        tk.op(DVE, lambda: nc.vector.memset(ONES[:, :], 1.0), writes=[bONES])
        tk.op(DVE, lambda: nc.vector.memset(KR2[:, :, :], 0.0), writes=[bKR])
        tk.op(POOL, lambda: nc.gpsimd.memset(XH[:, :, :], 0.0), writes=bXH)
        tk.op(POOL, lambda: nc.gpsimd.memset(XC[:, :, :], 0.0), writes=bXC)
        tk.op(POOL, lambda: nc.gpsimd.memset(MOD[:, :, :], 0.0), writes=[bMOD])
        ts(WDW[:, :, :], WDW[:, :, :], 0.5, None, ALU.mult, None, [bWDW], [bWDW])
        ts(LNH[:, 0, :], VEC[:, V_LNG:V_LNG + 8], 0.5, None, ALU.mult, None, [bVEC], [bLNH])
        ts(LNH[:, 1, :], VEC[:, V_LNB:V_LNB + 8], 0.5, None, ALU.mult, None, [bVEC], [bLNH])
        act(SC[:, :, :], CV4[:, :, :], AF.Tanh, [bCV4], [bSC], scale=0.5)
        ts(SC[:, :, :], SC[:, :, :], 0.5, 0.5, ALU.mult, ALU.add, [bSC], [bSC])
        tt(SC[:, :, :], SC[:, :, :], CV4[:, :, :], ALU.mult, [bSC, bCV4], [bSC])
        ts(SCB[:, :, :], SC[:, :, :], 1.0, None, ALU.mult, None, [bSC], [bSCB])

        ring_issue()

        def mod_part(p0, p1):
            mps, mpb = GEN.next()
            mview = mps[:, 0:192].rearrange("p (a b) -> p a b", b=4)
            for pc in range(p0, p1):
                slot = ring_get("wm", pc) if pc < NWM1 else ring_get("ws", 38 + pc - NWM1)
                wv = wview(slot, NCH, 512)
                for jj in range(4):
                    fc = pc * 4 + jj
                    mm(mview[:, fc, :], mpb, [(wv[:, kc, jj * 128:(jj + 1) * 128], SCB[:, kc, :]) for kc in range(NCH)],
                       reads=[bRING[slot], bSCB])
                ring_release(slot)
            f0, f1 = p0 * 4, p1 * 4
            for i in range(3):
                tt(MOD[:, f0:f1, i], mview[:, f0:f1, i], BMOD[:, f0:f1], ALU.add, [mpb, bBMOD], [bMOD])

        mod_part(0, NWM1)
        for i in range(3):
            stt(DER[:, i, 0, :], MOD[:, 8:16, i], 1.0, VEC[:, V_GMIX:V_GMIX + 8], ALU.add, ALU.mult, [bMOD, bVEC], [bDER])

        pre_rest = list(range(2, 9)) + list(range(38, NPIECE)) + list(range(9, 38))
        pre_chunks = [pre_rest[0:15], pre_rest[15:23], pre_rest[23:30], pre_rest[30:37], pre_rest[37:]]

        def mod_rest():
            mod_part(NWM1, NWM)
            for i in range(3):
                stt(DER[:, i, 1, :], MOD[:, 32:40, i], 1.0, VEC[:, V_GFFN:V_GFFN + 8], ALU.add, ALU.mult, [bMOD, bVEC], [bDER])
                ts(DER[:, i, 2, :], MOD[:, 16:24, i], 0.5, None, ALU.mult, None, [bMOD], [bDER])
                ts(DER[:, i, 3, :], MOD[:, 40:48, i], 0.5, None, ALU.mult, None, [bMOD], [bDER])
            if dbg:
                tk.dma(POOL, ch_dbg, dbgMOD, MOD[:, :, :], reads=[bMOD])

        def L(b):
            return list(b) if isinstance(b, (list, tuple)) else [b]

        def a_sets(st):
            if st == 0:
                return XH, [[b] for b in bXH], ch_xa[0], HX, bHX
            return XB, [[bAQ[2 * c], bAQ[2 * c + 1]] for c in range(NCH)], ch_xa[1], AU, bAU

        def phase_a_front(bl, src_ap, W, k0, smp, st):
            XA, bXA, chx, HA, bHA = a_sets(st)
            tk.dma(POOL, chx, XA[:, :, 0:W], src_ap, writes=[b for bb in bXA for b in bb])
            norm_mod(lambda c: XA[:, c, 0:W], lambda c: bXA[c], lambda c: HA[:, c, 0:W], lambda c: bHA[c], W,
                     lambda c: DER[:, smp, 0, c:c + 1], lambda c: MOD[:, c, smp:smp + 1])

        def phase_a_back(bl, src_ap, W, k0, smp, st, slot, slot1):
            XA, bXA, chx, HA, bHA = a_sets(st)
            wv = wview(slot, NCH, 512)
            kps = []
            for m in range(2):
                ps, pb = GEN.next()
                mm(ps[:, 0:W], pb, [(wv[:, c, m * 128:(m + 1) * 128], HA[:, c, 0:W]) for c in range(NCH)],
                   reads=[bRING[slot]] + bHA)
                kps.append((ps, pb))
            rms_stats([p[0][:, 0:W] for p in kps], [p[1] for p in kps], W, 1.0 / 256)
            for m in range(2):
                stt(NRM[:, m, 0:W], kps[m][0][:, 0:W], vcol(V_GKV, m), RS[:, 0:W], ALU.mult, ALU.mult,
                    [kps[m][1], bRS, bVEC], [bNRM[m]])
            tk.dma(POOL, ch_csa, CS[:, :, 0:W], tab[:, :, k0:k0 + W], writes=[bCS])
            psr, pbr = GEN.next()
            mm(psr[:, 0:W], pbr, [(wv[:, c, 256:384], HA[:, c, 0:W]) for c in range(NCH)], reads=[bRING[slot]] + bHA)
            tt(T1[:, 0:W], psr[:, 0:W], CS[:, 0, 0:W], ALU.mult, [pbr, bCS], [bT1])
            psp, pbp = GEN.next()
            mm(psp[:, 0:W], pbp, [(wv[:, c, 384:512], HA[:, c, 0:W]) for c in range(NCH)], reads=[bRING[slot]] + bHA)
            tt(T2[:, 0:W], psp[:, 0:W], CS[:, 1, 0:W], ALU.mult, [pbp, bCS], [bT2])
            tt(KR2[0:64, 0, k0:k0 + W], T1[0:64, 0:W], T2[0:64, 0:W], ALU.add, [bT1, bT2], [bKR])
            tt(KR2[64:128, 1, k0:k0 + W], T1[64:128, 0:W], T2[64:128, 0:W], ALU.add, [bT1, bT2], [bKR])
            slot = slot1
            wv = wview(slot, 2, 2048)
            for h in range(NH):
                ps, pb = GEN.next()
                mm(ps[:, 0:W], pb, [(wv[:, m, h * 128:(h + 1) * 128], NRM[:, m, 0:W]) for m in range(2)],
                   reads=[bRING[slot], bNRM[0], bNRM[1]])
                evac_copy(0 if h % 4 else 1, KT[:, h, k0:k0 + W], ps[:, 0:W], [pb], [bKT[h]])
            for s in range(W // P):
                for half in range(2):
                    ps, pb = GEN.next()
                    mm(ps[:, :], pb, [(NRM[:, m, s * P:(s + 1) * P], wv[:, m, 1024 + half * 512:1024 + (half + 1) * 512])
                                      for m in range(2)], reads=[bRING[slot], bNRM[0], bNRM[1]])
                    evac_copy(0 if (2 * s + half) % 4 else 1, V[:, k0 // P + s, half * 512:(half + 1) * 512], ps[:, :], [pb], [bV])

        def prologue(bl, j):
            t0 = j * TT
            lo = HL if j == 0 else 0
            hi = HL + TT if j == NT - 1 else HW
            kload = [0]

            def load(c):
                sl = kload[0] % 2
                kload[0] += 1
                tk.dma(ACT, ch_xc[sl], XC[:, sl, lo:hi], xT[bl][:, c, t0 - HL + lo:t0 - HL + hi], writes=[bXC[sl]])
                return sl

            ps, pb = STAT.next()
            ph, pbh = PS[7], PSB[7]
            sl = load(0)
            for c in range(NCH):
                if c > 0:
                    yield
                nxt = load(c + 1) if c + 1 < NCH else load(0)
                q = c % 3
                act(SQ[:, q, :], XC[:, sl, :], AF.Square, [bXC[sl]], [bSQ[q]])
                tk.op(PE, lambda c=c, q=q: nc.tensor.matmul(ps[:, :], ONES[:, :], SQ[:, q, 0:TT], start=(c == 0),
                                                              stop=(c == NCH - 1)),
                      reads=[bSQ[q], bONES], writes=[pb], inc=False)
                tk.op(PE, lambda c=c, q=q: nc.tensor.matmul(ph[:, 0:HW - TT], ONES[:, :], SQ[:, q, TT:HW], start=(c == 0),
                                                              stop=(c == NCH - 1)),
                      reads=[bSQ[q], bONES], writes=[pbh], inc=True)
                sl = nxt
            ts(RV[:, 0:TT], ps[:, :], 1.0 / D, EPS, ALU.mult, ALU.add, [pb], [bRV])
            ts(RV[:, TT:HW], ph[:, 0:HW - TT], 1.0 / D, EPS, ALU.mult, ALU.add, [pbh], [bRV])
            yield
            rsqrt_chain(HW)
            for c in range(NCH):
                yield
                nxt = load(c + 1) if c + 1 < NCH else None
                tmp, btmp = (T1, bT1) if c % 2 == 0 else (T2, bT2)
                stt(tmp[:, :], XC[:, sl, :], DER[:, bl, 0, c:c + 1], RS[:, :], ALU.mult, ALU.mult,
                    [bXC[sl], bRS, bDER], [btmp])
                act(HX[:, c, :], tmp[:, :], AF.Identity, [btmp, bMOD], [bHX[c]], bias=MOD[:, c, bl:bl + 1], scale=1.0)
                sl = nxt

        def phase_b(bl, j):
            t0 = j * TT
            tk.dma(SP, ch_x, XH[:, :, :], xT[bl][:, :, t0:t0 + TT], writes=bXH)
            tk.dma(SP, ch_cs, CS[:, :, :], tab[:, :, CTX + t0:CTX + t0 + TT], writes=[bCS])
            main = slice(HL, HL + TT)

            slot = ring_get("ws", 6)
            wv = wview(slot, NCH, 384)
            qps = []
            for m in range(3):
                ps, pb = GEN.next()
                mm(ps[:, :], pb, [(wv[:, c, m * 128:(m + 1) * 128], HX[:, c, main]) for c in range(NCH)],
                   [bRING[slot]] + bHX)
                qps.append((ps, pb))
            ring_release(slot)
            rms_stats([p[0][:, :] for p in qps], [p[1] for p in qps], TT, 1.0 / 384)
            for m in range(3):
                stt(NRM[:, m, :], qps[m][0][:, :], vcol(V_GQ, m), RS[:, 0:TT], ALU.mult, ALU.mult,
                    [qps[m][1], bRS, bVEC], [bNRM[m]])
            for q4 in range(4):
                slot = ring_get("ws", 2 + q4)
                wv = wview(slot, NCH, 512)
                for jj in range(2):
                    cc = 2 * q4 + jj
                    acol = slice(jj * 256, jj * 256 + 128)
                    gcol = slice(jj * 256 + 128, jj * 256 + 256)
                    psa, pba = GEN.next()
                    mm(psa[:, :], pba, [(wv[:, c, acol], HX[:, c, 0:TT]) for c in range(NCH)], [bRING[slot]] + bHX)
                    psg, pbg = GEN.next()
                    mm(psg[:, :], pbg, [(wv[:, c, gcol], HX[:, c, 0:TT]) for c in range(NCH)], [bRING[slot]] + bHX)
                    psh, pbh = STAT.next()
                    mm(psh[:, 0:32], pbh, [(wv[:, c, acol], HX[:, c, TT:HW]) for c in range(NCH)], [bRING[slot]] + bHX)
                    mm(psh[:, 32:64], pbh, [(wv[:, c, gcol], HX[:, c, TT:HW]) for c in range(NCH)], [bRING[slot]] + bHX)
                    sg = cc % 2
                    act(SG[:, sg, 0:TT], psg[:, :], AF.Tanh, [pbg], [bSG[sg]], scale=0.5)
                    act(SG[:, sg, TT:HW], psh[:, 32:64], AF.Tanh, [pbh], [bSG[sg]], scale=0.5)
                    stt(AU[:, cc, 0:TT], SG[:, sg, 0:TT], 1.0, psa[:, :], ALU.add, ALU.mult, [bSG[sg], pba], [bAU[cc]])
                    stt(AU[:, cc, TT:HW], SG[:, sg, TT:HW], 1.0, psh[:, 0:32], ALU.add, ALU.mult, [bSG[sg], pbh], [bAU[cc]])
                ring_release(slot)
            if j == 0:
                tk.op(DVE, lambda: nc.vector.memset(AU[:, :, 0:HL], 0.0), writes=bAU)
            if j == NT - 1:
                tk.op(DVE, lambda: nc.vector.memset(AU[:, :, HL + TT:HW], 0.0), writes=bAU)

            slot = ring_get("ws", 7)
            wv = wview(slot, 3, 1024)
            for h in range(NH):
                ps, pb = GEN.next()
                mm(ps[:, :], pb, [(wv[:, m, h * 128:(h + 1) * 128], NRM[:, m, :]) for m in range(3)],
                   [bRING[slot]] + bNRM)
                evac_copy(0 if h % 4 else 1, AQ[:, h, :], ps[:, :], [pb], [bAQ[h]])
            ring_release(slot)
            slot = ring_get("ws", 8)
            wv = wview(slot, 3, 1024)
            for hp in range(4):
                psr, pbr = GEN.next()
                mm(psr[:, :], pbr, [(wv[:, m, hp * 128:(hp + 1) * 128], NRM[:, m, :]) for m in range(3)],
                   [bRING[slot]] + bNRM)
                tt(T1[:, 0:TT], psr[:, :], CS[:, 0, :], ALU.mult, [pbr, bCS], [bT1])
                psp, pbp = GEN.next()
                mm(psp[:, :], pbp, [(wv[:, m, 512 + hp * 128:512 + (hp + 1) * 128], NRM[:, m, :]) for m in range(3)],
                   [bRING[slot]] + bNRM)
                tt(T2[:, 0:TT], psp[:, :], CS[:, 1, :], ALU.mult, [pbp, bCS], [bT2])
                tt(AQ[:, 8 + hp, :], T1[:, 0:TT], T2[:, 0:TT], ALU.add, [bT1, bT2], [bAQ[8 + hp]])
            ring_release(slot)

            def conv_pe(cc):
                cps, cpb = STAT.next()
                for k in range(NPE_TAPS):
                    sl = dg_cnt[0] % 4
                    dg_cnt[0] += 1
                    tk.op(POOL, lambda sl=sl, k=k: nc.gpsimd.tensor_scalar(out=DG[:, sl, :], in0=IDB[:, :],
                                                                          scalar1=WDW[:, cc, k:k + 1], scalar2=1.0,
                                                                          op0=ALU.mult, op1=ALU.mult),
                          reads=[bIDB, bWDW], writes=[bDG[sl]])
                    tk.op(PE, lambda sl=sl, k=k: nc.tensor.matmul(cps[:, :], DG[:, sl, :], AU[:, cc, k + 1:k + 1 + TT],
                                                                  start=(k == 0), stop=(k == NPE_TAPS - 1)),
                          reads=[bDG[sl], bAU[cc]], writes=[cpb], inc=True)
                return cps, cpb

            def conv_chunk(cc, cps, cpb):
                k0 = NPE_TAPS
                stt(ACC[:, 0, :], AU[:, cc, k0 + 1:k0 + 1 + TT], WDW[:, cc, k0:k0 + 1], cps[:, :], ALU.mult, ALU.add,
                    [bAU[cc], bWDW, cpb], [bACC[0]])
                act(ACC[:, 1, :], AU[:, cc, k0 + 2:k0 + 2 + TT], AF.Identity, [bAU[cc], bWDW], [bACC[1]],
                    scale=WDW[:, cc, k0 + 1:k0 + 2])
                for k in range(k0 + 2, 31):
                    a = (k - k0) % 2
                    stt(ACC[:, a, :], AU[:, cc, k + 1:k + 1 + TT], WDW[:, cc, k:k + 1], ACC[:, a, :], ALU.mult, ALU.add,
                        [bAU[cc], bWDW, bACC[a]], [bACC[a]])

            def conv_fin(cc):
                stt(AC[:, cc, :], ACC[:, 0, :], vcol(V_BDW, cc), ACC[:, 1, :], ALU.add, ALU.add,
                    [bACC[0], bACC[1], bVEC], [bAC[cc]])

            def s_matmul(h, kc):
                ps, pb = GEN3.next()
                hp, par = h // 2, h % 2
                ksl = slice(kc * P, (kc + 1) * P)
                mm(ps[:, :], pb, [(KT[:, h, ksl], AQ[:, h, :]), (KR2[:, par, ksl], AQ[:, 8 + hp, :])],
                   [bKT[h], bKR, bAQ[h], bAQ[8 + hp]])
                return ps, pb

            cpe = conv_pe(0)
            conv_chunk(0, *cpe)
            conv_fin(0)
            for h in range(NH):
                ops_, opb = OACC.next()
                dps, dpb = DACC.next()
                if h + 1 < NH:
                    cpe = conv_pe(h + 1)
                cur = s_matmul(h, 0)
                for kc in range(NKC):
                    nxt = s_matmul(h, kc + 1) if kc + 1 < NKC else None
                    pi = kc % 3
                    act(PT[:, pi, :], cur[0][:, :], AF.Exp, [cur[1]], [bPT[pi]], scale=ATTN_SCALE)
                    tk.op(PE, lambda kc=kc, pi=pi: nc.tensor.matmul(ops_[:, :], V[:, kc, h * P:(h + 1) * P], PT[:, pi, :],
                                                                    start=(kc == 0), stop=(kc == NKC - 1)),
                          reads=[bV, bPT[pi]], writes=[opb], inc=False)
                    tk.op(PE, lambda kc=kc, pi=pi: nc.tensor.matmul(dps[:, :], ONES[:, :], PT[:, pi, :],
                                                                    start=(kc == 0), stop=(kc == NKC - 1)),
                          reads=[bONES, bPT[pi]], writes=[dpb], inc=True)
                    cur = nxt
                if h + 1 < NH:
                    conv_chunk(h + 1, *cpe)
                    conv_fin(h + 1)
                tk.op(DVE, lambda: nc.vector.reciprocal(RA[:, 0:TT], dps[:, :]), reads=[dpb], writes=[bRA])
                tt(AQ[:, 12 + h, :], ops_[:, :], RA[:, 0:TT], ALU.mult, [opb, bRA], [bAQ[12 + h]])

            ps1, pb1 = STAT.next()
            for c in range(NCH):
                tk.op(PE, lambda c=c: nc.tensor.matmul(ps1[:, :], ONES[:, :], AC[:, c, :], start=(c == 0), stop=(c == NCH - 1)),
                      reads=[bAC[c], bONES], writes=[pb1], inc=(c == NCH - 1))
            ts(T2[:, 0:TT], ps1[:, :], 1.0 / D, None, ALU.mult, None, [pb1], [bT2])
            tt(T1[:, 0:TT], T2[:, 0:TT], T2[:, 0:TT], ALU.mult, [bT2], [bT1])
            ps2, pb2 = STAT.next()
            for c in range(NCH):
                q = c % 3
                act(SQ[:, q, 0:TT], AC[:, c, :], AF.Square, [bAC[c]], [bSQ[q]])
                tk.op(PE, lambda c=c, q=q: nc.tensor.matmul(ps2[:, :], ONES[:, :], SQ[:, q, 0:TT], start=(c == 0),
                                                              stop=(c == NCH - 1)),
                      reads=[bSQ[q], bONES], writes=[pb2], inc=True)
            stt(RV[:, 0:TT], ps2[:, :], 1.0 / D, T1[:, 0:TT], ALU.mult, ALU.subtract, [pb2, bT1], [bRV])
            ts(RV[:, 0:TT], RV[:, 0:TT], EPS, None, ALU.add, None, [bRV], [bRV])
            rsqrt_chain(TT)
            stt(T2[:, 0:TT], T2[:, 0:TT], -1.0, RS[:, 0:TT], ALU.mult, ALU.mult, [bT2, bRS], [bT2])

            def ln_a(c):
                a = c % 2
                tt(ACC[:, a, :], AC[:, c, :], RS[:, 0:TT], ALU.mult, [bAC[c], bRS], [bACC[a]])
                tt(ACC[:, a, :], ACC[:, a, :], T2[:, 0:TT], ALU.add, [bACC[a], bT2], [bACC[a]])
                act(XC[:, a, 0:TT], ACC[:, a, :], AF.Identity, [bACC[a], bLNH], [bXC[a]], bias=LNH[:, 1, c:c + 1],
                    scale=LNH[:, 0, c:c + 1])
                act(SG[:, a, 0:TT], XC[:, a, 0:TT], AF.Tanh, [bXC[a]], [bSG[a]])

            def ln_b(c):
                a = c % 2
                stt(AU[:, c, 0:TT], SG[:, a, 0:TT], 1.0, XC[:, a, 0:TT], ALU.add, ALU.mult, [bSG[a], bXC[a]], [bAU[c]])

            def ln_chunk(c):
                ln_a(c)
                if c > 0:
                    ln_b(c - 1)
                if c == NCH - 1:
                    ln_b(c)

            def gated_proj(first_id, rhs_of, rhs_bufs, evac, extra=None):
                for q4 in range(4):
                    sp = ring_get("ws", first_id + q4)
                    wv = wview(sp, NCH, 512)
                    for jj in range(2):
                        oc = q4 * 2 + jj
                        pcols = slice(jj * 256, jj * 256 + 128)
                        gcols = slice(jj * 256 + 128, jj * 256 + 256)
                        psg, pbg = GEN.next()
                        mm(psg[:, :], pbg, [(wv[:, c, gcols], HX[:, c, main]) for c in range(NCH)], [bRING[sp]] + bHX)
                        gt, bgt = (RA, bRA) if oc % 2 == 0 else (RV, bRV)
                        act(gt[:, 0:TT], psg[:, :], AF.Tanh, [pbg], [bgt], scale=0.5)
                        psy, pby = GEN.next()
                        mm(psy[:, :], pby, [(wv[:, c, pcols], rhs_of(c)) for c in range(NCH)], [bRING[sp]] + rhs_bufs)
                        evac(oc, psy, pby, gt, bgt)
                        if extra is not None:
                            extra(oc)
                    ring_release(sp)

            def evac_ag(oc, psy, pby, gt, bgt):
                stt(AQ[:, oc, :], gt[:, 0:TT], 1.0, psy[:, :], ALU.add, ALU.mult, [bgt, pby], [bAQ[oc]])

            ln_sched = {0: (0, 1), 1: (2, 3), 2: (4,), 3: (5,), 4: (6,), 5: (7,)}

            def ln_extra(oc):
                for c in ln_sched.get(oc, ()):
                    ln_chunk(c)

            gated_proj(13, lambda c: AQ[:, 12 + c, :], bAQ[12:20], evac_ag, extra=ln_extra)

            def evac_mg(oc, psy, pby, gt, bgt):
                tmp, btmp = (T1, bT1) if oc % 2 == 0 else (T2, bT2)
                stt(tmp[:, 0:TT], gt[:, 0:TT], 1.0, psy[:, :], ALU.add, ALU.mult, [bgt, pby], [btmp])
                tt(AQ[:, oc, :], tmp[:, 0:TT], AQ[:, oc, :], ALU.add, [btmp, bAQ[oc]], [bAQ[oc]])

            gated_proj(9, lambda c: AU[:, c, 0:TT], bAU, evac_mg)

            pro = prologue(bl, j + 1) if j + 1 < NT else iter(())

            def pro_step(n=1):
                for _ in range(n):
                    next(pro, None)

            for q2 in range(2):
                slot = ring_get("ws", 17 + q2)
                wv = wview(slot, NCH, 512)
                for jj in range(4):
                    oc = q2 * 4 + jj
                    ps, pb = GEN.next()
                    mm(ps[:, :], pb, [(wv[:, c, jj * 128:(jj + 1) * 128], AQ[:, c, :]) for c in range(NCH)],
                       [bRING[slot]] + bAQ[0:8])
                    stt(XH[:, oc, :], ps[:, :], DER[:, bl, 2, oc:oc + 1], XH[:, oc, :], ALU.mult, ALU.add,
                        [pb, bDER, bXH[oc]], [bXH[oc]])
                ring_release(slot)

            norm_mod(lambda c: XH[:, c, :], lambda c: bXH[c], lambda c: AU[:, c, 0:TT], lambda c: bAU[c], TT,
                     lambda c: DER[:, bl, 1, c:c + 1], lambda c: MOD[:, 24 + c, bl:bl + 1])
            for i in range(11):
                slot = ring_get("ws", 19 + i)
                wv = wview(slot, NCH, 512)
                for jj in range(2):
                    f = 2 * i + jj
                    psa, pba = GEN.next()
                    mm(psa[:, :], pba, [(wv[:, c, jj * 256:jj * 256 + 128], AU[:, c, 0:TT]) for c in range(NCH)],
                       [bRING[slot]] + bAU)
                    sg = f % 2
                    act(SG[:, sg, 0:TT], psa[:, :], AF.Tanh, [pba], [bSG[sg]], scale=0.5)
                    psb, pbb = GEN.next()
                    mm(psb[:, :], pbb, [(wv[:, c, jj * 256 + 128:jj * 256 + 256], AU[:, c, 0:TT]) for c in range(NCH)],
                       [bRING[slot]] + bAU)
                    tmp, btmp = (T1, bT1) if f % 2 == 0 else (T2, bT2)
                    stt(tmp[:, 0:TT], SG[:, sg, 0:TT], 1.0, psa[:, :], ALU.add, ALU.mult, [bSG[sg], pba], [btmp])
                    tt(AQ[:, f, :], tmp[:, 0:TT], psb[:, :], ALU.mult, [btmp, pbb], [bAQ[f]])
                ring_release(slot)
                pro_step()
            for oc in range(NCH):
                pro_step()
                slot = ring_get("ws", 30 + oc)
                wv = wview(slot, NF, 128)
                ps, pb = GEN.next()
                mm(ps[:, :], pb, [(wv[:, f, :], AQ[:, f, :]) for f in range(NF)], [bRING[slot]] + bAQ)
                ring_release(slot)
                stt(XH[:, oc, :], ps[:, :], DER[:, bl, 3, oc:oc + 1], XH[:, oc, :], ALU.mult, ALU.add,
                    [pb, bDER, bXH[oc]], [bXH[oc]])

            for _ in range(20):
                pro_step()
            rms_stats([XH[:, c, :] for c in range(NCH)], [bXH[c] for c in range(NCH)], TT, 1.0 / D)
            for c in range(NCH):
                stt(XH[:, c, :], XH[:, c, :], vcol(V_GFIN, c), RS[:, 0:TT], ALU.mult, ALU.mult,
                    [bXH[c], bVEC, bRS], [bXH[c]])
            tk.dma(SP, ch_out, outT[bl][:, :, t0:t0 + TT], XH[:, :, :], reads=bXH)

        for bl in range(NB):
            blocks = [(bl, cxT[bl], CTX, 0, 2, 0)]
            for j in range(NT):
                blocks.append((bl, xT[bl][:, :, j * TT:(j + 1) * TT], TT, CTX + j * TT, bl, (j + 1) % 2))
            sl0 = ring_get("ws", 0)
            sl1 = ring_get("ws", 1)
            phase_a_front(*blocks[0])
            pro0 = prologue(bl, 0)
            for i, blk in enumerate(blocks):
                if i + 1 < len(blocks):
                    phase_a_front(*blocks[i + 1])
                if bl == 0:
                    precast(pre_chunks[i])
                    ring_issue()
                phase_a_back(*blk, sl0, sl1)
            ring_release(sl0)
            ring_release(sl1)
            if bl == 0:
                mod_rest()
            if dbg and bl == 0:
                tk.dma(POOL, ch_dbg, dbgKT, KT[:, :, :], reads=bKT)
                tk.dma(POOL, ch_dbg, dbgV, V[:, :, :], reads=[bV])
                tk.dma(POOL, ch_dbg, dbgKR, KR2[:, :, :], reads=[bKR])
            for _ in pro0:
                pass
            for j in range(NT):
                phase_b(bl, j)

        assert ring_state["get"] == len(seq), (ring_state["get"], len(seq))
        nc.sync.wait_ge(ch_out.sem, ch_out.val)
        if dbg:
            nc.gpsimd.wait_ge(ch_dbg.sem, ch_dbg.val)
    return nc


def _kc(w):
    K, C = w.shape
    return np.ascontiguousarray(w.reshape(K // P, P, C).transpose(1, 0, 2))


def _piece(a3):
    flat = a3.reshape(P, -1)
    out = np.zeros((P, PIECE), np.float32)
    out[:, :flat.shape[1]] = flat
    return out


def _rope_tables():
    rows = SEQ // 64
    row = np.repeat(np.arange(rows, dtype=np.float32), 64)
    col = np.tile(np.arange(64, dtype=np.float32), rows)
    inv_freq = (np.float32(10000.0) ** (-np.arange(0, 32, 2, dtype=np.float32) / np.float32(32))).astype(np.float32)
    ang = [row[:, None] * inv_freq, col[:, None] * inv_freq]
    cosT = np.ones((64, NK), np.float32)
    sinT = np.zeros((64, NK), np.float32)
    perm = np.zeros(64, np.int64)
    for r in range(64):
        seg, w = r // 32, r % 32
        i, first = w % 16, w < 16
        cosT[r, CTX:] = np.cos(ang[seg][:, i]).astype(np.float32)
        s = np.sin(ang[seg][:, i]).astype(np.float32)
        sinT[r, CTX:] = -s if first else s
        perm[r] = r + 16 if first else r - 16
    tab = np.zeros((P, 2, NK), np.float32)
    tab[:64, 0], tab[64:, 0] = cosT, cosT
    tab[:64, 1], tab[64:, 1] = sinT, sinT
    return tab, perm


def _prep_shared(inp):
    f = lambda k: np.asarray(inp[k], np.float32)
    w_in = f("w_in")[0]
    w_uq = f("w_uq")[0].reshape(384, NH, 192)
    w_ukv = f("w_ukv")[0].reshape(256, NH, 256)
    w_13 = f("w_13")[0]
    w_2 = f("w_2")[0]
    tab, perm = _rope_tables()
    pieces = []
    kr = w_in[:, OFF_KR:OFF_KR + 64]
    krp = kr[:, perm]
    pieces.append(_piece(_kc(np.concatenate([w_in[:, OFF_KV:OFF_KV + 256], kr, kr, krp, krp], axis=1))))
    pieces.append(_piece(_kc(np.concatenate([w_ukv[:, :, :128].reshape(256, 1024),
                                             w_ukv[:, :, 128:].reshape(256, 1024)], axis=1))))
    a, g = w_in[:, 0:1024], w_in[:, 1024:2048]
    for q in range(4):
        cols = []
        for cc in (2 * q, 2 * q + 1):
            cols += [a[:, cc * P:(cc + 1) * P], g[:, cc * P:(cc + 1) * P]]
        pieces.append(_piece(_kc(np.concatenate(cols, axis=1))))
    pieces.append(_piece(_kc(w_in[:, OFF_Q:OFF_Q + 384])))
    pieces.append(_piece(_kc(w_uq[:, :, :128].reshape(384, 1024))))
    rope = w_uq[:, :, 128:]
    pieces.append(_piece(_kc(np.concatenate([rope.reshape(384, 512), rope[:, :, perm].reshape(384, 512)], axis=1))))
    w_pw, w_o, w_out = f("w_pw")[0], f("w_o_mla")[0], f("w_out")[0]
    gc, gm = w_in[:, OFF_GATE:OFF_GATE + 1024], w_in[:, OFF_GATE + 1024:OFF_GATE + 2048]
    for wp_, wg_ in ((w_pw, gc), (w_o, gm)):
        for q in range(4):
            cols = []
            for oc in (2 * q, 2 * q + 1):
                cols += [wp_[:, oc * P:(oc + 1) * P], wg_[:, oc * P:(oc + 1) * P]]
            pieces.append(_piece(_kc(np.concatenate(cols, axis=1))))
    for q in range(2):
        pieces.append(_piece(_kc(w_out[:, q * 512:(q + 1) * 512])))
    w1, w3 = w_13[:, :DFF], w_13[:, DFF:]
    for i in range(11):
        cols = []
        for ff in (2 * i, 2 * i + 1):
            cols += [w1[:, ff * P:(ff + 1) * P], w3[:, ff * P:(ff + 1) * P]]
        pieces.append(_piece(_kc(np.concatenate(cols, axis=1))))
    for oc in range(NCH):
        pieces.append(_piece(_kc(w_2[:, oc * P:(oc + 1) * P])))
    WF = np.stack(pieces)
    assert WF.shape == (38, P, PIECE)
    w_mod = f("w_mod")[0]
    wm_all = [_kc(w_mod[:, pc * 512:(pc + 1) * 512]).reshape(P, PIECE) for pc in range(NWM)]
    WM = np.stack(wm_all[:NWM1])
    WF = np.concatenate([WF, np.stack(wm_all[NWM1:])], axis=0)
    fm = lambda v: np.ascontiguousarray(np.asarray(v, np.float32).reshape(-1, P).T)
    bmod = fm(f("b_mod")[0])
    vecs = np.concatenate([fm(f("g_mix")[0]), fm(f("g_ffn")[0]), fm(f("g_q")[0]), fm(f("g_kv")[0]), fm(f("b_dw")[0]),
                           fm(f("ln_g")[0]), fm(f("ln_b")[0]), fm(f("g_final"))], axis=1)
    assert vecs.shape == (P, NV)
    wdw = np.ascontiguousarray(f("w_dw")[0].reshape(31, NCH, P).transpose(2, 1, 0))
    return dict(WM=WM, WF=WF, bmod=bmod, vecs=np.ascontiguousarray(vecs), wdw=wdw, tab=tab,
                ident=np.eye(P, dtype=np.float32))


def _fmT(a):
    n, T, _ = a.shape
    return np.ascontiguousarray(a.reshape(n, T, NCH, P).transpose(0, 3, 2, 1))


_NC_CACHE = {}


def kernel(**inputs):
    x = np.asarray(inputs["x"], np.float32)
    ctx = np.asarray(inputs["ctx"], np.float32)
    c = np.asarray(inputs["c"], np.float32)
    c_ctx = np.asarray(inputs["c_ctx"], np.float32)
    shared = _prep_shared(inputs)
    in_maps = []
    for core in range(N_CORES):
        b0 = core * NB
        cv = np.zeros((P, NCH, 4), np.float32)
        for i in range(NB):
            cv[:, :, i] = c[b0 + i].reshape(NCH, P).T
        cv[:, :, 2] = c_ctx.reshape(NCH, P).T
        m = dict(shared)
        m["xT"] = _fmT(x[b0:b0 + NB])
        m["cxT"] = _fmT(ctx[b0:b0 + NB])
        m["cvec"] = cv
        in_maps.append(m)
    if "nc" not in _NC_CACHE:
        _NC_CACHE["nc"] = build_nc()
    res = run_bass_kernel_spmd(_NC_CACHE["nc"], in_maps, core_ids=list(range(N_CORES)))
    out = np.empty((N_CORES * NB, SEQ, D), np.float32)
    for core in range(N_CORES):
        oT = np.asarray(res.results[core]["outT"], np.float32)
        out[core * NB:(core + 1) * NB] = oT.transpose(0, 3, 2, 1).reshape(NB, SEQ, D)
    return out
```

```python
import contextlib
import numpy as np
import concourse.bass as bass
import concourse.mybir as mybir
from concourse.bass_utils import run_bass_kernel_spmd

F32 = mybir.dt.float32
BF16 = mybir.dt.bfloat16
I32 = mybir.dt.int32
AF = mybir.ActivationFunctionType
ALU = mybir.AluOpType

P = 128
D = 1024
NCH = 8
SEQ = 2048
CTX = 256
NK = CTX + SEQ
NKC = NK // P
TT = 512
NT = SEQ // TT
HL = 16
HW = TT + 2 * HL
NH = 8
DFF = 2816
NF = DFF // P
EPS = 1e-6
ATTN_SCALE = float(192 ** -0.5)
N_CORES = 8
NB = 2
NPIECE = 46
PIECE = 4096
NWM = 12
NWM1 = 4
RING = 3
MAGIC = float(0x5F3759DF)
NPE_TAPS = 8
PRE_INFLIGHT = 2

V_GMIX, V_GFFN, V_GQ, V_GKV, V_BDW, V_LNG, V_LNB, V_GFIN, NV = 0, 8, 16, 19, 21, 29, 37, 45, 53

OFF_Q = 2048
OFF_KV = OFF_Q + 384
OFF_KR = OFF_KV + 256
OFF_GATE = OFF_KR + 64


class Tok:
    __slots__ = ("sem", "val", "key")

    def __init__(self, sem, val, key):
        self.sem, self.val, self.key = sem, val, key


class Buf:
    __slots__ = ("name", "w", "r")

    def __init__(self, name):
        self.name, self.w, self.r = name, None, {}


class Queue:
    def __init__(self, name, handle, sem, is_pe=False):
        self.name, self.h, self.sem, self.is_pe = name, handle, sem, is_pe
        self.cnt = 0
        self.seen = {}


class Chan:
    def __init__(self, name, sem):
        self.name, self.sem, self.val = name, sem, 0


import os
STRICT = os.environ.get("KSTRICT", "1") != "0"


class Tracker:
    def wait(self, q, tok):
        if tok is None:
            return
        if q.seen.get(tok.key, 0) >= tok.val:
            return
        q.h.wait_ge(tok.sem, tok.val)
        q.seen[tok.key] = tok.val

    def deps(self, q, reads, writes, chain_ok=False, is_dma=False):
        me = None if (is_dma or (STRICT and not q.is_pe)) else q.name
        for b in reads:
            t = b.w
            if t is None:
                continue
            if t.key == me and (q.is_pe or chain_ok):
                continue
            self.wait(q, t)
        for b in writes:
            t = b.w
            if t is not None and t.key != me:
                self.wait(q, t)
            for t in b.r.values():
                if t.key != me:
                    self.wait(q, t)

    def record(self, tok, reads, writes):
        for b in reads:
            old = b.r.get(tok.key)
            if old is None or old.val < tok.val:
                b.r[tok.key] = tok
        for b in writes:
            b.w = tok
            b.r = {}

    def op(self, q, fn, reads=(), writes=(), inc=True, chain_ok=False):
        self.deps(q, reads, writes, chain_ok)
        ins = fn()
        if inc:
            q.cnt += 1
            ins.then_inc(q.sem, 1)
            tok = Tok(q.sem, q.cnt, q.name)
        else:
            tok = Tok(q.sem, q.cnt + 1, q.name)
        self.record(tok, reads, writes)
        return tok

    def dma(self, q, chan, out, in_, reads=(), writes=()):
        self.deps(q, reads, writes, is_dma=True)
        chan.val += 16
        q.h.dma_start(out=out, in_=in_).then_inc(chan.sem, 16)
        tok = Tok(chan.sem, chan.val, chan.name)
        self.record(tok, reads, writes)
        return tok


def ring_sequence():
    seq = [("wm", i) for i in range(NWM1)]
    for _bl in range(NB):
        seq += [("ws", 0), ("ws", 1)]
        if _bl == 0:
            seq += [("ws", 38 + i) for i in range(NWM - NWM1)]
        for _t in range(NT):
            seq += [("ws", p) for p in (6, 2, 3, 4, 5, 7, 8, 13, 14, 15, 16, 9, 10, 11, 12, 17, 18)]
            seq += [("ws", p) for p in range(19, 30)]
            seq += [("ws", p) for p in range(30, 38)]
    return seq


def build_nc(dbg=False):
    nc = bass.Bass("TRN2", target_bir_lowering=False)
    dram = lambda name, shape, dt, kind: nc.dram_tensor(name, shape, dt, kind=kind).ap()
    xT = dram("xT", [NB, P, NCH, SEQ], F32, "ExternalInput")
    cxT = dram("cxT", [NB, P, NCH, CTX], F32, "ExternalInput")
    cvec = dram("cvec", [P, NCH, 4], F32, "ExternalInput")
    WM = dram("WM", [NWM1, P, PIECE], F32, "ExternalInput")
    WF = dram("WF", [NPIECE, P, PIECE], F32, "ExternalInput")
    bmod = dram("bmod", [P, 48], F32, "ExternalInput")
    vecs = dram("vecs", [P, NV], F32, "ExternalInput")
    wdw = dram("wdw", [P, NCH, 31], F32, "ExternalInput")
    tab = dram("tab", [P, 2, NK], F32, "ExternalInput")
    ident = dram("ident", [P, P], F32, "ExternalInput")
    outT = dram("outT", [NB, P, NCH, SEQ], F32, "ExternalOutput")
    WS = dram("WS", [NPIECE, P, PIECE], BF16, "Internal")
    if dbg:
        dbgKT = dram("dbgKT", [P, NH, NK], BF16, "ExternalOutput")
        dbgV = dram("dbgV", [P, NKC, 1024], BF16, "ExternalOutput")
        dbgKR = dram("dbgKR", [P, 2, NK], BF16, "ExternalOutput")
        dbgMOD = dram("dbgMOD", [P, 48, 4], F32, "ExternalOutput")

    tk = Tracker()
    with contextlib.ExitStack() as es:
        sb = lambda name, shape, dt: es.enter_context(nc.sbuf_tensor(name, shape, dt))
        sem = lambda name: es.enter_context(nc.semaphore(name))

        PE = Queue("pe", nc.tensor, sem("s_pe"), is_pe=True)
        ACT = Queue("act", nc.scalar, sem("s_act"))
        DVE = Queue("dve", nc.vector, sem("s_dve"))
        POOL = Queue("pool", nc.gpsimd, sem("s_pool"))
        SP = Queue("sp", nc.sync, sem("s_sp"))

        ONES = sb("ONES", [P, P], BF16)
        VEC = sb("VEC", [P, NV], F32)
        WDW = sb("WDW", [P, NCH, 31], F32)
        BMOD = sb("BMOD", [P, 48], F32)
        MOD = sb("MOD", [P, 48, 4], F32)
        SC = sb("SC", [P, NCH, 4], F32)
        SCB = sb("SCB", [P, NCH, 4], BF16)
        CV4 = sb("CV4", [P, NCH, 4], F32)
        DER = sb("DER", [P, 3, 4, NCH], F32)
        LNH = sb("LNH", [P, 2, NCH], F32)
        KT = sb("KT", [P, NH, NK], BF16)
        KR2 = sb("KR2", [P, 2, NK], BF16)
        V = sb("V", [P, NKC, 1024], BF16)
        XH = sb("XH", [P, NCH, TT], F32)
        XC = sb("XC", [P, 2, HW], F32)
        HX = sb("HX", [P, NCH, HW], BF16)
        AU = sb("AU", [P, NCH, HW], BF16)
        AC = sb("AC", [P, NCH, TT], BF16)
        AQF = sb("AQF", [P, NF * TT], BF16)
        AQ = AQF[:, :].rearrange("p (f t) -> p f t", t=TT)
        XB = AQF[:, 0:2 * NCH * TT].bitcast(F32).rearrange("p (c t) -> p c t", t=TT)
        SQ = sb("SQ", [P, 3, HW], BF16)
        PT = sb("PT", [P, 3, TT], BF16)
        SG = sb("SG", [P, 2, HW], BF16)
        RS = sb("RS", [P, HW], F32)
        RV = sb("RV", [P, HW], F32)
        RA = sb("RA", [P, HW], F32)
        T1 = sb("T1", [P, HW], F32)
        T2 = sb("T2", [P, HW], F32)
        ACC = sb("ACC", [P, 2, TT], F32)
        IDB = sb("IDB", [P, P], BF16)
        DG = sb("DG", [P, 4, P], BF16)
        CS = sb("CS", [P, 2, TT], F32)
        NRM = sb("NRM", [P, 3, TT], BF16)
        RINGT = sb("RINGT", [P, RING, PIECE], BF16)

        PS = [es.enter_context(nc.psum_tensor(f"ps{i}", [P, TT], F32)) for i in range(8)]
        PSB = [Buf(f"ps{i}") for i in range(8)]

        class Rot:
            def __init__(self, ids):
                self.ids, self.i = ids, 0

            def next(self):
                j = self.ids[self.i % len(self.ids)]
                self.i += 1
                return PS[j], PSB[j]

        GEN3 = Rot([0, 1, 2])
        GEN7 = Rot([0, 1, 2, 4, 5, 6])
        GEN = GEN7
        STAT = Rot([3])
        OACC = Rot([4, 5])
        DACC = Rot([6, 7])

        bVEC, bWDW, bBMOD, bMOD, bSC, bCV4, bDER, bLNH, bONES = (Buf(n) for n in
                                                                    "VEC WDW BMOD MOD SC CV4 DER LNH ONES".split())
        bKT = [Buf(f"KT{h}") for h in range(NH)]
        bKR = Buf("KR2")
        bV = Buf("V")
        bXH = [Buf(f"XH{c}") for c in range(NCH)]
        bXC = [Buf(f"XC{c}") for c in range(2)]
        bHX = [Buf(f"HX{c}") for c in range(NCH)]
        bAU = [Buf(f"AU{c}") for c in range(NCH)]
        bAC = [Buf(f"AC{c}") for c in range(NCH)]
        bAQ = [Buf(f"AQ{c}") for c in range(NF)]
        bSQ = [Buf(f"SQ{c}") for c in range(3)]
        bPT = [Buf(f"PT{c}") for c in range(3)]
        bSG = [Buf(f"SG{c}") for c in range(2)]
        bRS, bRV, bRA, bT1, bT2, bCS, bIDB = (Buf(n) for n in "RS RV RA T1 T2 CS IDB".split())
        bACC = [Buf("ACC0"), Buf("ACC1")]
        bDG = [Buf(f"DG{i}") for i in range(4)]
        dg_cnt = [0]
        bNRM = [Buf(f"NRM{c}") for c in range(3)]
        bRING = [Buf(f"RING{c}") for c in range(RING)]
        bWSg = [Buf(f"WSg{g}") for g in range(4)]
        bWSh = [Buf(f"WSh{g}") for g in range(4)]
        bGate = Buf("gate")
        bSCB = Buf("SCB")

        ch_const = Chan("ch_const", sem("c_const"))
        ch_pre = [Chan(f"ch_pre{g}", sem(f"c_pre{g}")) for g in range(2)]
        ch_ring = [Chan(f"ch_ring{s}", sem(f"c_ring{s}")) for s in range(RING)]
        ch_ringsw = [Chan(f"ch_ringsw{s}", sem(f"c_ringsw{s}")) for s in range(RING)]
        ch_x = Chan("ch_x", sem("c_x"))
        ch_xa = [Chan(f"ch_xa{i}", sem(f"c_xa{i}")) for i in range(2)]
        ch_csa = Chan("ch_csa", sem("c_csa"))
        ch_xc = [Chan(f"ch_xc{i}", sem(f"c_xc{i}")) for i in range(2)]
        ch_cs = Chan("ch_cs", sem("c_cs"))
        ch_out = Chan("ch_out", sem("c_out"))
        ch_dbg = Chan("ch_dbg", sem("c_dbg"))

        def vcol(off, c):
            return VEC[:, off + c:off + c + 1]

        seq = ring_sequence()
        ring_state = {"issue": 0, "get": 0, "free": list(range(RING)), "slot_of": {}}
        cast_emitted = set()

        def piece_group(pid):
            return 0 if pid < 2 else (1 if (pid < 9 or pid >= 38) else (2 if pid < 19 else 3))

        def ring_issue():
            st = ring_state
            while st["issue"] < len(seq) and st["free"]:
                n = st["issue"]
                kind, pid = seq[n]
                if kind == "ws" and pid not in cast_emitted:
                    break
                slot = st["free"].pop(0)
                if kind == "wm":
                    tk.dma(POOL, ch_ringsw[slot], RINGT[:, slot, :].rearrange("p (a b) -> p a b", b=2048),
                           WM[pid].rearrange("p (a b) -> p a b", b=2048), reads=[], writes=[bRING[slot]])
                    if pid == NWM1 - 1:
                        bGate.w = Tok(ch_ringsw[slot].sem, ch_ringsw[slot].val, ch_ringsw[slot].name)
                else:
                    tk.dma(SP, ch_ring[slot], RINGT[:, slot, :], WS[pid],
                           reads=[bWSg[piece_group(pid)], bWSh[piece_group(pid)]], writes=[bRING[slot]])
                st["slot_of"][n] = slot
                st["issue"] += 1

        def ring_get(kind, pid):
            st = ring_state
            n = st["get"]
            assert seq[n] == (kind, pid), (n, seq[n], kind, pid)
            if n not in st["slot_of"]:
                ring_issue()
            assert n in st["slot_of"], "ring deadlock: too many pieces held"
            st["get"] += 1
            return st["slot_of"][n]

        def ring_release(slot):
            ring_state["free"].append(slot)
            ring_issue()

        def wview(slot, kcn, cols):
            return RINGT[:, slot, 0:kcn * cols].rearrange("p (k c) -> p k c", c=cols)

        def mm(out_ap, outbuf, pairs, reads):
            n = len(pairs)
            tok = None
            for i, (l, r) in enumerate(pairs):
                tok = tk.op(PE, lambda l=l, r=r, i=i: nc.tensor.matmul(out_ap, l, r, start=(i == 0), stop=(i == n - 1)),
                            reads=reads, writes=[outbuf], inc=(i == n - 1))
            return tok

        def act(out, in_, func, reads, writes, bias=0.0, scale=1.0):
            return tk.op(ACT, lambda: nc.scalar.activation(out, in_, func, bias=bias, scale=scale),
                         reads=reads, writes=writes)

        def tt(out, in0, in1, op, reads, writes, q=None, chain_ok=False):
            q = q or DVE
            return tk.op(q, lambda: q.h.tensor_tensor(out=out, in0=in0, in1=in1, op=op), reads=reads, writes=writes,
                         chain_ok=chain_ok)

        def ts(out, in0, s1, s2, op0, op1, reads, writes, q=None, chain_ok=False):
            q = q or DVE
            if op1 is None:
                return tk.op(q, lambda: q.h.tensor_scalar(out=out, in0=in0, scalar1=s1, scalar2=None, op0=op0),
                             reads=reads, writes=writes, chain_ok=chain_ok)
            return tk.op(q, lambda: q.h.tensor_scalar(out=out, in0=in0, scalar1=s1, scalar2=s2, op0=op0, op1=op1),
                         reads=reads, writes=writes, chain_ok=chain_ok)

        def stt(out, in0, scalar, in1, op0, op1, reads, writes, chain_ok=False):
            return tk.op(DVE, lambda: nc.vector.scalar_tensor_tensor(out=out, in0=in0, scalar=scalar, in1=in1,
                                                                     op0=op0, op1=op1),
                         reads=reads, writes=writes, chain_ok=chain_ok)

        def rsqrt_chain(n, iters=2):
            rs, rv, ra = RS[:, 0:n], RV[:, 0:n], RA[:, 0:n]
            ts(rs.bitcast(I32), rv.bitcast(I32), -0.5, MAGIC, ALU.mult, ALU.add, [bRV], [bRS])
            for _ in range(iters):
                tt(ra, rs, rs, ALU.mult, [bRS], [bRA])
                stt(ra, ra, -0.5, rv, ALU.mult, ALU.mult, [bRA, bRV], [bRA])
                stt(rs, ra, 1.5, rs, ALU.add, ALU.mult, [bRA, bRS], [bRS])

        def rms_stats(srcs, srcbufs, n, inv_dim):
            ps, pb = STAT.next()
            pairs = []
            nsrc = len(srcs)
            for c, (s, b) in enumerate(zip(srcs, srcbufs)):
                q = c % 3
                act(SQ[:, q, 0:n], s, AF.Square, L(b), [bSQ[q]])
                tk.op(PE, lambda q=q, c=c: nc.tensor.matmul(ps[:, 0:n], ONES[:, :], SQ[:, q, 0:n], start=(c == 0),
                                                              stop=(c == nsrc - 1)),
                      reads=[bSQ[q], bONES], writes=[pb], inc=True)
            ts(RV[:, 0:n], ps[:, 0:n], inv_dim, EPS, ALU.mult, ALU.add, [pb], [bRV])
            rsqrt_chain(n)

        def norm_mod(src_of, srcbuf_of, dst_of, dstbuf_of, n, gm_of, shift_of):
            rms_stats([src_of(c) for c in range(NCH)], [srcbuf_of(c) for c in range(NCH)], n, 1.0 / D)
            for c in range(NCH):
                tmp, btmp = (T1, bT1) if c % 2 == 0 else (T2, bT2)
                stt(tmp[:, 0:n], src_of(c), gm_of(c), RS[:, 0:n], ALU.mult, ALU.mult, L(srcbuf_of(c)) + [bRS, bDER], [btmp])
                act(dst_of(c), tmp[:, 0:n], AF.Identity, [btmp, bMOD], [dstbuf_of(c)], bias=shift_of(c), scale=1.0)

        def evac_copy(i, out, in_, reads, writes, scale=1.0):
            if i % 2 == 0:
                act(out, in_, AF.Identity, reads, writes, scale=scale)
            else:
                ts(out, in_, scale, None, ALU.mult, None, reads, writes)

        tk.dma(POOL, ch_const, VEC[:, :], vecs, writes=[bVEC])
        tk.dma(POOL, ch_const, WDW[:, :, :], wdw, writes=[bWDW])
        tk.dma(POOL, ch_const, BMOD[:, :], bmod, writes=[bBMOD])
        tk.dma(POOL, ch_const, CV4[:, :, :], cvec, writes=[bCV4])
        tk.dma(POOL, ch_const, T1[:, 0:P], ident, writes=[bT1])
        full = Tok(ch_const.sem, ch_const.val, ch_const.name)
        for b in (bVEC, bWDW, bBMOD, bCV4, bT1):
            b.w = full
        ts(IDB[:, :], T1[:, 0:P], 1.0, None, ALU.mult, None, [bT1], [bIDB])
        pre_toks = []

        def precast(pids):
            for pid in pids:
                g = piece_group(pid)
                par = len(pre_toks) % 2
                if len(pre_toks) >= 2:
                    tk.wait(POOL, pre_toks[-2])
                t = tk.dma(POOL, ch_pre[par], WS[pid].rearrange("p (a b) -> p a b", b=2048),
                           WF[pid].rearrange("p (a b) -> p a b", b=2048), reads=[bGate], writes=[])
                pre_toks.append(t)
                (bWSg if par == 0 else bWSh)[g].w = t
                cast_emitted.add(pid)

        precast([0, 1])

        tk.op(DVE, lambda: nc.vector.memset(ONES[:, :], 1.0), writes=[bONES])
        tk.op(DVE, lambda: nc.vector.memset(KR2[:, :, :], 0.0), writes=[bKR])
        tk.op(POOL, lambda: nc.gpsimd.memset(XH[:, :, :], 0.0), writes=bXH)
        tk.op(POOL, lambda: nc.gpsimd.memset(XC[:, :, :], 0.0), writes=bXC)
        tk.op(POOL, lambda: nc.gpsimd.memset(MOD[:, :, :], 0.0), writes=[bMOD])
        ts(WDW[:, :, :], WDW[:, :, :], 0.5, None, ALU.mult, None, [bWDW], [bWDW])
        ts(LNH[:, 0, :], VEC[:, V_LNG:V_LNG + 8], 0.5, None, ALU.mult, None, [bVEC], [bLNH])
        ts(LNH[:, 1, :], VEC[:, V_LNB:V_LNB + 8], 0.5, None, ALU.mult, None, [bVEC], [bLNH])
        act(SC[:, :, :], CV4[:, :, :], AF.Tanh, [bCV4], [bSC], scale=0.5)
        ts(SC[:, :, :], SC[:, :, :], 0.5, 0.5, ALU.mult, ALU.add, [bSC], [bSC])
        tt(SC[:, :, :], SC[:, :, :], CV4[:, :, :], ALU.mult, [bSC, bCV4], [bSC])
        ts(SCB[:, :, :], SC[:, :, :], 1.0, None, ALU.mult, None, [bSC], [bSCB])

        ring_issue()

        def mod_part(p0, p1):
            mps, mpb = GEN.next()
            mview = mps[:, 0:192].rearrange("p (a b) -> p a b", b=4)
            for pc in range(p0, p1):
                slot = ring_get("wm", pc) if pc < NWM1 else ring_get("ws", 38 + pc - NWM1)
                wv = wview(slot, NCH, 512)
                for jj in range(4):
                    fc = pc * 4 + jj
                    mm(mview[:, fc, :], mpb, [(wv[:, kc, jj * 128:(jj + 1) * 128], SCB[:, kc, :]) for kc in range(NCH)],
                       reads=[bRING[slot], bSCB])
                ring_release(slot)
            f0, f1 = p0 * 4, p1 * 4
            for i in range(3):
                tt(MOD[:, f0:f1, i], mview[:, f0:f1, i], BMOD[:, f0:f1], ALU.add, [mpb, bBMOD], [bMOD])

        mod_part(0, NWM1)
        for i in range(3):
            stt(DER[:, i, 0, :], MOD[:, 8:16, i], 1.0, VEC[:, V_GMIX:V_GMIX + 8], ALU.add, ALU.mult, [bMOD, bVEC], [bDER])

        pre_rest = list(range(2, 9)) + list(range(38, NPIECE)) + list(range(9, 38))
        pre_chunks = [pre_rest[0:15], pre_rest[15:23], pre_rest[23:30], pre_rest[30:37], pre_rest[37:]]

        def mod_rest():
            mod_part(NWM1, NWM)
            for i in range(3):
                stt(DER[:, i, 1, :], MOD[:, 32:40, i], 1.0, VEC[:, V_GFFN:V_GFFN + 8], ALU.add, ALU.mult, [bMOD, bVEC], [bDER])
                ts(DER[:, i, 2, :], MOD[:, 16:24, i], 0.5, None, ALU.mult, None, [bMOD], [bDER])
                ts(DER[:, i, 3, :], MOD[:, 40:48, i], 0.5, None, ALU.mult, None, [bMOD], [bDER])
            if dbg:
                tk.dma(POOL, ch_dbg, dbgMOD, MOD[:, :, :], reads=[bMOD])

        def L(b):
            return list(b) if isinstance(b, (list, tuple)) else [b]

        def a_sets(st):
            if st == 0:
                return XH, [[b] for b in bXH], ch_xa[0], HX, bHX
            return XB, [[bAQ[2 * c], bAQ[2 * c + 1]] for c in range(NCH)], ch_xa[1], AU, bAU

        def phase_a_front(bl, src_ap, W, k0, smp, st):
            XA, bXA, chx, HA, bHA = a_sets(st)
            tk.dma(POOL, chx, XA[:, :, 0:W], src_ap, writes=[b for bb in bXA for b in bb])
            norm_mod(lambda c: XA[:, c, 0:W], lambda c: bXA[c], lambda c: HA[:, c, 0:W], lambda c: bHA[c], W,
                     lambda c: DER[:, smp, 0, c:c + 1], lambda c: MOD[:, c, smp:smp + 1])

        def phase_a_back(bl, src_ap, W, k0, smp, st, slot, slot1):
            XA, bXA, chx, HA, bHA = a_sets(st)
            wv = wview(slot, NCH, 512)
            kps = []
            for m in range(2):
                ps, pb = GEN.next()
                mm(ps[:, 0:W], pb, [(wv[:, c, m * 128:(m + 1) * 128], HA[:, c, 0:W]) for c in range(NCH)],
                   reads=[bRING[slot]] + bHA)
                kps.append((ps, pb))
            rms_stats([p[0][:, 0:W] for p in kps], [p[1] for p in kps], W, 1.0 / 256)
            for m in range(2):
                stt(NRM[:, m, 0:W], kps[m][0][:, 0:W], vcol(V_GKV, m), RS[:, 0:W], ALU.mult, ALU.mult,
                    [kps[m][1], bRS, bVEC], [bNRM[m]])
            tk.dma(POOL, ch_csa, CS[:, :, 0:W], tab[:, :, k0:k0 + W], writes=[bCS])
            psr, pbr = GEN.next()
            mm(psr[:, 0:W], pbr, [(wv[:, c, 256:384], HA[:, c, 0:W]) for c in range(NCH)], reads=[bRING[slot]] + bHA)
            tt(T1[:, 0:W], psr[:, 0:W], CS[:, 0, 0:W], ALU.mult, [pbr, bCS], [bT1])
            psp, pbp = GEN.next()
            mm(psp[:, 0:W], pbp, [(wv[:, c, 384:512], HA[:, c, 0:W]) for c in range(NCH)], reads=[bRING[slot]] + bHA)
            tt(T2[:, 0:W], psp[:, 0:W], CS[:, 1, 0:W], ALU.mult, [pbp, bCS], [bT2])
            tt(KR2[0:64, 0, k0:k0 + W], T1[0:64, 0:W], T2[0:64, 0:W], ALU.add, [bT1, bT2], [bKR])
            tt(KR2[64:128, 1, k0:k0 + W], T1[64:128, 0:W], T2[64:128, 0:W], ALU.add, [bT1, bT2], [bKR])
            slot = slot1
            wv = wview(slot, 2, 2048)
            for h in range(NH):
                ps, pb = GEN.next()
                mm(ps[:, 0:W], pb, [(wv[:, m, h * 128:(h + 1) * 128], NRM[:, m, 0:W]) for m in range(2)],
                   reads=[bRING[slot], bNRM[0], bNRM[1]])
                evac_copy(0 if h % 4 else 1, KT[:, h, k0:k0 + W], ps[:, 0:W], [pb], [bKT[h]])
            for s in range(W // P):
                for half in range(2):
                    ps, pb = GEN.next()
                    mm(ps[:, :], pb, [(NRM[:, m, s * P:(s + 1) * P], wv[:, m, 1024 + half * 512:1024 + (half + 1) * 512])
                                      for m in range(2)], reads=[bRING[slot], bNRM[0], bNRM[1]])
                    evac_copy(0 if (2 * s + half) % 4 else 1, V[:, k0 // P + s, half * 512:(half + 1) * 512], ps[:, :], [pb], [bV])

        def prologue(bl, j):
            t0 = j * TT
            lo = HL if j == 0 else 0
            hi = HL + TT if j == NT - 1 else HW
            kload = [0]

            def load(c):
                sl = kload[0] % 2
                kload[0] += 1
                tk.dma(ACT, ch_xc[sl], XC[:, sl, lo:hi], xT[bl][:, c, t0 - HL + lo:t0 - HL + hi], writes=[bXC[sl]])
                return sl

            ps, pb = STAT.next()
            ph, pbh = PS[7], PSB[7]
            sl = load(0)
            for c in range(NCH):
                if c > 0:
                    yield
                nxt = load(c + 1) if c + 1 < NCH else load(0)
                q = c % 3
                act(SQ[:, q, :], XC[:, sl, :], AF.Square, [bXC[sl]], [bSQ[q]])
                tk.op(PE, lambda c=c, q=q: nc.tensor.matmul(ps[:, :], ONES[:, :], SQ[:, q, 0:TT], start=(c == 0),
                                                              stop=(c == NCH - 1)),
                      reads=[bSQ[q], bONES], writes=[pb], inc=False)
                tk.op(PE, lambda c=c, q=q: nc.tensor.matmul(ph[:, 0:HW - TT], ONES[:, :], SQ[:, q, TT:HW], start=(c == 0),
                                                              stop=(c == NCH - 1)),
                      reads=[bSQ[q], bONES], writes=[pbh], inc=True)
                sl = nxt
            ts(RV[:, 0:TT], ps[:, :], 1.0 / D, EPS, ALU.mult, ALU.add, [pb], [bRV])
            ts(RV[:, TT:HW], ph[:, 0:HW - TT], 1.0 / D, EPS, ALU.mult, ALU.add, [pbh], [bRV])
            yield
            rsqrt_chain(HW)
            for c in range(NCH):
                yield
                nxt = load(c + 1) if c + 1 < NCH else None
                tmp, btmp = (T1, bT1) if c % 2 == 0 else (T2, bT2)
                stt(tmp[:, :], XC[:, sl, :], DER[:, bl, 0, c:c + 1], RS[:, :], ALU.mult, ALU.mult,
                    [bXC[sl], bRS, bDER], [btmp])
                act(HX[:, c, :], tmp[:, :], AF.Identity, [btmp, bMOD], [bHX[c]], bias=MOD[:, c, bl:bl + 1], scale=1.0)
                sl = nxt

        def phase_b(bl, j):
            t0 = j * TT
            tk.dma(SP, ch_x, XH[:, :, :], xT[bl][:, :, t0:t0 + TT], writes=bXH)
            tk.dma(SP, ch_cs, CS[:, :, :], tab[:, :, CTX + t0:CTX + t0 + TT], writes=[bCS])
            main = slice(HL, HL + TT)

            slot = ring_get("ws", 6)
            wv = wview(slot, NCH, 384)
            qps = []
            for m in range(3):
                ps, pb = GEN.next()
                mm(ps[:, :], pb, [(wv[:, c, m * 128:(m + 1) * 128], HX[:, c, main]) for c in range(NCH)],
                   [bRING[slot]] + bHX)
                qps.append((ps, pb))
            ring_release(slot)
            rms_stats([p[0][:, :] for p in qps], [p[1] for p in qps], TT, 1.0 / 384)
            for m in range(3):
                stt(NRM[:, m, :], qps[m][0][:, :], vcol(V_GQ, m), RS[:, 0:TT], ALU.mult, ALU.mult,
                    [qps[m][1], bRS, bVEC], [bNRM[m]])
            for q4 in range(4):
                slot = ring_get("ws", 2 + q4)
                wv = wview(slot, NCH, 512)
                for jj in range(2):
                    cc = 2 * q4 + jj
                    acol = slice(jj * 256, jj * 256 + 128)
                    gcol = slice(jj * 256 + 128, jj * 256 + 256)
                    psa, pba = GEN.next()
                    mm(psa[:, :], pba, [(wv[:, c, acol], HX[:, c, 0:TT]) for c in range(NCH)], [bRING[slot]] + bHX)
                    psg, pbg = GEN.next()
                    mm(psg[:, :], pbg, [(wv[:, c, gcol], HX[:, c, 0:TT]) for c in range(NCH)], [bRING[slot]] + bHX)
                    psh, pbh = STAT.next()
                    mm(psh[:, 0:32], pbh, [(wv[:, c, acol], HX[:, c, TT:HW]) for c in range(NCH)], [bRING[slot]] + bHX)
                    mm(psh[:, 32:64], pbh, [(wv[:, c, gcol], HX[:, c, TT:HW]) for c in range(NCH)], [bRING[slot]] + bHX)
                    sg = cc % 2
                    act(SG[:, sg, 0:TT], psg[:, :], AF.Tanh, [pbg], [bSG[sg]], scale=0.5)
                    act(SG[:, sg, TT:HW], psh[:, 32:64], AF.Tanh, [pbh], [bSG[sg]], scale=0.5)
                    stt(AU[:, cc, 0:TT], SG[:, sg, 0:TT], 1.0, psa[:, :], ALU.add, ALU.mult, [bSG[sg], pba], [bAU[cc]])
                    stt(AU[:, cc, TT:HW], SG[:, sg, TT:HW], 1.0, psh[:, 0:32], ALU.add, ALU.mult, [bSG[sg], pbh], [bAU[cc]])
                ring_release(slot)
            if j == 0:
                tk.op(DVE, lambda: nc.vector.memset(AU[:, :, 0:HL], 0.0), writes=bAU)
            if j == NT - 1:
                tk.op(DVE, lambda: nc.vector.memset(AU[:, :, HL + TT:HW], 0.0), writes=bAU)

            slot = ring_get("ws", 7)
            wv = wview(slot, 3, 1024)
            for h in range(NH):
                ps, pb = GEN.next()
                mm(ps[:, :], pb, [(wv[:, m, h * 128:(h + 1) * 128], NRM[:, m, :]) for m in range(3)],
                   [bRING[slot]] + bNRM)
                evac_copy(0 if h % 4 else 1, AQ[:, h, :], ps[:, :], [pb], [bAQ[h]])
            ring_release(slot)
            slot = ring_get("ws", 8)
            wv = wview(slot, 3, 1024)
            for hp in range(4):
                psr, pbr = GEN.next()
                mm(psr[:, :], pbr, [(wv[:, m, hp * 128:(hp + 1) * 128], NRM[:, m, :]) for m in range(3)],
                   [bRING[slot]] + bNRM)
                tt(T1[:, 0:TT], psr[:, :], CS[:, 0, :], ALU.mult, [pbr, bCS], [bT1])
                psp, pbp = GEN.next()
                mm(psp[:, :], pbp, [(wv[:, m, 512 + hp * 128:512 + (hp + 1) * 128], NRM[:, m, :]) for m in range(3)],
                   [bRING[slot]] + bNRM)
                tt(T2[:, 0:TT], psp[:, :], CS[:, 1, :], ALU.mult, [pbp, bCS], [bT2])
                tt(AQ[:, 8 + hp, :], T1[:, 0:TT], T2[:, 0:TT], ALU.add, [bT1, bT2], [bAQ[8 + hp]])
            ring_release(slot)

            def conv_pe_tap(cc, k, cps, cpb):
                sl = dg_cnt[0] % 4
                dg_cnt[0] += 1
                tk.op(POOL, lambda: nc.gpsimd.tensor_scalar(out=DG[:, sl, :], in0=IDB[:, :], scalar1=WDW[:, cc, k:k + 1],
                                                          scalar2=1.0, op0=ALU.mult, op1=ALU.mult),
                      reads=[bIDB, bWDW], writes=[bDG[sl]])
                tk.op(PE, lambda: nc.tensor.matmul(cps[:, :], DG[:, sl, :], AU[:, cc, k + 1:k + 1 + TT],
                                                   start=(k == 0), stop=(k == NPE_TAPS - 1)),
                      reads=[bDG[sl], bAU[cc]], writes=[cpb], inc=True)

            def conv_pe(cc):
                cps, cpb = STAT.next()
                for k in range(NPE_TAPS):
                    conv_pe_tap(cc, k, cps, cpb)
                return cps, cpb

            def conv_chunk(cc, cps, cpb):
                k0 = NPE_TAPS
                stt(ACC[:, 0, :], AU[:, cc, k0 + 1:k0 + 1 + TT], WDW[:, cc, k0:k0 + 1], cps[:, :], ALU.mult, ALU.add,
                    [bAU[cc], bWDW, cpb], [bACC[0]])
                act(ACC[:, 1, :], AU[:, cc, k0 + 2:k0 + 2 + TT], AF.Identity, [bAU[cc], bWDW], [bACC[1]],
                    scale=WDW[:, cc, k0 + 1:k0 + 2])
                for k in range(k0 + 2, 31):
                    a = (k - k0) % 2
                    stt(ACC[:, a, :], AU[:, cc, k + 1:k + 1 + TT], WDW[:, cc, k:k + 1], ACC[:, a, :], ALU.mult, ALU.add,
                        [bAU[cc], bWDW, bACC[a]], [bACC[a]])

            def conv_fin(cc):
                stt(AC[:, cc, :], ACC[:, 0, :], vcol(V_BDW, cc), ACC[:, 1, :], ALU.add, ALU.add,
                    [bACC[0], bACC[1], bVEC], [bAC[cc]])

            def s_matmul(h, kc):
                ps, pb = GEN3.next()
                hp, par = h // 2, h % 2
                ksl = slice(kc * P, (kc + 1) * P)
                mm(ps[:, :], pb, [(KT[:, h, ksl], AQ[:, h, :]), (KR2[:, par, ksl], AQ[:, 8 + hp, :])],
                   [bKT[h], bKR, bAQ[h], bAQ[8 + hp]])
                return ps, pb

            cpe = conv_pe(0)
            conv_chunk(0, *cpe)
            conv_fin(0)
            for h in range(NH):
                ops_, opb = OACC.next()
                dps, dpb = DACC.next()
                if h + 1 < NH:
                    cpe = STAT.next()
                cur = s_matmul(h, 0)
                for kc in range(NKC):
                    if h + 1 < NH and kc % 2 == 1 and kc // 2 < NPE_TAPS:
                        conv_pe_tap(h + 1, kc // 2, *cpe)
                    nxt = s_matmul(h, kc + 1) if kc + 1 < NKC else None
                    pi = kc % 3
                    act(PT[:, pi, :], cur[0][:, :], AF.Exp, [cur[1]], [bPT[pi]], scale=ATTN_SCALE)
                    tk.op(PE, lambda kc=kc, pi=pi: nc.tensor.matmul(ops_[:, :], V[:, kc, h * P:(h + 1) * P], PT[:, pi, :],
                                                                    start=(kc == 0), stop=(kc == NKC - 1)),
                          reads=[bV, bPT[pi]], writes=[opb], inc=False)
                    tk.op(PE, lambda kc=kc, pi=pi: nc.tensor.matmul(dps[:, :], ONES[:, :], PT[:, pi, :],
                                                                    start=(kc == 0), stop=(kc == NKC - 1)),
                          reads=[bONES, bPT[pi]], writes=[dpb], inc=True)
                    cur = nxt
                if h + 1 < NH:
                    conv_chunk(h + 1, *cpe)
                    conv_fin(h + 1)
                tk.op(DVE, lambda: nc.vector.reciprocal(RA[:, 0:TT], dps[:, :]), reads=[dpb], writes=[bRA])
                tt(AQ[:, 12 + h, :], ops_[:, :], RA[:, 0:TT], ALU.mult, [opb, bRA], [bAQ[12 + h]])

            ps1, pb1 = STAT.next()
            for c in range(NCH):
                tk.op(PE, lambda c=c: nc.tensor.matmul(ps1[:, :], ONES[:, :], AC[:, c, :], start=(c == 0), stop=(c == NCH - 1)),
                      reads=[bAC[c], bONES], writes=[pb1], inc=(c == NCH - 1))
            ts(T2[:, 0:TT], ps1[:, :], 1.0 / D, None, ALU.mult, None, [pb1], [bT2])
            tt(T1[:, 0:TT], T2[:, 0:TT], T2[:, 0:TT], ALU.mult, [bT2], [bT1])
            ps2, pb2 = STAT.next()
            for c in range(NCH):
                q = c % 3
                act(SQ[:, q, 0:TT], AC[:, c, :], AF.Square, [bAC[c]], [bSQ[q]])
                tk.op(PE, lambda c=c, q=q: nc.tensor.matmul(ps2[:, :], ONES[:, :], SQ[:, q, 0:TT], start=(c == 0),
                                                              stop=(c == NCH - 1)),
                      reads=[bSQ[q], bONES], writes=[pb2], inc=True)
            stt(RV[:, 0:TT], ps2[:, :], 1.0 / D, T1[:, 0:TT], ALU.mult, ALU.subtract, [pb2, bT1], [bRV])
            ts(RV[:, 0:TT], RV[:, 0:TT], EPS, None, ALU.add, None, [bRV], [bRV])
            rsqrt_chain(TT)
            stt(T2[:, 0:TT], T2[:, 0:TT], -1.0, RS[:, 0:TT], ALU.mult, ALU.mult, [bT2, bRS], [bT2])

            def ln_a(c):
                a = c % 2
                tt(ACC[:, a, :], AC[:, c, :], RS[:, 0:TT], ALU.mult, [bAC[c], bRS], [bACC[a]])
                tt(ACC[:, a, :], ACC[:, a, :], T2[:, 0:TT], ALU.add, [bACC[a], bT2], [bACC[a]])
                act(XC[:, a, 0:TT], ACC[:, a, :], AF.Identity, [bACC[a], bLNH], [bXC[a]], bias=LNH[:, 1, c:c + 1],
                    scale=LNH[:, 0, c:c + 1])
                act(SG[:, a, 0:TT], XC[:, a, 0:TT], AF.Tanh, [bXC[a]], [bSG[a]])

            def ln_b(c):
                a = c % 2
                stt(AU[:, c, 0:TT], SG[:, a, 0:TT], 1.0, XC[:, a, 0:TT], ALU.add, ALU.mult, [bSG[a], bXC[a]], [bAU[c]])

            def ln_chunk(c):
                ln_a(c)
                if c > 0:
                    ln_b(c - 1)
                if c == NCH - 1:
                    ln_b(c)

            def gated_proj(first_id, rhs_of, rhs_bufs, evac, extra=None):
                for q4 in range(4):
                    sp = ring_get("ws", first_id + q4)
                    wv = wview(sp, NCH, 512)
                    for jj in range(2):
                        oc = q4 * 2 + jj
                        pcols = slice(jj * 256, jj * 256 + 128)
                        gcols = slice(jj * 256 + 128, jj * 256 + 256)
                        psg, pbg = GEN.next()
                        mm(psg[:, :], pbg, [(wv[:, c, gcols], HX[:, c, main]) for c in range(NCH)], [bRING[sp]] + bHX)
                        gt, bgt = (RA, bRA) if oc % 2 == 0 else (RV, bRV)
                        act(gt[:, 0:TT], psg[:, :], AF.Tanh, [pbg], [bgt], scale=0.5)
                        psy, pby = GEN.next()
                        mm(psy[:, :], pby, [(wv[:, c, pcols], rhs_of(c)) for c in range(NCH)], [bRING[sp]] + rhs_bufs)
                        evac(oc, psy, pby, gt, bgt)
                        if extra is not None:
                            extra(oc)
                    ring_release(sp)

            def evac_ag(oc, psy, pby, gt, bgt):
                stt(AQ[:, oc, :], gt[:, 0:TT], 1.0, psy[:, :], ALU.add, ALU.mult, [bgt, pby], [bAQ[oc]])

            ln_sched = {0: (0, 1), 1: (2, 3), 2: (4,), 3: (5,), 4: (6,), 5: (7,)}

            def ln_extra(oc):
                for c in ln_sched.get(oc, ()):
                    ln_chunk(c)

            gated_proj(13, lambda c: AQ[:, 12 + c, :], bAQ[12:20], evac_ag, extra=ln_extra)

            def evac_mg(oc, psy, pby, gt, bgt):
                tmp, btmp = (T1, bT1) if oc % 2 == 0 else (T2, bT2)
                stt(tmp[:, 0:TT], gt[:, 0:TT], 1.0, psy[:, :], ALU.add, ALU.mult, [bgt, pby], [btmp])
                tt(AQ[:, oc, :], tmp[:, 0:TT], AQ[:, oc, :], ALU.add, [btmp, bAQ[oc]], [bAQ[oc]])

            gated_proj(9, lambda c: AU[:, c, 0:TT], bAU, evac_mg)

            pro = prologue(bl, j + 1) if j + 1 < NT else iter(())

            def pro_step(n=1):
                for _ in range(n):
                    next(pro, None)

            for q2 in range(2):
                slot = ring_get("ws", 17 + q2)
                wv = wview(slot, NCH, 512)
                for jj in range(4):
                    oc = q2 * 4 + jj
                    ps, pb = GEN.next()
                    mm(ps[:, :], pb, [(wv[:, c, jj * 128:(jj + 1) * 128], AQ[:, c, :]) for c in range(NCH)],
                       [bRING[slot]] + bAQ[0:8])
                    stt(XH[:, oc, :], ps[:, :], DER[:, bl, 2, oc:oc + 1], XH[:, oc, :], ALU.mult, ALU.add,
                        [pb, bDER, bXH[oc]], [bXH[oc]])
                ring_release(slot)

            norm_mod(lambda c: XH[:, c, :], lambda c: bXH[c], lambda c: AU[:, c, 0:TT], lambda c: bAU[c], TT,
                     lambda c: DER[:, bl, 1, c:c + 1], lambda c: MOD[:, 24 + c, bl:bl + 1])
            for i in range(11):
                slot = ring_get("ws", 19 + i)
                wv = wview(slot, NCH, 512)
                for jj in range(2):
                    f = 2 * i + jj
                    psa, pba = GEN.next()
                    mm(psa[:, :], pba, [(wv[:, c, jj * 256:jj * 256 + 128], AU[:, c, 0:TT]) for c in range(NCH)],
                       [bRING[slot]] + bAU)
                    sg = f % 2
                    act(SG[:, sg, 0:TT], psa[:, :], AF.Tanh, [pba], [bSG[sg]], scale=0.5)
                    psb, pbb = GEN.next()
                    mm(psb[:, :], pbb, [(wv[:, c, jj * 256 + 128:jj * 256 + 256], AU[:, c, 0:TT]) for c in range(NCH)],
                       [bRING[slot]] + bAU)
                    tmp, btmp = (T1, bT1) if f % 2 == 0 else (T2, bT2)
                    stt(tmp[:, 0:TT], SG[:, sg, 0:TT], 1.0, psa[:, :], ALU.add, ALU.mult, [bSG[sg], pba], [btmp])
                    tt(AQ[:, f, :], tmp[:, 0:TT], psb[:, :], ALU.mult, [btmp, pbb], [bAQ[f]])
                ring_release(slot)
                pro_step()
            for oc in range(NCH):
                pro_step()
                slot = ring_get("ws", 30 + oc)
                wv = wview(slot, NF, 128)
                ps, pb = GEN.next()
                mm(ps[:, :], pb, [(wv[:, f, :], AQ[:, f, :]) for f in range(NF)], [bRING[slot]] + bAQ)
                ring_release(slot)
                stt(XH[:, oc, :], ps[:, :], DER[:, bl, 3, oc:oc + 1], XH[:, oc, :], ALU.mult, ALU.add,
                    [pb, bDER, bXH[oc]], [bXH[oc]])

            for _ in range(20):
                pro_step()
            rms_stats([XH[:, c, :] for c in range(NCH)], [bXH[c] for c in range(NCH)], TT, 1.0 / D)
            for c in range(NCH):
                stt(XH[:, c, :], XH[:, c, :], vcol(V_GFIN, c), RS[:, 0:TT], ALU.mult, ALU.mult,
                    [bXH[c], bVEC, bRS], [bXH[c]])
            tk.dma(SP, ch_out, outT[bl][:, :, t0:t0 + TT], XH[:, :, :], reads=bXH)

        for bl in range(NB):
            blocks = [(bl, cxT[bl], CTX, 0, 2, 0)]
            for j in range(NT):
                blocks.append((bl, xT[bl][:, :, j * TT:(j + 1) * TT], TT, CTX + j * TT, bl, (j + 1) % 2))
            sl0 = ring_get("ws", 0)
            sl1 = ring_get("ws", 1)
            phase_a_front(*blocks[0])
            pro0 = prologue(bl, 0)
            for i, blk in enumerate(blocks):
                if i + 1 < len(blocks):
                    phase_a_front(*blocks[i + 1])
                if bl == 0:
                    precast(pre_chunks[i])
                    ring_issue()
                phase_a_back(*blk, sl0, sl1)
            ring_release(sl0)
            ring_release(sl1)
            if bl == 0:
                mod_rest()
            if dbg and bl == 0:
                tk.dma(POOL, ch_dbg, dbgKT, KT[:, :, :], reads=bKT)
                tk.dma(POOL, ch_dbg, dbgV, V[:, :, :], reads=[bV])
                tk.dma(POOL, ch_dbg, dbgKR, KR2[:, :, :], reads=[bKR])
            for _ in pro0:
                pass
            for j in range(NT):
                phase_b(bl, j)

        assert ring_state["get"] == len(seq), (ring_state["get"], len(seq))
        nc.sync.wait_ge(ch_out.sem, ch_out.val)
        if dbg:
            nc.gpsimd.wait_ge(ch_dbg.sem, ch_dbg.val)
    return nc


def _kc(w):
    K, C = w.shape
    return np.ascontiguousarray(w.reshape(K // P, P, C).transpose(1, 0, 2))


def _piece(a3):
    flat = a3.reshape(P, -1)
    out = np.zeros((P, PIECE), np.float32)
    out[:, :flat.shape[1]] = flat
    return out


def _rope_tables():
    rows = SEQ // 64
    row = np.repeat(np.arange(rows, dtype=np.float32), 64)
    col = np.tile(np.arange(64, dtype=np.float32), rows)
    inv_freq = (np.float32(10000.0) ** (-np.arange(0, 32, 2, dtype=np.float32) / np.float32(32))).astype(np.float32)
    ang = [row[:, None] * inv_freq, col[:, None] * inv_freq]
    cosT = np.ones((64, NK), np.float32)
    sinT = np.zeros((64, NK), np.float32)
    perm = np.zeros(64, np.int64)
    for r in range(64):
        seg, w = r // 32, r % 32
        i, first = w % 16, w < 16
        cosT[r, CTX:] = np.cos(ang[seg][:, i]).astype(np.float32)
        s = np.sin(ang[seg][:, i]).astype(np.float32)
        sinT[r, CTX:] = -s if first else s
        perm[r] = r + 16 if first else r - 16
    tab = np.zeros((P, 2, NK), np.float32)
    tab[:64, 0], tab[64:, 0] = cosT, cosT
    tab[:64, 1], tab[64:, 1] = sinT, sinT
    return tab, perm


def _prep_shared(inp):
    f = lambda k: np.asarray(inp[k], np.float32)
    w_in = f("w_in")[0]
    w_uq = f("w_uq")[0].reshape(384, NH, 192)
    w_ukv = f("w_ukv")[0].reshape(256, NH, 256)
    w_13 = f("w_13")[0]
    w_2 = f("w_2")[0]
    tab, perm = _rope_tables()
    pieces = []
    kr = w_in[:, OFF_KR:OFF_KR + 64]
    krp = kr[:, perm]
    pieces.append(_piece(_kc(np.concatenate([w_in[:, OFF_KV:OFF_KV + 256], kr, kr, krp, krp], axis=1))))
    pieces.append(_piece(_kc(np.concatenate([w_ukv[:, :, :128].reshape(256, 1024),
                                             w_ukv[:, :, 128:].reshape(256, 1024)], axis=1))))
    a, g = w_in[:, 0:1024], w_in[:, 1024:2048]
    for q in range(4):
        cols = []
        for cc in (2 * q, 2 * q + 1):
            cols += [a[:, cc * P:(cc + 1) * P], g[:, cc * P:(cc + 1) * P]]
        pieces.append(_piece(_kc(np.concatenate(cols, axis=1))))
    pieces.append(_piece(_kc(w_in[:, OFF_Q:OFF_Q + 384])))
    pieces.append(_piece(_kc(w_uq[:, :, :128].reshape(384, 1024))))
    rope = w_uq[:, :, 128:]
    pieces.append(_piece(_kc(np.concatenate([rope.reshape(384, 512), rope[:, :, perm].reshape(384, 512)], axis=1))))
    w_pw, w_o, w_out = f("w_pw")[0], f("w_o_mla")[0], f("w_out")[0]
    gc, gm = w_in[:, OFF_GATE:OFF_GATE + 1024], w_in[:, OFF_GATE + 1024:OFF_GATE + 2048]
    for wp_, wg_ in ((w_pw, gc), (w_o, gm)):
        for q in range(4):
            cols = []
            for oc in (2 * q, 2 * q + 1):
                cols += [wp_[:, oc * P:(oc + 1) * P], wg_[:, oc * P:(oc + 1) * P]]
            pieces.append(_piece(_kc(np.concatenate(cols, axis=1))))
    for q in range(2):
        pieces.append(_piece(_kc(w_out[:, q * 512:(q + 1) * 512])))
    w1, w3 = w_13[:, :DFF], w_13[:, DFF:]
    for i in range(11):
        cols = []
        for ff in (2 * i, 2 * i + 1):
            cols += [w1[:, ff * P:(ff + 1) * P], w3[:, ff * P:(ff + 1) * P]]
        pieces.append(_piece(_kc(np.concatenate(cols, axis=1))))
    for oc in range(NCH):
        pieces.append(_piece(_kc(w_2[:, oc * P:(oc + 1) * P])))
    WF = np.stack(pieces)
    assert WF.shape == (38, P, PIECE)
    w_mod = f("w_mod")[0]
    wm_all = [_kc(w_mod[:, pc * 512:(pc + 1) * 512]).reshape(P, PIECE) for pc in range(NWM)]
    WM = np.stack(wm_all[:NWM1])
    WF = np.concatenate([WF, np.stack(wm_all[NWM1:])], axis=0)
    fm = lambda v: np.ascontiguousarray(np.asarray(v, np.float32).reshape(-1, P).T)
    bmod = fm(f("b_mod")[0])
    vecs = np.concatenate([fm(f("g_mix")[0]), fm(f("g_ffn")[0]), fm(f("g_q")[0]), fm(f("g_kv")[0]), fm(f("b_dw")[0]),
                           fm(f("ln_g")[0]), fm(f("ln_b")[0]), fm(f("g_final"))], axis=1)
    assert vecs.shape == (P, NV)
    wdw = np.ascontiguousarray(f("w_dw")[0].reshape(31, NCH, P).transpose(2, 1, 0))
    return dict(WM=WM, WF=WF, bmod=bmod, vecs=np.ascontiguousarray(vecs), wdw=wdw, tab=tab,
                ident=np.eye(P, dtype=np.float32))


def _fmT(a):
    n, T, _ = a.shape
    return np.ascontiguousarray(a.reshape(n, T, NCH, P).transpose(0, 3, 2, 1))


_NC_CACHE = {}


def kernel(**inputs):
    x = np.asarray(inputs["x"], np.float32)
    ctx = np.asarray(inputs["ctx"], np.float32)
    c = np.asarray(inputs["c"], np.float32)
    c_ctx = np.asarray(inputs["c_ctx"], np.float32)
    shared = _prep_shared(inputs)
    in_maps = []
    for core in range(N_CORES):
        b0 = core * NB
        cv = np.zeros((P, NCH, 4), np.float32)
        for i in range(NB):
            cv[:, :, i] = c[b0 + i].reshape(NCH, P).T
        cv[:, :, 2] = c_ctx.reshape(NCH, P).T
        m = dict(shared)
        m["xT"] = _fmT(x[b0:b0 + NB])
        m["cxT"] = _fmT(ctx[b0:b0 + NB])
        m["cvec"] = cv
        in_maps.append(m)
    if "nc" not in _NC_CACHE:
        _NC_CACHE["nc"] = build_nc()
    res = run_bass_kernel_spmd(_NC_CACHE["nc"], in_maps, core_ids=list(range(N_CORES)))
    out = np.empty((N_CORES * NB, SEQ, D), np.float32)
    for core in range(N_CORES):
        oT = np.asarray(res.results[core]["outT"], np.float32)
        out[core * NB:(core + 1) * NB] = oT.transpose(0, 3, 2, 1).reshape(NB, SEQ, D)
    return out
```

```python
import contextlib
import numpy as np
import concourse.bass as bass
import concourse.mybir as mybir
from concourse.bass_utils import run_bass_kernel_spmd

F32 = mybir.dt.float32
BF16 = mybir.dt.bfloat16
I32 = mybir.dt.int32
AF = mybir.ActivationFunctionType
ALU = mybir.AluOpType

P = 128
D = 1024
NCH = 8
SEQ = 2048
CTX = 256
NK = CTX + SEQ
NKC = NK // P
TT = 512
NT = SEQ // TT
HL = 16
HW = TT + 2 * HL
NH = 8
DFF = 2816
NF = DFF // P
EPS = 1e-6
ATTN_SCALE = float(192 ** -0.5)
N_CORES = 8
NB = 2
NPIECE = 46
PIECE = 4096
NWM = 12
NWM1 = 4
RING = 3
MAGIC = float(0x5F3759DF)
NPE_TAPS = 9
PRE_INFLIGHT = 2

V_GMIX, V_GFFN, V_GQ, V_GKV, V_BDW, V_LNG, V_LNB, V_GFIN, NV = 0, 8, 16, 19, 21, 29, 37, 45, 53

OFF_Q = 2048
OFF_KV = OFF_Q + 384
OFF_KR = OFF_KV + 256
OFF_GATE = OFF_KR + 64


class Tok:
    __slots__ = ("sem", "val", "key")

    def __init__(self, sem, val, key):
        self.sem, self.val, self.key = sem, val, key


class Buf:
    __slots__ = ("name", "w", "r")

    def __init__(self, name):
        self.name, self.w, self.r = name, None, {}


class Queue:
    def __init__(self, name, handle, sem, is_pe=False):
        self.name, self.h, self.sem, self.is_pe = name, handle, sem, is_pe
        self.cnt = 0
        self.seen = {}


class Chan:
    def __init__(self, name, sem):
        self.name, self.sem, self.val = name, sem, 0


import os
STRICT = os.environ.get("KSTRICT", "1") != "0"


class Tracker:
    def wait(self, q, tok):
        if tok is None:
            return
        if q.seen.get(tok.key, 0) >= tok.val:
            return
        q.h.wait_ge(tok.sem, tok.val)
        q.seen[tok.key] = tok.val

    def deps(self, q, reads, writes, chain_ok=False, is_dma=False):
        me = None if (is_dma or (STRICT and not q.is_pe)) else q.name
        for b in reads:
            t = b.w
            if t is None:
                continue
            if t.key == me and (q.is_pe or chain_ok):
                continue
            self.wait(q, t)
        for b in writes:
            t = b.w
            if t is not None and t.key != me:
                self.wait(q, t)
            for t in b.r.values():
                if t.key != me:
                    self.wait(q, t)

    def record(self, tok, reads, writes):
        for b in reads:
            old = b.r.get(tok.key)
            if old is None or old.val < tok.val:
                b.r[tok.key] = tok
        for b in writes:
            b.w = tok
            b.r = {}

    def op(self, q, fn, reads=(), writes=(), inc=True, chain_ok=False):
        self.deps(q, reads, writes, chain_ok)
        ins = fn()
        if inc:
            q.cnt += 1
            ins.then_inc(q.sem, 1)
            tok = Tok(q.sem, q.cnt, q.name)
        else:
            tok = Tok(q.sem, q.cnt + 1, q.name)
        self.record(tok, reads, writes)
        return tok

    def dma(self, q, chan, out, in_, reads=(), writes=()):
        self.deps(q, reads, writes, is_dma=True)
        chan.val += 16
        q.h.dma_start(out=out, in_=in_).then_inc(chan.sem, 16)
        tok = Tok(chan.sem, chan.val, chan.name)
        self.record(tok, reads, writes)
        return tok


def ring_sequence():
    seq = [("wm", i) for i in range(NWM1)]
    for _bl in range(NB):
        seq += [("ws", 0), ("ws", 1)]
        if _bl == 0:
            seq += [("ws", 38 + i) for i in range(NWM - NWM1)]
        for _t in range(NT):
            seq += [("ws", p) for p in (6, 2, 3, 4, 5, 7, 8, 13, 14, 15, 16, 9, 10, 11, 12, 17, 18)]
            seq += [("ws", p) for p in range(19, 30)]
            seq += [("ws", p) for p in range(30, 38)]
    return seq


def build_nc(dbg=False):
    nc = bass.Bass("TRN2", target_bir_lowering=False)
    dram = lambda name, shape, dt, kind: nc.dram_tensor(name, shape, dt, kind=kind).ap()
    xT = dram("xT", [NB, P, NCH, SEQ], F32, "ExternalInput")
    cxT = dram("cxT", [NB, P, NCH, CTX], F32, "ExternalInput")
    cvec = dram("cvec", [P, NCH, 4], F32, "ExternalInput")
    WM = dram("WM", [NWM1, P, PIECE], F32, "ExternalInput")
    WF = dram("WF", [NPIECE, P, PIECE], F32, "ExternalInput")
    bmod = dram("bmod", [P, 48], F32, "ExternalInput")
    vecs = dram("vecs", [P, NV], F32, "ExternalInput")
    wdw = dram("wdw", [P, NCH, 31], F32, "ExternalInput")
    tab = dram("tab", [P, 2, NK], F32, "ExternalInput")
    ident = dram("ident", [P, P], F32, "ExternalInput")
    outT = dram("outT", [NB, P, NCH, SEQ], F32, "ExternalOutput")
    WS = dram("WS", [NPIECE, P, PIECE], BF16, "Internal")
    if dbg:
        dbgKT = dram("dbgKT", [P, NH, NK], BF16, "ExternalOutput")
        dbgV = dram("dbgV", [P, NKC, 1024], BF16, "ExternalOutput")
        dbgKR = dram("dbgKR", [P, 2, NK], BF16, "ExternalOutput")
        dbgMOD = dram("dbgMOD", [P, 48, 4], F32, "ExternalOutput")

    tk = Tracker()
    with contextlib.ExitStack() as es:
        sb = lambda name, shape, dt: es.enter_context(nc.sbuf_tensor(name, shape, dt))
        sem = lambda name: es.enter_context(nc.semaphore(name))

        PE = Queue("pe", nc.tensor, sem("s_pe"), is_pe=True)
        ACT = Queue("act", nc.scalar, sem("s_act"))
        DVE = Queue("dve", nc.vector, sem("s_dve"))
        POOL = Queue("pool", nc.gpsimd, sem("s_pool"))
        SP = Queue("sp", nc.sync, sem("s_sp"))

        ONES = sb("ONES", [P, P], BF16)
        VEC = sb("VEC", [P, NV], F32)
        WDW = sb("WDW", [P, NCH, 31], F32)
        BMOD = sb("BMOD", [P, 48], F32)
        MOD = sb("MOD", [P, 48, 4], F32)
        SC = sb("SC", [P, NCH, 4], F32)
        SCB = sb("SCB", [P, NCH, 4], BF16)
        CV4 = sb("CV4", [P, NCH, 4], F32)
        DER = sb("DER", [P, 3, 4, NCH], F32)
        LNH = sb("LNH", [P, 2, NCH], F32)
        KT = sb("KT", [P, NH, NK], BF16)
        KR2 = sb("KR2", [P, 2, NK], BF16)
        V = sb("V", [P, NKC, 1024], BF16)
        XH = sb("XH", [P, NCH, TT], F32)
        XC = sb("XC", [P, 2, HW], F32)
        HX = sb("HX", [P, NCH, HW], BF16)
        AU = sb("AU", [P, NCH, HW], BF16)
        AC = sb("AC", [P, NCH, TT], BF16)
        AQF = sb("AQF", [P, NF * TT], BF16)
        AQ = AQF[:, :].rearrange("p (f t) -> p f t", t=TT)
        XB = AQF[:, 0:2 * NCH * TT].bitcast(F32).rearrange("p (c t) -> p c t", t=TT)
        SQ = sb("SQ", [P, 3, HW], BF16)
        PT = sb("PT", [P, 3, TT], BF16)
        SG = sb("SG", [P, 2, HW], BF16)
        RS = sb("RS", [P, HW], F32)
        RV = sb("RV", [P, HW], F32)
        RA = sb("RA", [P, HW], F32)
        T1 = sb("T1", [P, HW], F32)
        T2 = sb("T2", [P, HW], F32)
        ACC = sb("ACC", [P, 2, TT], F32)
        IDB = sb("IDB", [P, P], BF16)
        DG = sb("DG", [P, 4, P], BF16)
        CS = sb("CS", [P, 2, TT], F32)
        NRM = sb("NRM", [P, 3, TT], BF16)
        RINGT = sb("RINGT", [P, RING, PIECE], BF16)

        PS = [es.enter_context(nc.psum_tensor(f"ps{i}", [P, TT], F32)) for i in range(8)]
        PSB = [Buf(f"ps{i}") for i in range(8)]

        class Rot:
            def __init__(self, ids):
                self.ids, self.i = ids, 0

            def next(self):
                j = self.ids[self.i % len(self.ids)]
                self.i += 1
                return PS[j], PSB[j]

        GEN3 = Rot([0, 1, 2])
        GEN7 = Rot([0, 1, 2, 4, 5, 6])
        GEN = GEN7
        STAT = Rot([3])
        OACC = Rot([4, 5])
        DACC = Rot([6, 7])

        bVEC, bWDW, bBMOD, bMOD, bSC, bCV4, bDER, bLNH, bONES = (Buf(n) for n in
                                                                    "VEC WDW BMOD MOD SC CV4 DER LNH ONES".split())
        bKT = [Buf(f"KT{h}") for h in range(NH)]
        bKR = Buf("KR2")
        bV = Buf("V")
        bXH = [Buf(f"XH{c}") for c in range(NCH)]
        bXC = [Buf(f"XC{c}") for c in range(2)]
        bHX = [Buf(f"HX{c}") for c in range(NCH)]
        bAU = [Buf(f"AU{c}") for c in range(NCH)]
        bAC = [Buf(f"AC{c}") for c in range(NCH)]
        bAQ = [Buf(f"AQ{c}") for c in range(NF)]
        bSQ = [Buf(f"SQ{c}") for c in range(3)]
        bPT = [Buf(f"PT{c}") for c in range(3)]
        bSG = [Buf(f"SG{c}") for c in range(2)]
        bRS, bRV, bRA, bT1, bT2, bCS, bIDB = (Buf(n) for n in "RS RV RA T1 T2 CS IDB".split())
        bACC = [Buf("ACC0"), Buf("ACC1")]
        bDG = [Buf(f"DG{i}") for i in range(4)]
        dg_cnt = [0]
        bNRM = [Buf(f"NRM{c}") for c in range(3)]
        bRING = [Buf(f"RING{c}") for c in range(RING)]
        bWSg = [Buf(f"WSg{g}") for g in range(4)]
        bWSh = [Buf(f"WSh{g}") for g in range(4)]
        bGate = Buf("gate")
        bSCB = Buf("SCB")

        ch_const = Chan("ch_const", sem("c_const"))
        ch_pre = [Chan(f"ch_pre{g}", sem(f"c_pre{g}")) for g in range(2)]
        ch_ring = [Chan(f"ch_ring{s}", sem(f"c_ring{s}")) for s in range(RING)]
        ch_ringsw = [Chan(f"ch_ringsw{s}", sem(f"c_ringsw{s}")) for s in range(RING)]
        ch_x = Chan("ch_x", sem("c_x"))
        ch_xa = [Chan(f"ch_xa{i}", sem(f"c_xa{i}")) for i in range(2)]
        ch_csa = Chan("ch_csa", sem("c_csa"))
        ch_xc = [Chan(f"ch_xc{i}", sem(f"c_xc{i}")) for i in range(2)]
        ch_cs = Chan("ch_cs", sem("c_cs"))
        ch_out = Chan("ch_out", sem("c_out"))
        ch_dbg = Chan("ch_dbg", sem("c_dbg"))

        def vcol(off, c):
            return VEC[:, off + c:off + c + 1]

        seq = ring_sequence()
        ring_state = {"issue": 0, "get": 0, "free": list(range(RING)), "slot_of": {}}
        cast_emitted = set()

        def piece_group(pid):
            return 0 if pid < 2 else (1 if (pid < 9 or pid >= 38) else (2 if pid < 19 else 3))

        def ring_issue():
            st = ring_state
            while st["issue"] < len(seq) and st["free"]:
                n = st["issue"]
                kind, pid = seq[n]
                if kind == "ws" and pid not in cast_emitted:
                    break
                slot = st["free"].pop(0)
                if kind == "wm":
                    tk.dma(POOL, ch_ringsw[slot], RINGT[:, slot, :].rearrange("p (a b) -> p a b", b=2048),
                           WM[pid].rearrange("p (a b) -> p a b", b=2048), reads=[], writes=[bRING[slot]])
                    if pid == NWM1 - 1:
                        bGate.w = Tok(ch_ringsw[slot].sem, ch_ringsw[slot].val, ch_ringsw[slot].name)
                else:
                    tk.dma(SP, ch_ring[slot], RINGT[:, slot, :], WS[pid],
                           reads=[bWSg[piece_group(pid)], bWSh[piece_group(pid)]], writes=[bRING[slot]])
                st["slot_of"][n] = slot
                st["issue"] += 1

        def ring_get(kind, pid):
            st = ring_state
            n = st["get"]
            assert seq[n] == (kind, pid), (n, seq[n], kind, pid)
            if n not in st["slot_of"]:
                ring_issue()
            assert n in st["slot_of"], "ring deadlock: too many pieces held"
            st["get"] += 1
            return st["slot_of"][n]

        def ring_release(slot):
            ring_state["free"].append(slot)
            ring_issue()

        def wview(slot, kcn, cols):
            return RINGT[:, slot, 0:kcn * cols].rearrange("p (k c) -> p k c", c=cols)

        def mm(out_ap, outbuf, pairs, reads):
            n = len(pairs)
            tok = None
            for i, (l, r) in enumerate(pairs):
                tok = tk.op(PE, lambda l=l, r=r, i=i: nc.tensor.matmul(out_ap, l, r, start=(i == 0), stop=(i == n - 1)),
                            reads=reads, writes=[outbuf], inc=(i == n - 1))
            return tok

        def act(out, in_, func, reads, writes, bias=0.0, scale=1.0):
            return tk.op(ACT, lambda: nc.scalar.activation(out, in_, func, bias=bias, scale=scale),
                         reads=reads, writes=writes)

        def tt(out, in0, in1, op, reads, writes, q=None, chain_ok=False):
            q = q or DVE
            return tk.op(q, lambda: q.h.tensor_tensor(out=out, in0=in0, in1=in1, op=op), reads=reads, writes=writes,
                         chain_ok=chain_ok)

        def ts(out, in0, s1, s2, op0, op1, reads, writes, q=None, chain_ok=False):
            q = q or DVE
            if op1 is None:
                return tk.op(q, lambda: q.h.tensor_scalar(out=out, in0=in0, scalar1=s1, scalar2=None, op0=op0),
                             reads=reads, writes=writes, chain_ok=chain_ok)
            return tk.op(q, lambda: q.h.tensor_scalar(out=out, in0=in0, scalar1=s1, scalar2=s2, op0=op0, op1=op1),
                         reads=reads, writes=writes, chain_ok=chain_ok)

        def stt(out, in0, scalar, in1, op0, op1, reads, writes, chain_ok=False):
            return tk.op(DVE, lambda: nc.vector.scalar_tensor_tensor(out=out, in0=in0, scalar=scalar, in1=in1,
                                                                     op0=op0, op1=op1),
                         reads=reads, writes=writes, chain_ok=chain_ok)

        def rsqrt_chain(n, iters=2):
            rs, rv, ra = RS[:, 0:n], RV[:, 0:n], RA[:, 0:n]
            ts(rs.bitcast(I32), rv.bitcast(I32), -0.5, MAGIC, ALU.mult, ALU.add, [bRV], [bRS])
            for _ in range(iters):
                tt(ra, rs, rs, ALU.mult, [bRS], [bRA])
                stt(ra, ra, -0.5, rv, ALU.mult, ALU.mult, [bRA, bRV], [bRA])
                stt(rs, ra, 1.5, rs, ALU.add, ALU.mult, [bRA, bRS], [bRS])

        def rms_stats(srcs, srcbufs, n, inv_dim):
            ps, pb = STAT.next()
            pairs = []
            nsrc = len(srcs)
            for c, (s, b) in enumerate(zip(srcs, srcbufs)):
                q = c % 3
                act(SQ[:, q, 0:n], s, AF.Square, L(b), [bSQ[q]])
                tk.op(PE, lambda q=q, c=c: nc.tensor.matmul(ps[:, 0:n], ONES[:, :], SQ[:, q, 0:n], start=(c == 0),
                                                              stop=(c == nsrc - 1)),
                      reads=[bSQ[q], bONES], writes=[pb], inc=True)
            ts(RV[:, 0:n], ps[:, 0:n], inv_dim, EPS, ALU.mult, ALU.add, [pb], [bRV])
            rsqrt_chain(n)

        def norm_mod(src_of, srcbuf_of, dst_of, dstbuf_of, n, gm_of, shift_of):
            rms_stats([src_of(c) for c in range(NCH)], [srcbuf_of(c) for c in range(NCH)], n, 1.0 / D)
            for c in range(NCH):
                tmp, btmp = (T1, bT1) if c % 2 == 0 else (T2, bT2)
                stt(tmp[:, 0:n], src_of(c), gm_of(c), RS[:, 0:n], ALU.mult, ALU.mult, L(srcbuf_of(c)) + [bRS, bDER], [btmp])
                act(dst_of(c), tmp[:, 0:n], AF.Identity, [btmp, bMOD], [dstbuf_of(c)], bias=shift_of(c), scale=1.0)

        def evac_copy(i, out, in_, reads, writes, scale=1.0):
            if i % 2 == 0:
                act(out, in_, AF.Identity, reads, writes, scale=scale)
            else:
                ts(out, in_, scale, None, ALU.mult, None, reads, writes)

        tk.dma(POOL, ch_const, VEC[:, :], vecs, writes=[bVEC])
        tk.dma(POOL, ch_const, WDW[:, :, :], wdw, writes=[bWDW])
        tk.dma(POOL, ch_const, BMOD[:, :], bmod, writes=[bBMOD])
        tk.dma(POOL, ch_const, CV4[:, :, :], cvec, writes=[bCV4])
        tk.dma(POOL, ch_const, T1[:, 0:P], ident, writes=[bT1])
        full = Tok(ch_const.sem, ch_const.val, ch_const.name)
        for b in (bVEC, bWDW, bBMOD, bCV4, bT1):
            b.w = full
        ts(IDB[:, :], T1[:, 0:P], 1.0, None, ALU.mult, None, [bT1], [bIDB])
        pre_toks = []

        def precast(pids):
            for pid in pids:
                g = piece_group(pid)
                par = len(pre_toks) % 2
                if len(pre_toks) >= 2:
                    tk.wait(POOL, pre_toks[-2])
                t = tk.dma(POOL, ch_pre[par], WS[pid].rearrange("p (a b) -> p a b", b=2048),
                           WF[pid].rearrange("p (a b) -> p a b", b=2048), reads=[bGate], writes=[])
                pre_toks.append(t)
                (bWSg if par == 0 else bWSh)[g].w = t
                cast_emitted.add(pid)

        precast([0, 1])

        tk.op(DVE, lambda: nc.vector.memset(ONES[:, :], 1.0), writes=[bONES])
        tk.op(DVE, lambda: nc.vector.memset(KR2[:, :, :], 0.0), writes=[bKR])
        tk.op(POOL, lambda: nc.gpsimd.memset(XH[:, :, :], 0.0), writes=bXH)
        tk.op(POOL, lambda: nc.gpsimd.memset(XC[:, :, :], 0.0), writes=bXC)
        tk.op(POOL, lambda: nc.gpsimd.memset(MOD[:, :, :], 0.0), writes=[bMOD])
        ts(WDW[:, :, :], WDW[:, :, :], 0.5, None, ALU.mult, None, [bWDW], [bWDW])
        ts(LNH[:, 0, :], VEC[:, V_LNG:V_LNG + 8], 0.5, None, ALU.mult, None, [bVEC], [bLNH])
        ts(LNH[:, 1, :], VEC[:, V_LNB:V_LNB + 8], 0.5, None, ALU.mult, None, [bVEC], [bLNH])
        act(SC[:, :, :], CV4[:, :, :], AF.Tanh, [bCV4], [bSC], scale=0.5)
        ts(SC[:, :, :], SC[:, :, :], 0.5, 0.5, ALU.mult, ALU.add, [bSC], [bSC])
        tt(SC[:, :, :], SC[:, :, :], CV4[:, :, :], ALU.mult, [bSC, bCV4], [bSC])
        ts(SCB[:, :, :], SC[:, :, :], 1.0, None, ALU.mult, None, [bSC], [bSCB])

        ring_issue()

        def mod_part(p0, p1):
            mps, mpb = GEN.next()
            mview = mps[:, 0:192].rearrange("p (a b) -> p a b", b=4)
            for pc in range(p0, p1):
                slot = ring_get("wm", pc) if pc < NWM1 else ring_get("ws", 38 + pc - NWM1)
                wv = wview(slot, NCH, 512)
                for jj in range(4):
                    fc = pc * 4 + jj
                    mm(mview[:, fc, :], mpb, [(wv[:, kc, jj * 128:(jj + 1) * 128], SCB[:, kc, :]) for kc in range(NCH)],
                       reads=[bRING[slot], bSCB])
                ring_release(slot)
            f0, f1 = p0 * 4, p1 * 4
            for i in range(3):
                tt(MOD[:, f0:f1, i], mview[:, f0:f1, i], BMOD[:, f0:f1], ALU.add, [mpb, bBMOD], [bMOD])

        mod_part(0, NWM1)
        for i in range(3):
            stt(DER[:, i, 0, :], MOD[:, 8:16, i], 1.0, VEC[:, V_GMIX:V_GMIX + 8], ALU.add, ALU.mult, [bMOD, bVEC], [bDER])

        pre_rest = list(range(2, 9)) + list(range(38, NPIECE)) + list(range(9, 38))
        pre_chunks = [pre_rest[0:15], pre_rest[15:23], pre_rest[23:30], pre_rest[30:37], pre_rest[37:]]

        def mod_rest():
            mod_part(NWM1, NWM)
            for i in range(3):
                stt(DER[:, i, 1, :], MOD[:, 32:40, i], 1.0, VEC[:, V_GFFN:V_GFFN + 8], ALU.add, ALU.mult, [bMOD, bVEC], [bDER])
                ts(DER[:, i, 2, :], MOD[:, 16:24, i], 0.5, None, ALU.mult, None, [bMOD], [bDER])
                ts(DER[:, i, 3, :], MOD[:, 40:48, i], 0.5, None, ALU.mult, None, [bMOD], [bDER])
            if dbg:
                tk.dma(POOL, ch_dbg, dbgMOD, MOD[:, :, :], reads=[bMOD])

        def L(b):
            return list(b) if isinstance(b, (list, tuple)) else [b]

        def a_sets(st):
            if st == 0:
                return XH, [[b] for b in bXH], ch_xa[0], HX, bHX
            return XB, [[bAQ[2 * c], bAQ[2 * c + 1]] for c in range(NCH)], ch_xa[1], AU, bAU

        def phase_a_front(bl, src_ap, W, k0, smp, st):
            XA, bXA, chx, HA, bHA = a_sets(st)
            tk.dma(POOL, chx, XA[:, :, 0:W], src_ap, writes=[b for bb in bXA for b in bb])
            norm_mod(lambda c: XA[:, c, 0:W], lambda c: bXA[c], lambda c: HA[:, c, 0:W], lambda c: bHA[c], W,
                     lambda c: DER[:, smp, 0, c:c + 1], lambda c: MOD[:, c, smp:smp + 1])

        def phase_a_back(bl, src_ap, W, k0, smp, st, slot, slot1):
            XA, bXA, chx, HA, bHA = a_sets(st)
            wv = wview(slot, NCH, 512)
            kps = []
            for m in range(2):
                ps, pb = GEN.next()
                mm(ps[:, 0:W], pb, [(wv[:, c, m * 128:(m + 1) * 128], HA[:, c, 0:W]) for c in range(NCH)],
                   reads=[bRING[slot]] + bHA)
                kps.append((ps, pb))
            rms_stats([p[0][:, 0:W] for p in kps], [p[1] for p in kps], W, 1.0 / 256)
            for m in range(2):
                stt(NRM[:, m, 0:W], kps[m][0][:, 0:W], vcol(V_GKV, m), RS[:, 0:W], ALU.mult, ALU.mult,
                    [kps[m][1], bRS, bVEC], [bNRM[m]])
            tk.dma(POOL, ch_csa, CS[:, :, 0:W], tab[:, :, k0:k0 + W], writes=[bCS])
            psr, pbr = GEN.next()
            mm(psr[:, 0:W], pbr, [(wv[:, c, 256:384], HA[:, c, 0:W]) for c in range(NCH)], reads=[bRING[slot]] + bHA)
            tt(T1[:, 0:W], psr[:, 0:W], CS[:, 0, 0:W], ALU.mult, [pbr, bCS], [bT1])
            psp, pbp = GEN.next()
            mm(psp[:, 0:W], pbp, [(wv[:, c, 384:512], HA[:, c, 0:W]) for c in range(NCH)], reads=[bRING[slot]] + bHA)
            tt(T2[:, 0:W], psp[:, 0:W], CS[:, 1, 0:W], ALU.mult, [pbp, bCS], [bT2])
            tt(KR2[0:64, 0, k0:k0 + W], T1[0:64, 0:W], T2[0:64, 0:W], ALU.add, [bT1, bT2], [bKR])
            tt(KR2[64:128, 1, k0:k0 + W], T1[64:128, 0:W], T2[64:128, 0:W], ALU.add, [bT1, bT2], [bKR])
            slot = slot1
            wv = wview(slot, 2, 2048)
            for h in range(NH):
                ps, pb = GEN.next()
                mm(ps[:, 0:W], pb, [(wv[:, m, h * 128:(h + 1) * 128], NRM[:, m, 0:W]) for m in range(2)],
                   reads=[bRING[slot], bNRM[0], bNRM[1]])
                evac_copy(0 if h % 4 else 1, KT[:, h, k0:k0 + W], ps[:, 0:W], [pb], [bKT[h]])
            for s in range(W // P):
                for half in range(2):
                    ps, pb = GEN.next()
                    mm(ps[:, :], pb, [(NRM[:, m, s * P:(s + 1) * P], wv[:, m, 1024 + half * 512:1024 + (half + 1) * 512])
                                      for m in range(2)], reads=[bRING[slot], bNRM[0], bNRM[1]])
                    evac_copy(0 if (2 * s + half) % 4 else 1, V[:, k0 // P + s, half * 512:(half + 1) * 512], ps[:, :], [pb], [bV])

        def prologue(bl, j):
            t0 = j * TT
            lo = HL if j == 0 else 0
            hi = HL + TT if j == NT - 1 else HW
            kload = [0]

            def load(c):
                sl = kload[0] % 2
                kload[0] += 1
                tk.dma(ACT, ch_xc[sl], XC[:, sl, lo:hi], xT[bl][:, c, t0 - HL + lo:t0 - HL + hi], writes=[bXC[sl]])
                return sl

            ps, pb = STAT.next()
            ph, pbh = PS[7], PSB[7]
            sl = load(0)
            for c in range(NCH):
                if c > 0:
                    yield
                nxt = load(c + 1) if c + 1 < NCH else load(0)
                q = c % 3
                act(SQ[:, q, :], XC[:, sl, :], AF.Square, [bXC[sl]], [bSQ[q]])
                tk.op(PE, lambda c=c, q=q: nc.tensor.matmul(ps[:, :], ONES[:, :], SQ[:, q, 0:TT], start=(c == 0),
                                                              stop=(c == NCH - 1)),
                      reads=[bSQ[q], bONES], writes=[pb], inc=False)
                tk.op(PE, lambda c=c, q=q: nc.tensor.matmul(ph[:, 0:HW - TT], ONES[:, :], SQ[:, q, TT:HW], start=(c == 0),
                                                              stop=(c == NCH - 1)),
                      reads=[bSQ[q], bONES], writes=[pbh], inc=True)
                sl = nxt
            ts(RV[:, 0:TT], ps[:, :], 1.0 / D, EPS, ALU.mult, ALU.add, [pb], [bRV])
            ts(RV[:, TT:HW], ph[:, 0:HW - TT], 1.0 / D, EPS, ALU.mult, ALU.add, [pbh], [bRV])
            yield
            rsqrt_chain(HW)
            for c in range(NCH):
                yield
                nxt = load(c + 1) if c + 1 < NCH else None
                tmp, btmp = (T1, bT1) if c % 2 == 0 else (T2, bT2)
                stt(tmp[:, :], XC[:, sl, :], DER[:, bl, 0, c:c + 1], RS[:, :], ALU.mult, ALU.mult,
                    [bXC[sl], bRS, bDER], [btmp])
                act(HX[:, c, :], tmp[:, :], AF.Identity, [btmp, bMOD], [bHX[c]], bias=MOD[:, c, bl:bl + 1], scale=1.0)
                sl = nxt

        def phase_b(bl, j):
            t0 = j * TT
            tk.dma(SP, ch_x, XH[:, :, :], xT[bl][:, :, t0:t0 + TT], writes=bXH)
            tk.dma(SP, ch_cs, CS[:, :, :], tab[:, :, CTX + t0:CTX + t0 + TT], writes=[bCS])
            main = slice(HL, HL + TT)

            slot = ring_get("ws", 6)
            wv = wview(slot, NCH, 384)
            qps = []
            for m in range(3):
                ps, pb = GEN.next()
                mm(ps[:, :], pb, [(wv[:, c, m * 128:(m + 1) * 128], HX[:, c, main]) for c in range(NCH)],
                   [bRING[slot]] + bHX)
                qps.append((ps, pb))
            ring_release(slot)
            rms_stats([p[0][:, :] for p in qps], [p[1] for p in qps], TT, 1.0 / 384)
            for m in range(3):
                stt(NRM[:, m, :], qps[m][0][:, :], vcol(V_GQ, m), RS[:, 0:TT], ALU.mult, ALU.mult,
                    [qps[m][1], bRS, bVEC], [bNRM[m]])
            for q4 in range(4):
                slot = ring_get("ws", 2 + q4)
                wv = wview(slot, NCH, 512)
                for jj in range(2):
                    cc = 2 * q4 + jj
                    acol = slice(jj * 256, jj * 256 + 128)
                    gcol = slice(jj * 256 + 128, jj * 256 + 256)
                    psa, pba = GEN.next()
                    mm(psa[:, :], pba, [(wv[:, c, acol], HX[:, c, 0:TT]) for c in range(NCH)], [bRING[slot]] + bHX)
                    psg, pbg = GEN.next()
                    mm(psg[:, :], pbg, [(wv[:, c, gcol], HX[:, c, 0:TT]) for c in range(NCH)], [bRING[slot]] + bHX)
                    psh, pbh = STAT.next()
                    mm(psh[:, 0:32], pbh, [(wv[:, c, acol], HX[:, c, TT:HW]) for c in range(NCH)], [bRING[slot]] + bHX)
                    mm(psh[:, 32:64], pbh, [(wv[:, c, gcol], HX[:, c, TT:HW]) for c in range(NCH)], [bRING[slot]] + bHX)
                    sg = cc % 2
                    act(SG[:, sg, 0:TT], psg[:, :], AF.Tanh, [pbg], [bSG[sg]], scale=0.5)
                    act(SG[:, sg, TT:HW], psh[:, 32:64], AF.Tanh, [pbh], [bSG[sg]], scale=0.5)
                    stt(AU[:, cc, 0:TT], SG[:, sg, 0:TT], 1.0, psa[:, :], ALU.add, ALU.mult, [bSG[sg], pba], [bAU[cc]])
                    stt(AU[:, cc, TT:HW], SG[:, sg, TT:HW], 1.0, psh[:, 0:32], ALU.add, ALU.mult, [bSG[sg], pbh], [bAU[cc]])
                ring_release(slot)
            if j == 0:
                tk.op(DVE, lambda: nc.vector.memset(AU[:, :, 0:HL], 0.0), writes=bAU)
            if j == NT - 1:
                tk.op(DVE, lambda: nc.vector.memset(AU[:, :, HL + TT:HW], 0.0), writes=bAU)

            slot = ring_get("ws", 7)
            wv = wview(slot, 3, 1024)
            for h in range(NH):
                ps, pb = GEN.next()
                mm(ps[:, :], pb, [(wv[:, m, h * 128:(h + 1) * 128], NRM[:, m, :]) for m in range(3)],
                   [bRING[slot]] + bNRM)
                evac_copy(0 if h % 4 else 1, AQ[:, h, :], ps[:, :], [pb], [bAQ[h]])
            ring_release(slot)
            slot = ring_get("ws", 8)
            wv = wview(slot, 3, 1024)
            for hp in range(4):
                psr, pbr = GEN.next()
                mm(psr[:, :], pbr, [(wv[:, m, hp * 128:(hp + 1) * 128], NRM[:, m, :]) for m in range(3)],
                   [bRING[slot]] + bNRM)
                tt(T1[:, 0:TT], psr[:, :], CS[:, 0, :], ALU.mult, [pbr, bCS], [bT1])
                psp, pbp = GEN.next()
                mm(psp[:, :], pbp, [(wv[:, m, 512 + hp * 128:512 + (hp + 1) * 128], NRM[:, m, :]) for m in range(3)],
                   [bRING[slot]] + bNRM)
                tt(T2[:, 0:TT], psp[:, :], CS[:, 1, :], ALU.mult, [pbp, bCS], [bT2])
                tt(AQ[:, 8 + hp, :], T1[:, 0:TT], T2[:, 0:TT], ALU.add, [bT1, bT2], [bAQ[8 + hp]])
            ring_release(slot)

            def conv_pe_tap(cc, k, cps, cpb):
                sl = dg_cnt[0] % 4
                dg_cnt[0] += 1
                tk.op(POOL, lambda: nc.gpsimd.tensor_scalar(out=DG[:, sl, :], in0=IDB[:, :], scalar1=WDW[:, cc, k:k + 1],
                                                          scalar2=1.0, op0=ALU.mult, op1=ALU.mult),
                      reads=[bIDB, bWDW], writes=[bDG[sl]])
                tk.op(PE, lambda: nc.tensor.matmul(cps[:, :], DG[:, sl, :], AU[:, cc, k + 1:k + 1 + TT],
                                                   start=(k == 0), stop=(k == NPE_TAPS - 1)),
                      reads=[bDG[sl], bAU[cc]], writes=[cpb], inc=True)

            def conv_pe(cc):
                cps, cpb = STAT.next()
                for k in range(NPE_TAPS):
                    conv_pe_tap(cc, k, cps, cpb)
                return cps, cpb

            def conv_chunk(cc, cps, cpb):
                k0 = NPE_TAPS
                stt(ACC[:, 0, :], AU[:, cc, k0 + 1:k0 + 1 + TT], WDW[:, cc, k0:k0 + 1], cps[:, :], ALU.mult, ALU.add,
                    [bAU[cc], bWDW, cpb], [bACC[0]])
                act(ACC[:, 1, :], AU[:, cc, k0 + 2:k0 + 2 + TT], AF.Identity, [bAU[cc], bWDW], [bACC[1]],
                    scale=WDW[:, cc, k0 + 1:k0 + 2])
                for k in range(k0 + 2, 31):
                    a = (k - k0) % 2
                    stt(ACC[:, a, :], AU[:, cc, k + 1:k + 1 + TT], WDW[:, cc, k:k + 1], ACC[:, a, :], ALU.mult, ALU.add,
                        [bAU[cc], bWDW, bACC[a]], [bACC[a]])

            def conv_fin(cc):
                stt(AC[:, cc, :], ACC[:, 0, :], vcol(V_BDW, cc), ACC[:, 1, :], ALU.add, ALU.add,
                    [bACC[0], bACC[1], bVEC], [bAC[cc]])

            def s_matmul(h, kc):
                ps, pb = GEN3.next()
                hp, par = h // 2, h % 2
                ksl = slice(kc * P, (kc + 1) * P)
                mm(ps[:, :], pb, [(KT[:, h, ksl], AQ[:, h, :]), (KR2[:, par, ksl], AQ[:, 8 + hp, :])],
                   [bKT[h], bKR, bAQ[h], bAQ[8 + hp]])
                return ps, pb

            cpe = conv_pe(0)
            conv_chunk(0, *cpe)
            conv_fin(0)
            for h in range(NH):
                ops_, opb = OACC.next()
                dps, dpb = DACC.next()
                if h + 1 < NH:
                    cpe = STAT.next()
                cur = s_matmul(h, 0)
                for kc in range(NKC):
                    if h + 1 < NH and kc % 2 == 1 and kc // 2 < NPE_TAPS:
                        conv_pe_tap(h + 1, kc // 2, *cpe)
                    nxt = s_matmul(h, kc + 1) if kc + 1 < NKC else None
                    pi = kc % 3
                    act(PT[:, pi, :], cur[0][:, :], AF.Exp, [cur[1]], [bPT[pi]], scale=ATTN_SCALE)
                    tk.op(PE, lambda kc=kc, pi=pi: nc.tensor.matmul(ops_[:, :], V[:, kc, h * P:(h + 1) * P], PT[:, pi, :],
                                                                    start=(kc == 0), stop=(kc == NKC - 1)),
                          reads=[bV, bPT[pi]], writes=[opb], inc=False)
                    tk.op(PE, lambda kc=kc, pi=pi: nc.tensor.matmul(dps[:, :], ONES[:, :], PT[:, pi, :],
                                                                    start=(kc == 0), stop=(kc == NKC - 1)),
                          reads=[bONES, bPT[pi]], writes=[dpb], inc=True)
                    cur = nxt
                if h + 1 < NH:
                    conv_chunk(h + 1, *cpe)
                    conv_fin(h + 1)
                tk.op(DVE, lambda: nc.vector.reciprocal(RA[:, 0:TT], dps[:, :]), reads=[dpb], writes=[bRA])
                tt(AQ[:, 12 + h, :], ops_[:, :], RA[:, 0:TT], ALU.mult, [opb, bRA], [bAQ[12 + h]])

            ps1, pb1 = STAT.next()
            for c in range(NCH):
                tk.op(PE, lambda c=c: nc.tensor.matmul(ps1[:, :], ONES[:, :], AC[:, c, :], start=(c == 0), stop=(c == NCH - 1)),
                      reads=[bAC[c], bONES], writes=[pb1], inc=(c == NCH - 1))
            ts(T2[:, 0:TT], ps1[:, :], 1.0 / D, None, ALU.mult, None, [pb1], [bT2])
            tt(T1[:, 0:TT], T2[:, 0:TT], T2[:, 0:TT], ALU.mult, [bT2], [bT1])
            ps2, pb2 = STAT.next()
            for c in range(NCH):
                q = c % 3
                act(SQ[:, q, 0:TT], AC[:, c, :], AF.Square, [bAC[c]], [bSQ[q]])
                tk.op(PE, lambda c=c, q=q: nc.tensor.matmul(ps2[:, :], ONES[:, :], SQ[:, q, 0:TT], start=(c == 0),
                                                              stop=(c == NCH - 1)),
                      reads=[bSQ[q], bONES], writes=[pb2], inc=True)
            stt(RV[:, 0:TT], ps2[:, :], 1.0 / D, T1[:, 0:TT], ALU.mult, ALU.subtract, [pb2, bT1], [bRV])
            ts(RV[:, 0:TT], RV[:, 0:TT], EPS, None, ALU.add, None, [bRV], [bRV])
            rsqrt_chain(TT)
            stt(T2[:, 0:TT], T2[:, 0:TT], -1.0, RS[:, 0:TT], ALU.mult, ALU.mult, [bT2, bRS], [bT2])

            def ln_a(c):
                a = c % 2
                tt(ACC[:, a, :], AC[:, c, :], RS[:, 0:TT], ALU.mult, [bAC[c], bRS], [bACC[a]])
                tt(ACC[:, a, :], ACC[:, a, :], T2[:, 0:TT], ALU.add, [bACC[a], bT2], [bACC[a]])
                act(XC[:, a, 0:TT], ACC[:, a, :], AF.Identity, [bACC[a], bLNH], [bXC[a]], bias=LNH[:, 1, c:c + 1],
                    scale=LNH[:, 0, c:c + 1])
                act(SG[:, a, 0:TT], XC[:, a, 0:TT], AF.Tanh, [bXC[a]], [bSG[a]])

            def ln_b(c):
                a = c % 2
                stt(AU[:, c, 0:TT], SG[:, a, 0:TT], 1.0, XC[:, a, 0:TT], ALU.add, ALU.mult, [bSG[a], bXC[a]], [bAU[c]])

            def ln_chunk(c):
                ln_a(c)
                if c > 0:
                    ln_b(c - 1)
                if c == NCH - 1:
                    ln_b(c)

            def gated_proj(first_id, rhs_of, rhs_bufs, evac, extra=None):
                for q4 in range(4):
                    sp = ring_get("ws", first_id + q4)
                    wv = wview(sp, NCH, 512)
                    for jj in range(2):
                        oc = q4 * 2 + jj
                        pcols = slice(jj * 256, jj * 256 + 128)
                        gcols = slice(jj * 256 + 128, jj * 256 + 256)
                        psg, pbg = GEN.next()
                        mm(psg[:, :], pbg, [(wv[:, c, gcols], HX[:, c, main]) for c in range(NCH)], [bRING[sp]] + bHX)
                        gt, bgt = (RA, bRA) if oc % 2 == 0 else (RV, bRV)
                        act(gt[:, 0:TT], psg[:, :], AF.Tanh, [pbg], [bgt], scale=0.5)
                        psy, pby = GEN.next()
                        mm(psy[:, :], pby, [(wv[:, c, pcols], rhs_of(c)) for c in range(NCH)], [bRING[sp]] + rhs_bufs)
                        evac(oc, psy, pby, gt, bgt)
                        if extra is not None:
                            extra(oc)
                    ring_release(sp)

            def evac_ag(oc, psy, pby, gt, bgt):
                stt(AQ[:, oc, :], gt[:, 0:TT], 1.0, psy[:, :], ALU.add, ALU.mult, [bgt, pby], [bAQ[oc]])

            ln_sched = {0: (0, 1), 1: (2, 3), 2: (4,), 3: (5,), 4: (6,), 5: (7,)}

            def ln_extra(oc):
                for c in ln_sched.get(oc, ()):
                    ln_chunk(c)

            gated_proj(13, lambda c: AQ[:, 12 + c, :], bAQ[12:20], evac_ag, extra=ln_extra)

            def evac_mg(oc, psy, pby, gt, bgt):
                tmp, btmp = (T1, bT1) if oc % 2 == 0 else (T2, bT2)
                stt(tmp[:, 0:TT], gt[:, 0:TT], 1.0, psy[:, :], ALU.add, ALU.mult, [bgt, pby], [btmp])
                tt(AQ[:, oc, :], tmp[:, 0:TT], AQ[:, oc, :], ALU.add, [btmp, bAQ[oc]], [bAQ[oc]])

            gated_proj(9, lambda c: AU[:, c, 0:TT], bAU, evac_mg)

            pro = prologue(bl, j + 1) if j + 1 < NT else iter(())

            def pro_step(n=1):
                for _ in range(n):
                    next(pro, None)

            for q2 in range(2):
                slot = ring_get("ws", 17 + q2)
                wv = wview(slot, NCH, 512)
                for jj in range(4):
                    oc = q2 * 4 + jj
                    ps, pb = GEN.next()
                    mm(ps[:, :], pb, [(wv[:, c, jj * 128:(jj + 1) * 128], AQ[:, c, :]) for c in range(NCH)],
                       [bRING[slot]] + bAQ[0:8])
                    stt(XH[:, oc, :], ps[:, :], DER[:, bl, 2, oc:oc + 1], XH[:, oc, :], ALU.mult, ALU.add,
                        [pb, bDER, bXH[oc]], [bXH[oc]])
                ring_release(slot)

            norm_mod(lambda c: XH[:, c, :], lambda c: bXH[c], lambda c: AU[:, c, 0:TT], lambda c: bAU[c], TT,
                     lambda c: DER[:, bl, 1, c:c + 1], lambda c: MOD[:, 24 + c, bl:bl + 1])
            for i in range(11):
                slot = ring_get("ws", 19 + i)
                wv = wview(slot, NCH, 512)
                for jj in range(2):
                    f = 2 * i + jj
                    psa, pba = GEN.next()
                    mm(psa[:, :], pba, [(wv[:, c, jj * 256:jj * 256 + 128], AU[:, c, 0:TT]) for c in range(NCH)],
                       [bRING[slot]] + bAU)
                    sg = f % 2
                    act(SG[:, sg, 0:TT], psa[:, :], AF.Tanh, [pba], [bSG[sg]], scale=0.5)
                    psb, pbb = GEN.next()
                    mm(psb[:, :], pbb, [(wv[:, c, jj * 256 + 128:jj * 256 + 256], AU[:, c, 0:TT]) for c in range(NCH)],
                       [bRING[slot]] + bAU)
                    tmp, btmp = (T1, bT1) if f % 2 == 0 else (T2, bT2)
                    stt(tmp[:, 0:TT], SG[:, sg, 0:TT], 1.0, psa[:, :], ALU.add, ALU.mult, [bSG[sg], pba], [btmp])
                    tt(AQ[:, f, :], tmp[:, 0:TT], psb[:, :], ALU.mult, [btmp, pbb], [bAQ[f]])
                ring_release(slot)
                pro_step()
            for oc in range(NCH):
                pro_step()
                slot = ring_get("ws", 30 + oc)
                wv = wview(slot, NF, 128)
                ps, pb = GEN.next()
                mm(ps[:, :], pb, [(wv[:, f, :], AQ[:, f, :]) for f in range(NF)], [bRING[slot]] + bAQ)
                ring_release(slot)
                stt(XH[:, oc, :], ps[:, :], DER[:, bl, 3, oc:oc + 1], XH[:, oc, :], ALU.mult, ALU.add,
                    [pb, bDER, bXH[oc]], [bXH[oc]])

            for _ in range(20):
                pro_step()
            rms_stats([XH[:, c, :] for c in range(NCH)], [bXH[c] for c in range(NCH)], TT, 1.0 / D)
            for c in range(NCH):
                stt(XH[:, c, :], XH[:, c, :], vcol(V_GFIN, c), RS[:, 0:TT], ALU.mult, ALU.mult,
                    [bXH[c], bVEC, bRS], [bXH[c]])
            tk.dma(SP, ch_out, outT[bl][:, :, t0:t0 + TT], XH[:, :, :], reads=bXH)

        for bl in range(NB):
            blocks = [(bl, cxT[bl], CTX, 0, 2, 0)]
            for j in range(NT):
                blocks.append((bl, xT[bl][:, :, j * TT:(j + 1) * TT], TT, CTX + j * TT, bl, (j + 1) % 2))
            sl0 = ring_get("ws", 0)
            sl1 = ring_get("ws", 1)
            phase_a_front(*blocks[0])
            pro0 = prologue(bl, 0)
            for i, blk in enumerate(blocks):
                if i + 1 < len(blocks):
                    phase_a_front(*blocks[i + 1])
                if bl == 0:
                    precast(pre_chunks[i])
                    ring_issue()
                phase_a_back(*blk, sl0, sl1)
            ring_release(sl0)
            ring_release(sl1)
            if bl == 0:
                mod_rest()
            if dbg and bl == 0:
                tk.dma(POOL, ch_dbg, dbgKT, KT[:, :, :], reads=bKT)
                tk.dma(POOL, ch_dbg, dbgV, V[:, :, :], reads=[bV])
                tk.dma(POOL, ch_dbg, dbgKR, KR2[:, :, :], reads=[bKR])
            for _ in pro0:
                pass
            for j in range(NT):
                phase_b(bl, j)

        assert ring_state["get"] == len(seq), (ring_state["get"], len(seq))
        nc.sync.wait_ge(ch_out.sem, ch_out.val)
        if dbg:
            nc.gpsimd.wait_ge(ch_dbg.sem, ch_dbg.val)
    return nc


def _kc(w):
    K, C = w.shape
    return np.ascontiguousarray(w.reshape(K // P, P, C).transpose(1, 0, 2))


def _piece(a3):
    flat = a3.reshape(P, -1)
    out = np.zeros((P, PIECE), np.float32)
    out[:, :flat.shape[1]] = flat
    return out


def _rope_tables():
    rows = SEQ // 64
    row = np.repeat(np.arange(rows, dtype=np.float32), 64)
    col = np.tile(np.arange(64, dtype=np.float32), rows)
    inv_freq = (np.float32(10000.0) ** (-np.arange(0, 32, 2, dtype=np.float32) / np.float32(32))).astype(np.float32)
    ang = [row[:, None] * inv_freq, col[:, None] * inv_freq]
    cosT = np.ones((64, NK), np.float32)
    sinT = np.zeros((64, NK), np.float32)
    perm = np.zeros(64, np.int64)
    for r in range(64):
        seg, w = r // 32, r % 32
        i, first = w % 16, w < 16
        cosT[r, CTX:] = np.cos(ang[seg][:, i]).astype(np.float32)
        s = np.sin(ang[seg][:, i]).astype(np.float32)
        sinT[r, CTX:] = -s if first else s
        perm[r] = r + 16 if first else r - 16
    tab = np.zeros((P, 2, NK), np.float32)
    tab[:64, 0], tab[64:, 0] = cosT, cosT
    tab[:64, 1], tab[64:, 1] = sinT, sinT
    return tab, perm


def _prep_shared(inp):
    f = lambda k: np.asarray(inp[k], np.float32)
    w_in = f("w_in")[0]
    w_uq = f("w_uq")[0].reshape(384, NH, 192)
    w_ukv = f("w_ukv")[0].reshape(256, NH, 256)
    w_13 = f("w_13")[0]
    w_2 = f("w_2")[0]
    tab, perm = _rope_tables()
    pieces = []
    kr = w_in[:, OFF_KR:OFF_KR + 64]
    krp = kr[:, perm]
    pieces.append(_piece(_kc(np.concatenate([w_in[:, OFF_KV:OFF_KV + 256], kr, kr, krp, krp], axis=1))))
    pieces.append(_piece(_kc(np.concatenate([w_ukv[:, :, :128].reshape(256, 1024),
                                             w_ukv[:, :, 128:].reshape(256, 1024)], axis=1))))
    a, g = w_in[:, 0:1024], w_in[:, 1024:2048]
    for q in range(4):
        cols = []
        for cc in (2 * q, 2 * q + 1):
            cols += [a[:, cc * P:(cc + 1) * P], g[:, cc * P:(cc + 1) * P]]
        pieces.append(_piece(_kc(np.concatenate(cols, axis=1))))
    pieces.append(_piece(_kc(w_in[:, OFF_Q:OFF_Q + 384])))
    pieces.append(_piece(_kc(w_uq[:, :, :128].reshape(384, 1024))))
    rope = w_uq[:, :, 128:]
    pieces.append(_piece(_kc(np.concatenate([rope.reshape(384, 512), rope[:, :, perm].reshape(384, 512)], axis=1))))
    w_pw, w_o, w_out = f("w_pw")[0], f("w_o_mla")[0], f("w_out")[0]
    gc, gm = w_in[:, OFF_GATE:OFF_GATE + 1024], w_in[:, OFF_GATE + 1024:OFF_GATE + 2048]
    for wp_, wg_ in ((w_pw, gc), (w_o, gm)):
        for q in range(4):
            cols = []
            for oc in (2 * q, 2 * q + 1):
                cols += [wp_[:, oc * P:(oc + 1) * P], wg_[:, oc * P:(oc + 1) * P]]
            pieces.append(_piece(_kc(np.concatenate(cols, axis=1))))
    for q in range(2):
        pieces.append(_piece(_kc(w_out[:, q * 512:(q + 1) * 512])))
    w1, w3 = w_13[:, :DFF], w_13[:, DFF:]
    for i in range(11):
        cols = []
        for ff in (2 * i, 2 * i + 1):
            cols += [w1[:, ff * P:(ff + 1) * P], w3[:, ff * P:(ff + 1) * P]]
        pieces.append(_piece(_kc(np.concatenate(cols, axis=1))))
    for oc in range(NCH):
        pieces.append(_piece(_kc(w_2[:, oc * P:(oc + 1) * P])))
    WF = np.stack(pieces)
    assert WF.shape == (38, P, PIECE)
    w_mod = f("w_mod")[0]
    wm_all = [_kc(w_mod[:, pc * 512:(pc + 1) * 512]).reshape(P, PIECE) for pc in range(NWM)]
    WM = np.stack(wm_all[:NWM1])
    WF = np.concatenate([WF, np.stack(wm_all[NWM1:])], axis=0)
    fm = lambda v: np.ascontiguousarray(np.asarray(v, np.float32).reshape(-1, P).T)
    bmod = fm(f("b_mod")[0])
    vecs = np.concatenate([fm(f("g_mix")[0]), fm(f("g_ffn")[0]), fm(f("g_q")[0]), fm(f("g_kv")[0]), fm(f("b_dw")[0]),
                           fm(f("ln_g")[0]), fm(f("ln_b")[0]), fm(f("g_final"))], axis=1)
    assert vecs.shape == (P, NV)
    wdw = np.ascontiguousarray(f("w_dw")[0].reshape(31, NCH, P).transpose(2, 1, 0))
    return dict(WM=WM, WF=WF, bmod=bmod, vecs=np.ascontiguousarray(vecs), wdw=wdw, tab=tab,
                ident=np.eye(P, dtype=np.float32))


def _fmT(a):
    n, T, _ = a.shape
    return np.ascontiguousarray(a.reshape(n, T, NCH, P).transpose(0, 3, 2, 1))


_NC_CACHE = {}


def kernel(**inputs):
    x = np.asarray(inputs["x"], np.float32)
    ctx = np.asarray(inputs["ctx"], np.float32)
    c = np.asarray(inputs["c"], np.float32)
    c_ctx = np.asarray(inputs["c_ctx"], np.float32)
    shared = _prep_shared(inputs)
    in_maps = []
    for core in range(N_CORES):
        b0 = core * NB
        cv = np.zeros((P, NCH, 4), np.float32)
        for i in range(NB):
            cv[:, :, i] = c[b0 + i].reshape(NCH, P).T
        cv[:, :, 2] = c_ctx.reshape(NCH, P).T
        m = dict(shared)
        m["xT"] = _fmT(x[b0:b0 + NB])
        m["cxT"] = _fmT(ctx[b0:b0 + NB])
        m["cvec"] = cv
        in_maps.append(m)
    if "nc" not in _NC_CACHE:
        _NC_CACHE["nc"] = build_nc()
    res = run_bass_kernel_spmd(_NC_CACHE["nc"], in_maps, core_ids=list(range(N_CORES)))
    out = np.empty((N_CORES * NB, SEQ, D), np.float32)
    for core in range(N_CORES):
        oT = np.asarray(res.results[core]["outT"], np.float32)
        out[core * NB:(core + 1) * NB] = oT.transpose(0, 3, 2, 1).reshape(NB, SEQ, D)
    return out
```

```python
import contextlib
import numpy as np
import concourse.bass as bass
import concourse.mybir as mybir
from concourse.bass_utils import run_bass_kernel_spmd

F32 = mybir.dt.float32
BF16 = mybir.dt.bfloat16
I32 = mybir.dt.int32
AF = mybir.ActivationFunctionType
ALU = mybir.AluOpType

P = 128
D = 1024
NCH = 8
SEQ = 2048
CTX = 256
NK = CTX + SEQ
NKC = NK // P
TT = 512
NT = SEQ // TT
HL = 16
HW = TT + 2 * HL
NH = 8
DFF = 2816
NF = DFF // P
EPS = 1e-6
ATTN_SCALE = float(192 ** -0.5)
N_CORES = 8
NB = 2
NPIECE = 46
PIECE = 4096
NWM = 12
NWM1 = 4
RING = 3
MAGIC = float(0x5F3759DF)
NPE_TAPS = 8
PRE_INFLIGHT = 2

V_GMIX, V_GFFN, V_GQ, V_GKV, V_BDW, V_LNG, V_LNB, V_GFIN, NV = 0, 8, 16, 19, 21, 29, 37, 45, 53

OFF_Q = 2048
OFF_KV = OFF_Q + 384
OFF_KR = OFF_KV + 256
OFF_GATE = OFF_KR + 64


class Tok:
    __slots__ = ("sem", "val", "key")

    def __init__(self, sem, val, key):
        self.sem, self.val, self.key = sem, val, key


class Buf:
    __slots__ = ("name", "w", "r")

    def __init__(self, name):
        self.name, self.w, self.r = name, None, {}


class Queue:
    def __init__(self, name, handle, sem, is_pe=False):
        self.name, self.h, self.sem, self.is_pe = name, handle, sem, is_pe
        self.cnt = 0
        self.seen = {}


class Chan:
    def __init__(self, name, sem):
        self.name, self.sem, self.val = name, sem, 0


import os
STRICT = os.environ.get("KSTRICT", "1") != "0"


class Tracker:
    def wait(self, q, tok):
        if tok is None:
            return
        if q.seen.get(tok.key, 0) >= tok.val:
            return
        q.h.wait_ge(tok.sem, tok.val)
        q.seen[tok.key] = tok.val

    def deps(self, q, reads, writes, chain_ok=False, is_dma=False):
        me = None if (is_dma or (STRICT and not q.is_pe)) else q.name
        for b in reads:
            t = b.w
            if t is None:
                continue
            if t.key == me and (q.is_pe or chain_ok):
                continue
            self.wait(q, t)
        for b in writes:
            t = b.w
            if t is not None and t.key != me:
                self.wait(q, t)
            for t in b.r.values():
                if t.key != me:
                    self.wait(q, t)

    def record(self, tok, reads, writes):
        for b in reads:
            old = b.r.get(tok.key)
            if old is None or old.val < tok.val:
                b.r[tok.key] = tok
        for b in writes:
            b.w = tok
            b.r = {}

    def op(self, q, fn, reads=(), writes=(), inc=True, chain_ok=False):
        self.deps(q, reads, writes, chain_ok)
        ins = fn()
        if inc:
            q.cnt += 1
            ins.then_inc(q.sem, 1)
            tok = Tok(q.sem, q.cnt, q.name)
        else:
            tok = Tok(q.sem, q.cnt + 1, q.name)
        self.record(tok, reads, writes)
        return tok

    def dma(self, q, chan, out, in_, reads=(), writes=()):
        self.deps(q, reads, writes, is_dma=True)
        chan.val += 16
        q.h.dma_start(out=out, in_=in_).then_inc(chan.sem, 16)
        tok = Tok(chan.sem, chan.val, chan.name)
        self.record(tok, reads, writes)
        return tok


def ring_sequence():
    seq = [("wm", i) for i in range(NWM1)]
    for _bl in range(NB):
        seq += [("ws", 0), ("ws", 1)]
        if _bl == 0:
            seq += [("ws", 38 + i) for i in range(NWM - NWM1)]
        for _t in range(NT):
            seq += [("ws", p) for p in (6, 2, 3, 4, 5, 7, 8, 13, 14, 15, 16, 9, 10, 11, 12, 17, 18)]
            seq += [("ws", p) for p in range(19, 30)]
            seq += [("ws", p) for p in range(30, 38)]
    return seq


def build_nc(dbg=False):
    nc = bass.Bass("TRN2", target_bir_lowering=False)
    dram = lambda name, shape, dt, kind: nc.dram_tensor(name, shape, dt, kind=kind).ap()
    xT = dram("xT", [NB, P, NCH, SEQ], F32, "ExternalInput")
    cxT = dram("cxT", [NB, P, NCH, CTX], F32, "ExternalInput")
    cvec = dram("cvec", [P, NCH, 4], F32, "ExternalInput")
    WM = dram("WM", [NWM1, P, PIECE], F32, "ExternalInput")
    WF = dram("WF", [NPIECE, P, PIECE], F32, "ExternalInput")
    bmod = dram("bmod", [P, 48], F32, "ExternalInput")
    vecs = dram("vecs", [P, NV], F32, "ExternalInput")
    wdw = dram("wdw", [P, NCH, 31], F32, "ExternalInput")
    tab = dram("tab", [P, 2, NK], F32, "ExternalInput")
    ident = dram("ident", [P, P], F32, "ExternalInput")
    outT = dram("outT", [NB, P, NCH, SEQ], F32, "ExternalOutput")
    WS = dram("WS", [NPIECE, P, PIECE], BF16, "Internal")
    if dbg:
        dbgKT = dram("dbgKT", [P, NH, NK], BF16, "ExternalOutput")
        dbgV = dram("dbgV", [P, NKC, 1024], BF16, "ExternalOutput")
        dbgKR = dram("dbgKR", [P, 2, NK], BF16, "ExternalOutput")
        dbgMOD = dram("dbgMOD", [P, 48, 4], F32, "ExternalOutput")

    tk = Tracker()
    with contextlib.ExitStack() as es:
        sb = lambda name, shape, dt: es.enter_context(nc.sbuf_tensor(name, shape, dt))
        sem = lambda name: es.enter_context(nc.semaphore(name))

        PE = Queue("pe", nc.tensor, sem("s_pe"), is_pe=True)
        ACT = Queue("act", nc.scalar, sem("s_act"))
        DVE = Queue("dve", nc.vector, sem("s_dve"))
        POOL = Queue("pool", nc.gpsimd, sem("s_pool"))
        SP = Queue("sp", nc.sync, sem("s_sp"))

        ONES = sb("ONES", [P, P], BF16)
        VEC = sb("VEC", [P, NV], F32)
        WDW = sb("WDW", [P, NCH, 31], F32)
        BMOD = sb("BMOD", [P, 48], F32)
        MOD = sb("MOD", [P, 48, 4], F32)
        SC = sb("SC", [P, NCH, 4], F32)
        SCB = sb("SCB", [P, NCH, 4], BF16)
        CV4 = sb("CV4", [P, NCH, 4], F32)
        DER = sb("DER", [P, 3, 4, NCH], F32)
        LNH = sb("LNH", [P, 2, NCH], F32)
        KT = sb("KT", [P, NH, NK], BF16)
        KR2 = sb("KR2", [P, 2, NK], BF16)
        V = sb("V", [P, NKC, 1024], BF16)
        XH = sb("XH", [P, NCH, TT], F32)
        XC = sb("XC", [P, 2, HW], F32)
        HX = sb("HX", [P, NCH, HW], BF16)
        AU = sb("AU", [P, NCH, HW], BF16)
        AC = sb("AC", [P, NCH, TT], BF16)
        AQF = sb("AQF", [P, NF * TT], BF16)
        AQ = AQF[:, :].rearrange("p (f t) -> p f t", t=TT)
        XB = AQF[:, 0:2 * NCH * TT].bitcast(F32).rearrange("p (c t) -> p c t", t=TT)
        SQ = sb("SQ", [P, 3, HW], BF16)
        PT = sb("PT", [P, 3, TT], BF16)
        SG = sb("SG", [P, 2, HW], BF16)
        RS = sb("RS", [P, HW], F32)
        RV = sb("RV", [P, HW], F32)
        RA = sb("RA", [P, HW], F32)
        T1 = sb("T1", [P, HW], F32)
        T2 = sb("T2", [P, HW], F32)
        ACC = sb("ACC", [P, 2, TT], F32)
        IDB = sb("IDB", [P, P], BF16)
        DG = sb("DG", [P, 4, P], BF16)
        CS = sb("CS", [P, 2, TT], F32)
        NRM = sb("NRM", [P, 3, TT], BF16)
        RINGT = sb("RINGT", [P, RING, PIECE], BF16)

        PS = [es.enter_context(nc.psum_tensor(f"ps{i}", [P, TT], F32)) for i in range(8)]
        PSB = [Buf(f"ps{i}") for i in range(8)]

        class Rot:
            def __init__(self, ids):
                self.ids, self.i = ids, 0

            def next(self):
                j = self.ids[self.i % len(self.ids)]
                self.i += 1
                return PS[j], PSB[j]

        GEN3 = Rot([0, 1, 2])
        GEN7 = Rot([0, 1, 2, 4, 5, 6])
        GEN = GEN7
        STAT = Rot([3])
        OACC = Rot([4, 5])
        DACC = Rot([6, 7])

        bVEC, bWDW, bBMOD, bMOD, bSC, bCV4, bDER, bLNH, bONES = (Buf(n) for n in
                                                                    "VEC WDW BMOD MOD SC CV4 DER LNH ONES".split())
        bKT = [Buf(f"KT{h}") for h in range(NH)]
        bKR = Buf("KR2")
        bV = Buf("V")
        bXH = [Buf(f"XH{c}") for c in range(NCH)]
        bXC = [Buf(f"XC{c}") for c in range(2)]
        bHX = [Buf(f"HX{c}") for c in range(NCH)]
        bAU = [Buf(f"AU{c}") for c in range(NCH)]
        bAC = [Buf(f"AC{c}") for c in range(NCH)]
        bAQ = [Buf(f"AQ{c}") for c in range(NF)]
        bSQ = [Buf(f"SQ{c}") for c in range(3)]
        bPT = [Buf(f"PT{c}") for c in range(3)]
        bSG = [Buf(f"SG{c}") for c in range(2)]
        bRS, bRV, bRA, bT1, bT2, bCS, bIDB = (Buf(n) for n in "RS RV RA T1 T2 CS IDB".split())
        bACC = [Buf("ACC0"), Buf("ACC1")]
        bDG = [Buf(f"DG{i}") for i in range(4)]
        dg_cnt = [0]
        bNRM = [Buf(f"NRM{c}") for c in range(3)]
        bRING = [Buf(f"RING{c}") for c in range(RING)]
        bWSg = [Buf(f"WSg{g}") for g in range(4)]
        bWSh = [Buf(f"WSh{g}") for g in range(4)]
        bGate = Buf("gate")
        bSCB = Buf("SCB")

        ch_const = Chan("ch_const", sem("c_const"))
        ch_pre = [Chan(f"ch_pre{g}", sem(f"c_pre{g}")) for g in range(2)]
        ch_ring = [Chan(f"ch_ring{s}", sem(f"c_ring{s}")) for s in range(RING)]
        ch_ringsw = [Chan(f"ch_ringsw{s}", sem(f"c_ringsw{s}")) for s in range(RING)]
        ch_x = Chan("ch_x", sem("c_x"))
        ch_xa = [Chan(f"ch_xa{i}", sem(f"c_xa{i}")) for i in range(2)]
        ch_csa = Chan("ch_csa", sem("c_csa"))
        ch_xc = [Chan(f"ch_xc{i}", sem(f"c_xc{i}")) for i in range(2)]
        ch_cs = Chan("ch_cs", sem("c_cs"))
        ch_out = Chan("ch_out", sem("c_out"))
        ch_dbg = Chan("ch_dbg", sem("c_dbg"))

        def vcol(off, c):
            return VEC[:, off + c:off + c + 1]

        seq = ring_sequence()
        ring_state = {"issue": 0, "get": 0, "free": list(range(RING)), "slot_of": {}}
        cast_emitted = set()

        def piece_group(pid):
            return 0 if pid < 2 else (1 if (pid < 9 or pid >= 38) else (2 if pid < 19 else 3))

        def ring_issue():
            st = ring_state
            while st["issue"] < len(seq) and st["free"]:
                n = st["issue"]
                kind, pid = seq[n]
                if kind == "ws" and pid not in cast_emitted:
                    break
                slot = st["free"].pop(0)
                if kind == "wm":
                    tk.dma(POOL, ch_ringsw[slot], RINGT[:, slot, :].rearrange("p (a b) -> p a b", b=2048),
                           WM[pid].rearrange("p (a b) -> p a b", b=2048), reads=[], writes=[bRING[slot]])
                    if pid == NWM1 - 1:
                        bGate.w = Tok(ch_ringsw[slot].sem, ch_ringsw[slot].val, ch_ringsw[slot].name)
                else:
                    tk.dma(SP, ch_ring[slot], RINGT[:, slot, :], WS[pid],
                           reads=[bWSg[piece_group(pid)], bWSh[piece_group(pid)]], writes=[bRING[slot]])
                st["slot_of"][n] = slot
                st["issue"] += 1

        def ring_get(kind, pid):
            st = ring_state
            n = st["get"]
            assert seq[n] == (kind, pid), (n, seq[n], kind, pid)
            if n not in st["slot_of"]:
                ring_issue()
            assert n in st["slot_of"], "ring deadlock: too many pieces held"
            st["get"] += 1
            return st["slot_of"][n]

        def ring_release(slot):
            ring_state["free"].append(slot)
            ring_issue()

        def wview(slot, kcn, cols):
            return RINGT[:, slot, 0:kcn * cols].rearrange("p (k c) -> p k c", c=cols)

        def mm(out_ap, outbuf, pairs, reads):
            n = len(pairs)
            tok = None
            for i, (l, r) in enumerate(pairs):
                tok = tk.op(PE, lambda l=l, r=r, i=i: nc.tensor.matmul(out_ap, l, r, start=(i == 0), stop=(i == n - 1)),
                            reads=reads, writes=[outbuf], inc=(i == n - 1))
            return tok

        def act(out, in_, func, reads, writes, bias=0.0, scale=1.0):
            return tk.op(ACT, lambda: nc.scalar.activation(out, in_, func, bias=bias, scale=scale),
                         reads=reads, writes=writes)

        def tt(out, in0, in1, op, reads, writes, q=None, chain_ok=False):
            q = q or DVE
            return tk.op(q, lambda: q.h.tensor_tensor(out=out, in0=in0, in1=in1, op=op), reads=reads, writes=writes,
                         chain_ok=chain_ok)

        def ts(out, in0, s1, s2, op0, op1, reads, writes, q=None, chain_ok=False):
            q = q or DVE
            if op1 is None:
                return tk.op(q, lambda: q.h.tensor_scalar(out=out, in0=in0, scalar1=s1, scalar2=None, op0=op0),
                             reads=reads, writes=writes, chain_ok=chain_ok)
            return tk.op(q, lambda: q.h.tensor_scalar(out=out, in0=in0, scalar1=s1, scalar2=s2, op0=op0, op1=op1),
                         reads=reads, writes=writes, chain_ok=chain_ok)

        def stt(out, in0, scalar, in1, op0, op1, reads, writes, chain_ok=False):
            return tk.op(DVE, lambda: nc.vector.scalar_tensor_tensor(out=out, in0=in0, scalar=scalar, in1=in1,
                                                                     op0=op0, op1=op1),
                         reads=reads, writes=writes, chain_ok=chain_ok)

        def rsqrt_chain(n, iters=2):
            rs, rv, ra = RS[:, 0:n], RV[:, 0:n], RA[:, 0:n]
            ts(rs.bitcast(I32), rv.bitcast(I32), -0.5, MAGIC, ALU.mult, ALU.add, [bRV], [bRS])
            for _ in range(iters):
                tt(ra, rs, rs, ALU.mult, [bRS], [bRA])
                stt(ra, ra, -0.5, rv, ALU.mult, ALU.mult, [bRA, bRV], [bRA])
                stt(rs, ra, 1.5, rs, ALU.add, ALU.mult, [bRA, bRS], [bRS])

        def rms_stats(srcs, srcbufs, n, inv_dim):
            ps, pb = STAT.next()
            pairs = []
            nsrc = len(srcs)
            for c, (s, b) in enumerate(zip(srcs, srcbufs)):
                q = c % 3
                act(SQ[:, q, 0:n], s, AF.Square, L(b), [bSQ[q]])
                tk.op(PE, lambda q=q, c=c: nc.tensor.matmul(ps[:, 0:n], ONES[:, :], SQ[:, q, 0:n], start=(c == 0),
                                                              stop=(c == nsrc - 1)),
                      reads=[bSQ[q], bONES], writes=[pb], inc=True)
            ts(RV[:, 0:n], ps[:, 0:n], inv_dim, EPS, ALU.mult, ALU.add, [pb], [bRV])
            rsqrt_chain(n)

        def norm_mod(src_of, srcbuf_of, dst_of, dstbuf_of, n, gm_of, shift_of):
            rms_stats([src_of(c) for c in range(NCH)], [srcbuf_of(c) for c in range(NCH)], n, 1.0 / D)
            for c in range(NCH):
                tmp, btmp = (T1, bT1) if c % 2 == 0 else (T2, bT2)
                stt(tmp[:, 0:n], src_of(c), gm_of(c), RS[:, 0:n], ALU.mult, ALU.mult, L(srcbuf_of(c)) + [bRS, bDER], [btmp])
                act(dst_of(c), tmp[:, 0:n], AF.Identity, [btmp, bMOD], [dstbuf_of(c)], bias=shift_of(c), scale=1.0)

        def evac_copy(i, out, in_, reads, writes, scale=1.0):
            if i % 2 == 0:
                act(out, in_, AF.Identity, reads, writes, scale=scale)
            else:
                ts(out, in_, scale, None, ALU.mult, None, reads, writes)

        tk.dma(POOL, ch_const, VEC[:, :], vecs, writes=[bVEC])
        tk.dma(POOL, ch_const, WDW[:, :, :], wdw, writes=[bWDW])
        tk.dma(POOL, ch_const, BMOD[:, :], bmod, writes=[bBMOD])
        tk.dma(POOL, ch_const, CV4[:, :, :], cvec, writes=[bCV4])
        tk.dma(POOL, ch_const, T1[:, 0:P], ident, writes=[bT1])
        full = Tok(ch_const.sem, ch_const.val, ch_const.name)
        for b in (bVEC, bWDW, bBMOD, bCV4, bT1):
            b.w = full
        ts(IDB[:, :], T1[:, 0:P], 1.0, None, ALU.mult, None, [bT1], [bIDB])
        pre_toks = []

        def precast(pids):
            for pid in pids:
                g = piece_group(pid)
                par = len(pre_toks) % 2
                if len(pre_toks) >= 2:
                    tk.wait(POOL, pre_toks[-2])
                t = tk.dma(POOL, ch_pre[par], WS[pid].rearrange("p (a b) -> p a b", b=2048),
                           WF[pid].rearrange("p (a b) -> p a b", b=2048), reads=[bGate], writes=[])
                pre_toks.append(t)
                (bWSg if par == 0 else bWSh)[g].w = t
                cast_emitted.add(pid)

        precast([0, 1])

        tk.op(DVE, lambda: nc.vector.memset(ONES[:, :], 1.0), writes=[bONES])
        tk.op(DVE, lambda: nc.vector.memset(KR2[:, :, :], 0.0), writes=[bKR])
        tk.op(POOL, lambda: nc.gpsimd.memset(XH[:, :, :], 0.0), writes=bXH)
        tk.op(POOL, lambda: nc.gpsimd.memset(XC[:, :, :], 0.0), writes=bXC)
        tk.op(POOL, lambda: nc.gpsimd.memset(MOD[:, :, :], 0.0), writes=[bMOD])
        ts(WDW[:, :, :], WDW[:, :, :], 0.5, None, ALU.mult, None, [bWDW], [bWDW])
        ts(LNH[:, 0, :], VEC[:, V_LNG:V_LNG + 8], 0.5, None, ALU.mult, None, [bVEC], [bLNH])
        ts(LNH[:, 1, :], VEC[:, V_LNB:V_LNB + 8], 0.5, None, ALU.mult, None, [bVEC], [bLNH])
        act(SC[:, :, :], CV4[:, :, :], AF.Tanh, [bCV4], [bSC], scale=0.5)
        ts(SC[:, :, :], SC[:, :, :], 0.5, 0.5, ALU.mult, ALU.add, [bSC], [bSC])
        tt(SC[:, :, :], SC[:, :, :], CV4[:, :, :], ALU.mult, [bSC, bCV4], [bSC])
        ts(SCB[:, :, :], SC[:, :, :], 1.0, None, ALU.mult, None, [bSC], [bSCB])

        ring_issue()

        def mod_part(p0, p1):
            mps, mpb = GEN.next()
            mview = mps[:, 0:192].rearrange("p (a b) -> p a b", b=4)
            for pc in range(p0, p1):
                slot = ring_get("wm", pc) if pc < NWM1 else ring_get("ws", 38 + pc - NWM1)
                wv = wview(slot, NCH, 512)
                for jj in range(4):
                    fc = pc * 4 + jj
                    mm(mview[:, fc, :], mpb, [(wv[:, kc, jj * 128:(jj + 1) * 128], SCB[:, kc, :]) for kc in range(NCH)],
                       reads=[bRING[slot], bSCB])
                ring_release(slot)
            f0, f1 = p0 * 4, p1 * 4
            for i in range(3):
                tt(MOD[:, f0:f1, i], mview[:, f0:f1, i], BMOD[:, f0:f1], ALU.add, [mpb, bBMOD], [bMOD])

        mod_part(0, NWM1)
        for i in range(3):
            stt(DER[:, i, 0, :], MOD[:, 8:16, i], 1.0, VEC[:, V_GMIX:V_GMIX + 8], ALU.add, ALU.mult, [bMOD, bVEC], [bDER])

        pre_rest = list(range(2, 9)) + list(range(38, NPIECE)) + list(range(9, 38))
        pre_chunks = [pre_rest[0:15], pre_rest[15:23], pre_rest[23:30], pre_rest[30:37], pre_rest[37:]]

        def mod_rest():
            mod_part(NWM1, NWM)
            for i in range(3):
                stt(DER[:, i, 1, :], MOD[:, 32:40, i], 1.0, VEC[:, V_GFFN:V_GFFN + 8], ALU.add, ALU.mult, [bMOD, bVEC], [bDER])
                ts(DER[:, i, 2, :], MOD[:, 16:24, i], 0.5, None, ALU.mult, None, [bMOD], [bDER])
                ts(DER[:, i, 3, :], MOD[:, 40:48, i], 0.5, None, ALU.mult, None, [bMOD], [bDER])
            if dbg:
                tk.dma(POOL, ch_dbg, dbgMOD, MOD[:, :, :], reads=[bMOD])

        def L(b):
            return list(b) if isinstance(b, (list, tuple)) else [b]

        def a_sets(st):
            if st == 0:
                return XH, [[b] for b in bXH], ch_xa[0], HX, bHX
            return XB, [[bAQ[2 * c], bAQ[2 * c + 1]] for c in range(NCH)], ch_xa[1], AU, bAU

        def phase_a_front(bl, src_ap, W, k0, smp, st):
            XA, bXA, chx, HA, bHA = a_sets(st)
            tk.dma(POOL, chx, XA[:, :, 0:W], src_ap, writes=[b for bb in bXA for b in bb])
            norm_mod(lambda c: XA[:, c, 0:W], lambda c: bXA[c], lambda c: HA[:, c, 0:W], lambda c: bHA[c], W,
                     lambda c: DER[:, smp, 0, c:c + 1], lambda c: MOD[:, c, smp:smp + 1])

        def phase_a_back(bl, src_ap, W, k0, smp, st, slot, slot1):
            XA, bXA, chx, HA, bHA = a_sets(st)
            wv = wview(slot, NCH, 512)
            kps = []
            for m in range(2):
                ps, pb = GEN.next()
                mm(ps[:, 0:W], pb, [(wv[:, c, m * 128:(m + 1) * 128], HA[:, c, 0:W]) for c in range(NCH)],
                   reads=[bRING[slot]] + bHA)
                kps.append((ps, pb))
            rms_stats([p[0][:, 0:W] for p in kps], [p[1] for p in kps], W, 1.0 / 256)
            for m in range(2):
                stt(NRM[:, m, 0:W], kps[m][0][:, 0:W], vcol(V_GKV, m), RS[:, 0:W], ALU.mult, ALU.mult,
                    [kps[m][1], bRS, bVEC], [bNRM[m]])
            tk.dma(POOL, ch_csa, CS[:, :, 0:W], tab[:, :, k0:k0 + W], writes=[bCS])
            psr, pbr = GEN.next()
            mm(psr[:, 0:W], pbr, [(wv[:, c, 256:384], HA[:, c, 0:W]) for c in range(NCH)], reads=[bRING[slot]] + bHA)
            tt(T1[:, 0:W], psr[:, 0:W], CS[:, 0, 0:W], ALU.mult, [pbr, bCS], [bT1])
            psp, pbp = GEN.next()
            mm(psp[:, 0:W], pbp, [(wv[:, c, 384:512], HA[:, c, 0:W]) for c in range(NCH)], reads=[bRING[slot]] + bHA)
            tt(T2[:, 0:W], psp[:, 0:W], CS[:, 1, 0:W], ALU.mult, [pbp, bCS], [bT2])
            tt(KR2[0:64, 0, k0:k0 + W], T1[0:64, 0:W], T2[0:64, 0:W], ALU.add, [bT1, bT2], [bKR])
            tt(KR2[64:128, 1, k0:k0 + W], T1[64:128, 0:W], T2[64:128, 0:W], ALU.add, [bT1, bT2], [bKR])
            slot = slot1
            wv = wview(slot, 2, 2048)
            for h in range(NH):
                ps, pb = GEN.next()
                mm(ps[:, 0:W], pb, [(wv[:, m, h * 128:(h + 1) * 128], NRM[:, m, 0:W]) for m in range(2)],
                   reads=[bRING[slot], bNRM[0], bNRM[1]])
                evac_copy(0 if h % 4 else 1, KT[:, h, k0:k0 + W], ps[:, 0:W], [pb], [bKT[h]])
            for s in range(W // P):
                for half in range(2):
                    ps, pb = GEN.next()
                    mm(ps[:, :], pb, [(NRM[:, m, s * P:(s + 1) * P], wv[:, m, 1024 + half * 512:1024 + (half + 1) * 512])
                                      for m in range(2)], reads=[bRING[slot], bNRM[0], bNRM[1]])
                    evac_copy(0 if (2 * s + half) % 4 else 1, V[:, k0 // P + s, half * 512:(half + 1) * 512], ps[:, :], [pb], [bV])

        def prologue(bl, j):
            t0 = j * TT
            lo = HL if j == 0 else 0
            hi = HL + TT if j == NT - 1 else HW
            kload = [0]

            def load(c):
                sl = kload[0] % 2
                kload[0] += 1
                tk.dma(ACT, ch_xc[sl], XC[:, sl, lo:hi], xT[bl][:, c, t0 - HL + lo:t0 - HL + hi], writes=[bXC[sl]])
                return sl

            ps, pb = STAT.next()
            ph, pbh = PS[7], PSB[7]
            sl = load(0)
            for c in range(NCH):
                if c > 0:
                    yield
                nxt = load(c + 1) if c + 1 < NCH else load(0)
                q = c % 3
                act(SQ[:, q, :], XC[:, sl, :], AF.Square, [bXC[sl]], [bSQ[q]])
                tk.op(PE, lambda c=c, q=q: nc.tensor.matmul(ps[:, :], ONES[:, :], SQ[:, q, 0:TT], start=(c == 0),
                                                              stop=(c == NCH - 1)),
                      reads=[bSQ[q], bONES], writes=[pb], inc=False)
                tk.op(PE, lambda c=c, q=q: nc.tensor.matmul(ph[:, 0:HW - TT], ONES[:, :], SQ[:, q, TT:HW], start=(c == 0),
                                                              stop=(c == NCH - 1)),
                      reads=[bSQ[q], bONES], writes=[pbh], inc=True)
                sl = nxt
            ts(RV[:, 0:TT], ps[:, :], 1.0 / D, EPS, ALU.mult, ALU.add, [pb], [bRV])
            ts(RV[:, TT:HW], ph[:, 0:HW - TT], 1.0 / D, EPS, ALU.mult, ALU.add, [pbh], [bRV])
            yield
            rsqrt_chain(HW)
            for c in range(NCH):
                yield
                nxt = load(c + 1) if c + 1 < NCH else None
                tmp, btmp = (T1, bT1) if c % 2 == 0 else (T2, bT2)
                stt(tmp[:, :], XC[:, sl, :], DER[:, bl, 0, c:c + 1], RS[:, :], ALU.mult, ALU.mult,
                    [bXC[sl], bRS, bDER], [btmp])
                act(HX[:, c, :], tmp[:, :], AF.Identity, [btmp, bMOD], [bHX[c]], bias=MOD[:, c, bl:bl + 1], scale=1.0)
                sl = nxt

        def phase_b(bl, j):
            t0 = j * TT
            tk.dma(SP, ch_x, XH[:, :, :], xT[bl][:, :, t0:t0 + TT], writes=bXH)
            tk.dma(SP, ch_cs, CS[:, :, :], tab[:, :, CTX + t0:CTX + t0 + TT], writes=[bCS])
            main = slice(HL, HL + TT)

            slot = ring_get("ws", 6)
            wv = wview(slot, NCH, 384)
            qps = []
            for m in range(3):
                ps, pb = GEN.next()
                mm(ps[:, :], pb, [(wv[:, c, m * 128:(m + 1) * 128], HX[:, c, main]) for c in range(NCH)],
                   [bRING[slot]] + bHX)
                qps.append((ps, pb))
            ring_release(slot)
            rms_stats([p[0][:, :] for p in qps], [p[1] for p in qps], TT, 1.0 / 384)
            for m in range(3):
                stt(NRM[:, m, :], qps[m][0][:, :], vcol(V_GQ, m), RS[:, 0:TT], ALU.mult, ALU.mult,
                    [qps[m][1], bRS, bVEC], [bNRM[m]])
            for q4 in range(4):
                slot = ring_get("ws", 2 + q4)
                wv = wview(slot, NCH, 512)
                for jj in range(2):
                    cc = 2 * q4 + jj
                    acol = slice(jj * 256, jj * 256 + 128)
                    gcol = slice(jj * 256 + 128, jj * 256 + 256)
                    psa, pba = GEN.next()
                    mm(psa[:, :], pba, [(wv[:, c, acol], HX[:, c, 0:TT]) for c in range(NCH)], [bRING[slot]] + bHX)
                    psg, pbg = GEN.next()
                    mm(psg[:, :], pbg, [(wv[:, c, gcol], HX[:, c, 0:TT]) for c in range(NCH)], [bRING[slot]] + bHX)
                    psh, pbh = STAT.next()
                    mm(psh[:, 0:32], pbh, [(wv[:, c, acol], HX[:, c, TT:HW]) for c in range(NCH)], [bRING[slot]] + bHX)
                    mm(psh[:, 32:64], pbh, [(wv[:, c, gcol], HX[:, c, TT:HW]) for c in range(NCH)], [bRING[slot]] + bHX)
                    sg = cc % 2
                    act(SG[:, sg, 0:TT], psg[:, :], AF.Tanh, [pbg], [bSG[sg]], scale=0.5)
                    act(SG[:, sg, TT:HW], psh[:, 32:64], AF.Tanh, [pbh], [bSG[sg]], scale=0.5)
                    stt(AU[:, cc, 0:TT], SG[:, sg, 0:TT], 1.0, psa[:, :], ALU.add, ALU.mult, [bSG[sg], pba], [bAU[cc]])
                    stt(AU[:, cc, TT:HW], SG[:, sg, TT:HW], 1.0, psh[:, 0:32], ALU.add, ALU.mult, [bSG[sg], pbh], [bAU[cc]])
                ring_release(slot)
            if j == 0:
                tk.op(DVE, lambda: nc.vector.memset(AU[:, :, 0:HL], 0.0), writes=bAU)
            if j == NT - 1:
                tk.op(DVE, lambda: nc.vector.memset(AU[:, :, HL + TT:HW], 0.0), writes=bAU)

            slot = ring_get("ws", 7)
            wv = wview(slot, 3, 1024)
            for h in range(NH):
                ps, pb = GEN.next()
                mm(ps[:, :], pb, [(wv[:, m, h * 128:(h + 1) * 128], NRM[:, m, :]) for m in range(3)],
                   [bRING[slot]] + bNRM)
                evac_copy(0 if h % 4 else 1, AQ[:, h, :], ps[:, :], [pb], [bAQ[h]])
            ring_release(slot)
            slot = ring_get("ws", 8)
            wv = wview(slot, 3, 1024)
            for hp in range(4):
                psr, pbr = GEN.next()
                mm(psr[:, :], pbr, [(wv[:, m, hp * 128:(hp + 1) * 128], NRM[:, m, :]) for m in range(3)],
                   [bRING[slot]] + bNRM)
                ta, bta, tb, btb = (T1, bT1, T2, bT2) if hp % 2 == 0 else (RA, bRA, RV, bRV)
                tt(ta[:, 0:TT], psr[:, :], CS[:, 0, :], ALU.mult, [pbr, bCS], [bta])
                psp, pbp = GEN.next()
                mm(psp[:, :], pbp, [(wv[:, m, 512 + hp * 128:512 + (hp + 1) * 128], NRM[:, m, :]) for m in range(3)],
                   [bRING[slot]] + bNRM)
                tt(tb[:, 0:TT], psp[:, :], CS[:, 1, :], ALU.mult, [pbp, bCS], [btb])
                tt(AQ[:, 8 + hp, :], ta[:, 0:TT], tb[:, 0:TT], ALU.add, [bta, btb], [bAQ[8 + hp]])
            ring_release(slot)

            def conv_pe_tap(cc, k, cps, cpb):
                sl = dg_cnt[0] % 4
                dg_cnt[0] += 1
                tk.op(POOL, lambda: nc.gpsimd.tensor_scalar(out=DG[:, sl, :], in0=IDB[:, :], scalar1=WDW[:, cc, k:k + 1],
                                                          scalar2=1.0, op0=ALU.mult, op1=ALU.mult),
                      reads=[bIDB, bWDW], writes=[bDG[sl]])
                tk.op(PE, lambda: nc.tensor.matmul(cps[:, :], DG[:, sl, :], AU[:, cc, k + 1:k + 1 + TT],
                                                   start=(k == 0), stop=(k == NPE_TAPS - 1)),
                      reads=[bDG[sl], bAU[cc]], writes=[cpb], inc=True)

            def conv_pe(cc):
                cps, cpb = STAT.next()
                for k in range(NPE_TAPS):
                    conv_pe_tap(cc, k, cps, cpb)
                return cps, cpb

            def conv_chunk(cc, cps, cpb):
                k0 = NPE_TAPS
                stt(ACC[:, 0, :], AU[:, cc, k0 + 1:k0 + 1 + TT], WDW[:, cc, k0:k0 + 1], cps[:, :], ALU.mult, ALU.add,
                    [bAU[cc], bWDW, cpb], [bACC[0]])
                act(ACC[:, 1, :], AU[:, cc, k0 + 2:k0 + 2 + TT], AF.Identity, [bAU[cc], bWDW], [bACC[1]],
                    scale=WDW[:, cc, k0 + 1:k0 + 2])
                for k in range(k0 + 2, 31):
                    a = (k - k0) % 2
                    stt(ACC[:, a, :], AU[:, cc, k + 1:k + 1 + TT], WDW[:, cc, k:k + 1], ACC[:, a, :], ALU.mult, ALU.add,
                        [bAU[cc], bWDW, bACC[a]], [bACC[a]])

            def conv_fin(cc):
                stt(AC[:, cc, :], ACC[:, 0, :], vcol(V_BDW, cc), ACC[:, 1, :], ALU.add, ALU.add,
                    [bACC[0], bACC[1], bVEC], [bAC[cc]])

            def s_matmul(h, kc):
                ps, pb = GEN3.next()
                hp, par = h // 2, h % 2
                ksl = slice(kc * P, (kc + 1) * P)
                mm(ps[:, :], pb, [(KT[:, h, ksl], AQ[:, h, :]), (KR2[:, par, ksl], AQ[:, 8 + hp, :])],
                   [bKT[h], bKR, bAQ[h], bAQ[8 + hp]])
                return ps, pb

            cpe = conv_pe(0)
            conv_chunk(0, *cpe)
            conv_fin(0)
            for h in range(NH):
                ops_, opb = OACC.next()
                dps, dpb = DACC.next()
                if h + 1 < NH:
                    cpe = STAT.next()
                cur = s_matmul(h, 0)
                for kc in range(NKC):
                    if h + 1 < NH and kc % 2 == 1 and kc // 2 < NPE_TAPS:
                        conv_pe_tap(h + 1, kc // 2, *cpe)
                    nxt = s_matmul(h, kc + 1) if kc + 1 < NKC else None
                    pi = kc % 3
                    act(PT[:, pi, :], cur[0][:, :], AF.Exp, [cur[1]], [bPT[pi]], scale=ATTN_SCALE)
                    tk.op(PE, lambda kc=kc, pi=pi: nc.tensor.matmul(ops_[:, :], V[:, kc, h * P:(h + 1) * P], PT[:, pi, :],
                                                                    start=(kc == 0), stop=(kc == NKC - 1)),
                          reads=[bV, bPT[pi]], writes=[opb], inc=False)
                    tk.op(PE, lambda kc=kc, pi=pi: nc.tensor.matmul(dps[:, :], ONES[:, :], PT[:, pi, :],
                                                                    start=(kc == 0), stop=(kc == NKC - 1)),
                          reads=[bONES, bPT[pi]], writes=[dpb], inc=True)
                    cur = nxt
                if h + 1 < NH:
                    conv_chunk(h + 1, *cpe)
                    conv_fin(h + 1)
                tk.op(DVE, lambda: nc.vector.reciprocal(RA[:, 0:TT], dps[:, :]), reads=[dpb], writes=[bRA])
                tt(AQ[:, 12 + h, :], ops_[:, :], RA[:, 0:TT], ALU.mult, [opb, bRA], [bAQ[12 + h]])

            ps1, pb1 = STAT.next()
            for c in range(NCH):
                tk.op(PE, lambda c=c: nc.tensor.matmul(ps1[:, :], ONES[:, :], AC[:, c, :], start=(c == 0), stop=(c == NCH - 1)),
                      reads=[bAC[c], bONES], writes=[pb1], inc=(c == NCH - 1))
            ts(T2[:, 0:TT], ps1[:, :], 1.0 / D, None, ALU.mult, None, [pb1], [bT2])
            tt(T1[:, 0:TT], T2[:, 0:TT], T2[:, 0:TT], ALU.mult, [bT2], [bT1])
            ps2, pb2 = STAT.next()
            for c in range(NCH):
                q = c % 3
                act(SQ[:, q, 0:TT], AC[:, c, :], AF.Square, [bAC[c]], [bSQ[q]])
                tk.op(PE, lambda c=c, q=q: nc.tensor.matmul(ps2[:, :], ONES[:, :], SQ[:, q, 0:TT], start=(c == 0),
                                                              stop=(c == NCH - 1)),
                      reads=[bSQ[q], bONES], writes=[pb2], inc=True)
            stt(RV[:, 0:TT], ps2[:, :], 1.0 / D, T1[:, 0:TT], ALU.mult, ALU.subtract, [pb2, bT1], [bRV])
            ts(RV[:, 0:TT], RV[:, 0:TT], EPS, None, ALU.add, None, [bRV], [bRV])
            rsqrt_chain(TT)
            stt(T2[:, 0:TT], T2[:, 0:TT], -1.0, RS[:, 0:TT], ALU.mult, ALU.mult, [bT2, bRS], [bT2])

            def ln_a(c):
                a = c % 2
                tt(ACC[:, a, :], AC[:, c, :], RS[:, 0:TT], ALU.mult, [bAC[c], bRS], [bACC[a]])
                tt(ACC[:, a, :], ACC[:, a, :], T2[:, 0:TT], ALU.add, [bACC[a], bT2], [bACC[a]])
                act(XC[:, a, 0:TT], ACC[:, a, :], AF.Identity, [bACC[a], bLNH], [bXC[a]], bias=LNH[:, 1, c:c + 1],
                    scale=LNH[:, 0, c:c + 1])
                act(SG[:, a, 0:TT], XC[:, a, 0:TT], AF.Tanh, [bXC[a]], [bSG[a]])

            def ln_b(c):
                a = c % 2
                stt(AU[:, c, 0:TT], SG[:, a, 0:TT], 1.0, XC[:, a, 0:TT], ALU.add, ALU.mult, [bSG[a], bXC[a]], [bAU[c]])

            def ln_chunk(c):
                ln_a(c)
                if c > 0:
                    ln_b(c - 1)
                if c == NCH - 1:
                    ln_b(c)

            def gated_proj(first_id, rhs_of, rhs_bufs, evac, extra=None):
                for q4 in range(4):
                    sp = ring_get("ws", first_id + q4)
                    wv = wview(sp, NCH, 512)
                    for jj in range(2):
                        oc = q4 * 2 + jj
                        pcols = slice(jj * 256, jj * 256 + 128)
                        gcols = slice(jj * 256 + 128, jj * 256 + 256)
                        psg, pbg = GEN.next()
                        mm(psg[:, :], pbg, [(wv[:, c, gcols], HX[:, c, main]) for c in range(NCH)], [bRING[sp]] + bHX)
                        gt, bgt = (RA, bRA) if oc % 2 == 0 else (RV, bRV)
                        act(gt[:, 0:TT], psg[:, :], AF.Tanh, [pbg], [bgt], scale=0.5)
                        psy, pby = GEN.next()
                        mm(psy[:, :], pby, [(wv[:, c, pcols], rhs_of(c)) for c in range(NCH)], [bRING[sp]] + rhs_bufs)
                        evac(oc, psy, pby, gt, bgt)
                        if extra is not None:
                            extra(oc)
                    ring_release(sp)

            def evac_ag(oc, psy, pby, gt, bgt):
                stt(AQ[:, oc, :], gt[:, 0:TT], 1.0, psy[:, :], ALU.add, ALU.mult, [bgt, pby], [bAQ[oc]])

            ln_sched = {0: (0, 1), 1: (2, 3), 2: (4,), 3: (5,), 4: (6,), 5: (7,)}

            def ln_extra(oc):
                for c in ln_sched.get(oc, ()):
                    ln_chunk(c)

            gated_proj(13, lambda c: AQ[:, 12 + c, :], bAQ[12:20], evac_ag, extra=ln_extra)

            def evac_mg(oc, psy, pby, gt, bgt):
                tmp, btmp = (T1, bT1) if oc % 2 == 0 else (T2, bT2)
                stt(tmp[:, 0:TT], gt[:, 0:TT], 1.0, psy[:, :], ALU.add, ALU.mult, [bgt, pby], [btmp])
                tt(AQ[:, oc, :], tmp[:, 0:TT], AQ[:, oc, :], ALU.add, [btmp, bAQ[oc]], [bAQ[oc]])

            gated_proj(9, lambda c: AU[:, c, 0:TT], bAU, evac_mg)

            pro = prologue(bl, j + 1) if j + 1 < NT else iter(())

            def pro_step(n=1):
                for _ in range(n):
                    next(pro, None)

            for q2 in range(2):
                slot = ring_get("ws", 17 + q2)
                wv = wview(slot, NCH, 512)
                for jj in range(4):
                    oc = q2 * 4 + jj
                    ps, pb = GEN.next()
                    mm(ps[:, :], pb, [(wv[:, c, jj * 128:(jj + 1) * 128], AQ[:, c, :]) for c in range(NCH)],
                       [bRING[slot]] + bAQ[0:8])
                    stt(XH[:, oc, :], ps[:, :], DER[:, bl, 2, oc:oc + 1], XH[:, oc, :], ALU.mult, ALU.add,
                        [pb, bDER, bXH[oc]], [bXH[oc]])
                ring_release(slot)

            norm_mod(lambda c: XH[:, c, :], lambda c: bXH[c], lambda c: AU[:, c, 0:TT], lambda c: bAU[c], TT,
                     lambda c: DER[:, bl, 1, c:c + 1], lambda c: MOD[:, 24 + c, bl:bl + 1])
            for i in range(11):
                slot = ring_get("ws", 19 + i)
                wv = wview(slot, NCH, 512)
                for jj in range(2):
                    f = 2 * i + jj
                    psa, pba = GEN.next()
                    mm(psa[:, :], pba, [(wv[:, c, jj * 256:jj * 256 + 128], AU[:, c, 0:TT]) for c in range(NCH)],
                       [bRING[slot]] + bAU)
                    sg = f % 2
                    act(SG[:, sg, 0:TT], psa[:, :], AF.Tanh, [pba], [bSG[sg]], scale=0.5)
                    psb, pbb = GEN.next()
                    mm(psb[:, :], pbb, [(wv[:, c, jj * 256 + 128:jj * 256 + 256], AU[:, c, 0:TT]) for c in range(NCH)],
                       [bRING[slot]] + bAU)
                    tmp, btmp = (T1, bT1) if f % 2 == 0 else (T2, bT2)
                    stt(tmp[:, 0:TT], SG[:, sg, 0:TT], 1.0, psa[:, :], ALU.add, ALU.mult, [bSG[sg], pba], [btmp])
                    tt(AQ[:, f, :], tmp[:, 0:TT], psb[:, :], ALU.mult, [btmp, pbb], [bAQ[f]])
                ring_release(slot)
                pro_step()
            for oc in range(NCH):
                pro_step()
                slot = ring_get("ws", 30 + oc)
                wv = wview(slot, NF, 128)
                ps, pb = GEN.next()
                mm(ps[:, :], pb, [(wv[:, f, :], AQ[:, f, :]) for f in range(NF)], [bRING[slot]] + bAQ)
                ring_release(slot)
                stt(XH[:, oc, :], ps[:, :], DER[:, bl, 3, oc:oc + 1], XH[:, oc, :], ALU.mult, ALU.add,
                    [pb, bDER, bXH[oc]], [bXH[oc]])

            for _ in range(20):
                pro_step()
            rms_stats([XH[:, c, :] for c in range(NCH)], [bXH[c] for c in range(NCH)], TT, 1.0 / D)
            for c in range(NCH):
                stt(XH[:, c, :], XH[:, c, :], vcol(V_GFIN, c), RS[:, 0:TT], ALU.mult, ALU.mult,
                    [bXH[c], bVEC, bRS], [bXH[c]])
            tk.dma(SP, ch_out, outT[bl][:, :, t0:t0 + TT], XH[:, :, :], reads=bXH)

        for bl in range(NB):
            blocks = [(bl, cxT[bl], CTX, 0, 2, 0)]
            for j in range(NT):
                blocks.append((bl, xT[bl][:, :, j * TT:(j + 1) * TT], TT, CTX + j * TT, bl, (j + 1) % 2))
            sl0 = ring_get("ws", 0)
            sl1 = ring_get("ws", 1)
            phase_a_front(*blocks[0])
            pro0 = prologue(bl, 0)
            for i, blk in enumerate(blocks):
                if i + 1 < len(blocks):
                    phase_a_front(*blocks[i + 1])
                if bl == 0:
                    precast(pre_chunks[i])
                    ring_issue()
                phase_a_back(*blk, sl0, sl1)
            ring_release(sl0)
            ring_release(sl1)
            if bl == 0:
                mod_rest()
            if dbg and bl == 0:
                tk.dma(POOL, ch_dbg, dbgKT, KT[:, :, :], reads=bKT)
                tk.dma(POOL, ch_dbg, dbgV, V[:, :, :], reads=[bV])
                tk.dma(POOL, ch_dbg, dbgKR, KR2[:, :, :], reads=[bKR])
            for _ in pro0:
                pass
            for j in range(NT):
                phase_b(bl, j)

        assert ring_state["get"] == len(seq), (ring_state["get"], len(seq))
        nc.sync.wait_ge(ch_out.sem, ch_out.val)
        if dbg:
            nc.gpsimd.wait_ge(ch_dbg.sem, ch_dbg.val)
    return nc


def _kc(w):
    K, C = w.shape
    return np.ascontiguousarray(w.reshape(K // P, P, C).transpose(1, 0, 2))


def _piece(a3):
    flat = a3.reshape(P, -1)
    out = np.zeros((P, PIECE), np.float32)
    out[:, :flat.shape[1]] = flat
    return out


def _rope_tables():
    rows = SEQ // 64
    row = np.repeat(np.arange(rows, dtype=np.float32), 64)
    col = np.tile(np.arange(64, dtype=np.float32), rows)
    inv_freq = (np.float32(10000.0) ** (-np.arange(0, 32, 2, dtype=np.float32) / np.float32(32))).astype(np.float32)
    ang = [row[:, None] * inv_freq, col[:, None] * inv_freq]
    cosT = np.ones((64, NK), np.float32)
    sinT = np.zeros((64, NK), np.float32)
    perm = np.zeros(64, np.int64)
    for r in range(64):
        seg, w = r // 32, r % 32
        i, first = w % 16, w < 16
        cosT[r, CTX:] = np.cos(ang[seg][:, i]).astype(np.float32)
        s = np.sin(ang[seg][:, i]).astype(np.float32)
        sinT[r, CTX:] = -s if first else s
        perm[r] = r + 16 if first else r - 16
    tab = np.zeros((P, 2, NK), np.float32)
    tab[:64, 0], tab[64:, 0] = cosT, cosT
    tab[:64, 1], tab[64:, 1] = sinT, sinT
    return tab, perm


def _prep_shared(inp):
    f = lambda k: np.asarray(inp[k], np.float32)
    w_in = f("w_in")[0]
    w_uq = f("w_uq")[0].reshape(384, NH, 192)
    w_ukv = f("w_ukv")[0].reshape(256, NH, 256)
    w_13 = f("w_13")[0]
    w_2 = f("w_2")[0]
    tab, perm = _rope_tables()
    pieces = []
    kr = w_in[:, OFF_KR:OFF_KR + 64]
    krp = kr[:, perm]
    pieces.append(_piece(_kc(np.concatenate([w_in[:, OFF_KV:OFF_KV + 256], kr, kr, krp, krp], axis=1))))
    pieces.append(_piece(_kc(np.concatenate([w_ukv[:, :, :128].reshape(256, 1024),
                                             w_ukv[:, :, 128:].reshape(256, 1024)], axis=1))))
    a, g = w_in[:, 0:1024], w_in[:, 1024:2048]
    for q in range(4):
        cols = []
        for cc in (2 * q, 2 * q + 1):
            cols += [a[:, cc * P:(cc + 1) * P], g[:, cc * P:(cc + 1) * P]]
        pieces.append(_piece(_kc(np.concatenate(cols, axis=1))))
    pieces.append(_piece(_kc(w_in[:, OFF_Q:OFF_Q + 384])))
    pieces.append(_piece(_kc(w_uq[:, :, :128].reshape(384, 1024))))
    rope = w_uq[:, :, 128:]
    pieces.append(_piece(_kc(np.concatenate([rope.reshape(384, 512), rope[:, :, perm].reshape(384, 512)], axis=1))))
    w_pw, w_o, w_out = f("w_pw")[0], f("w_o_mla")[0], f("w_out")[0]
    gc, gm = w_in[:, OFF_GATE:OFF_GATE + 1024], w_in[:, OFF_GATE + 1024:OFF_GATE + 2048]
    for wp_, wg_ in ((w_pw, gc), (w_o, gm)):
        for q in range(4):
            cols = []
            for oc in (2 * q, 2 * q + 1):
                cols += [wp_[:, oc * P:(oc + 1) * P], wg_[:, oc * P:(oc + 1) * P]]
            pieces.append(_piece(_kc(np.concatenate(cols, axis=1))))
    for q in range(2):
        pieces.append(_piece(_kc(w_out[:, q * 512:(q + 1) * 512])))
    w1, w3 = w_13[:, :DFF], w_13[:, DFF:]
    for i in range(11):
        cols = []
        for ff in (2 * i, 2 * i + 1):
            cols += [w1[:, ff * P:(ff + 1) * P], w3[:, ff * P:(ff + 1) * P]]
        pieces.append(_piece(_kc(np.concatenate(cols, axis=1))))
    for oc in range(NCH):
        pieces.append(_piece(_kc(w_2[:, oc * P:(oc + 1) * P])))
    WF = np.stack(pieces)
    assert WF.shape == (38, P, PIECE)
    w_mod = f("w_mod")[0]
    wm_all = [_kc(w_mod[:, pc * 512:(pc + 1) * 512]).reshape(P, PIECE) for pc in range(NWM)]
    WM = np.stack(wm_all[:NWM1])
    WF = np.concatenate([WF, np.stack(wm_all[NWM1:])], axis=0)
    fm = lambda v: np.ascontiguousarray(np.asarray(v, np.float32).reshape(-1, P).T)
    bmod = fm(f("b_mod")[0])
    vecs = np.concatenate([fm(f("g_mix")[0]), fm(f("g_ffn")[0]), fm(f("g_q")[0]), fm(f("g_kv")[0]), fm(f("b_dw")[0]),
                           fm(f("ln_g")[0]), fm(f("ln_b")[0]), fm(f("g_final"))], axis=1)
    assert vecs.shape == (P, NV)
    wdw = np.ascontiguousarray(f("w_dw")[0].reshape(31, NCH, P).transpose(2, 1, 0))
    return dict(WM=WM, WF=WF, bmod=bmod, vecs=np.ascontiguousarray(vecs), wdw=wdw, tab=tab,
                ident=np.eye(P, dtype=np.float32))


def _fmT(a):
    n, T, _ = a.shape
    return np.ascontiguousarray(a.reshape(n, T, NCH, P).transpose(0, 3, 2, 1))


_NC_CACHE = {}


def kernel(**inputs):
    x = np.asarray(inputs["x"], np.float32)
    ctx = np.asarray(inputs["ctx"], np.float32)
    c = np.asarray(inputs["c"], np.float32)
    c_ctx = np.asarray(inputs["c_ctx"], np.float32)
    shared = _prep_shared(inputs)
    in_maps = []
    for core in range(N_CORES):
        b0 = core * NB
        cv = np.zeros((P, NCH, 4), np.float32)
        for i in range(NB):
            cv[:, :, i] = c[b0 + i].reshape(NCH, P).T
        cv[:, :, 2] = c_ctx.reshape(NCH, P).T
        m = dict(shared)
        m["xT"] = _fmT(x[b0:b0 + NB])
        m["cxT"] = _fmT(ctx[b0:b0 + NB])
        m["cvec"] = cv
        in_maps.append(m)
    if "nc" not in _NC_CACHE:
        _NC_CACHE["nc"] = build_nc()
    res = run_bass_kernel_spmd(_NC_CACHE["nc"], in_maps, core_ids=list(range(N_CORES)))
    out = np.empty((N_CORES * NB, SEQ, D), np.float32)
    for core in range(N_CORES):
        oT = np.asarray(res.results[core]["outT"], np.float32)
        out[core * NB:(core + 1) * NB] = oT.transpose(0, 3, 2, 1).reshape(NB, SEQ, D)
    return out
```
